# Optimizing a Trainium2 kernel written in Bass

```python
import jax, jax.numpy as jnp
from jax import lax
import numpy as np

D_MODEL = 1024
BATCH = 32
SEQ = 2048
DEPTH = 1

HEAD_DIM = 64
N_HEADS_A = 8
N_KV_A = 2
GROUP_A = N_HEADS_A // N_KV_A
N_HEADS_B = 8
WIDTH_A = N_HEADS_A * HEAD_DIM
KV_WIDTH_A = N_KV_A * HEAD_DIM
WIDTH_B = N_HEADS_B * HEAD_DIM
MIX_WIDTH = WIDTH_A + WIDTH_B
IN_COLS = WIDTH_A + 2 * KV_WIDTH_A + 3 * WIDTH_B
SPLITS = (WIDTH_A, WIDTH_A + KV_WIDTH_A, WIDTH_A + 2 * KV_WIDTH_A,
          WIDTH_A + 2 * KV_WIDTH_A + WIDTH_B, WIDTH_A + 2 * KV_WIDTH_A + 2 * WIDTH_B)
WINDOW = 128
BLOCK = 128
N_META = 16
PAD = BLOCK - N_META
D_FF = -(-8 * D_MODEL // (3 * 256)) * 256
ROPE_THETA = 10000.0
EPS = 1e-6

kernel_name = "hymba_swa_sink_stickbreaking_block"


def rmsnorm(x, g):
    xf = x.astype(jnp.float32)
    y = xf * lax.rsqrt(jnp.mean(xf * xf, axis=-1, keepdims=True) + EPS)
    return (y * g.astype(jnp.float32)).astype(x.dtype)


def rope(x, pos):
    half = HEAD_DIM // 2
    inv_freq = ROPE_THETA ** (-jnp.arange(half, dtype=jnp.float32) / half)
    ang = pos.astype(jnp.float32)[:, None] * inv_freq[None, :]
    cos = jnp.cos(ang)[None, :, None, :]
    sin = jnp.sin(ang)[None, :, None, :]
    xf = x.astype(jnp.float32)
    x1, x2 = xf[..., :half], xf[..., half:]
    return jnp.concatenate([x1 * cos - x2 * sin, x2 * cos + x1 * sin], axis=-1).astype(x.dtype)


def swa_sink_attention(q, k, v, sinks):
    B, P = q.shape[0], q.shape[1]
    nb = P // BLOCK
    qb = q.reshape(B, nb, BLOCK, N_KV_A, GROUP_A, HEAD_DIM)
    kb = k.reshape(B, nb, BLOCK, N_KV_A, HEAD_DIM)
    vb = v.reshape(B, nb, BLOCK, N_KV_A, HEAD_DIM)

    def with_context(t):
        meta = jnp.broadcast_to(t[:, :1, PAD:BLOCK], (B, nb, N_META, N_KV_A, HEAD_DIM))
        prev = jnp.concatenate([jnp.zeros_like(t[:, :1]), t[:, :-1]], axis=1)
        return jnp.concatenate([meta, prev, t], axis=2)

    kc, vc = with_context(kb), with_context(vb)
    blk = jnp.arange(nb)[:, None]
    ar = jnp.arange(BLOCK)[None, :]
    qpos = blk * BLOCK + ar
    kpos = jnp.concatenate([
        jnp.broadcast_to(jnp.arange(PAD, BLOCK)[None, :], (nb, N_META)),
        (blk - 1) * BLOCK + ar,
        blk * BLOCK + ar], axis=1)
    nk = kpos.shape[1]
    dist = qpos[:, :, None] - kpos[:, None, :]
    in_meta_seg = (jnp.arange(nk) < N_META)[None, None, :]
    band = (kpos[:, None, :] >= PAD) & (dist >= 0) & (dist < WINDOW)
    mask = jnp.where(in_meta_seg, dist >= WINDOW, band)

    s = jnp.einsum('bnqhgd,bnkhd->bnhgqk', qb, kc).astype(jnp.float32) * (HEAD_DIM ** -0.5)
    s = jnp.where(mask[None, :, None, None], s, -jnp.inf)
    sink = sinks.astype(jnp.float32).reshape(N_KV_A, GROUP_A)[None, None, :, :, None, None]
    m = jnp.maximum(jnp.max(s, axis=-1, keepdims=True), sink)
    p = jnp.exp(s - m)
    denom = jnp.sum(p, axis=-1, keepdims=True) + jnp.exp(sink - m)
    p = (p / denom).astype(v.dtype)
    o = jnp.einsum('bnhgqk,bnkhd->bnqhgd', p, vc)
    return o.reshape(B, P, WIDTH_A)


def stick_breaking_attention(q, k, v):
    B, P = q.shape[0], q.shape[1]
    nb = P // BLOCK
    qb = q.reshape(B, nb, BLOCK, N_HEADS_B, HEAD_DIM).transpose(1, 0, 3, 2, 4)
    kt = k.transpose(0, 2, 1, 3)
    vt = v.transpose(0, 2, 1, 3)
    kpos = jnp.arange(P)
    scale = HEAD_DIM ** -0.5

    def one_block(args):
        q_blk, i = args
        qpos = i * BLOCK + jnp.arange(BLOCK)
        mask = (kpos[None, :] >= PAD) & (kpos[None, :] < qpos[:, None])
        z = jnp.einsum('bhqd,bhkd->bhqk', q_blk, kt).astype(jnp.float32) * scale
        log_1m_beta = jnp.where(mask, -jax.nn.softplus(z), 0.0)
        later = lax.cumsum(log_1m_beta, axis=3, reverse=True) - log_1m_beta
        a = jnp.where(mask, jnp.exp(-jax.nn.softplus(-z) + later), 0.0)
        return jnp.einsum('bhqk,bhkd->bhqd', a.astype(v.dtype), vt)

    o = lax.map(one_block, (qb, jnp.arange(nb)))
    return o.transpose(1, 0, 3, 2, 4).reshape(B, P, WIDTH_B)


def setup_inputs(seed: int = 0) -> dict:
    key = jax.random.key(seed)
    ks = jax.random.split(key, 16)
    f32 = jnp.float32

    def gain(k, shape):
        return 1.0 + 0.02 * jax.random.normal(k, shape, f32)

    return {
        "x": jax.random.normal(ks[0], (BATCH, SEQ, D_MODEL), f32),
        "meta_tokens": jax.random.normal(ks[1], (N_META, D_MODEL), f32),
        "norm_mix": gain(ks[2], (DEPTH, D_MODEL)),
        "w_in": jax.random.normal(ks[3], (DEPTH, D_MODEL, IN_COLS), f32) * D_MODEL ** -0.5,
        "sinks": 0.5 * jax.random.normal(ks[4], (DEPTH, N_HEADS_A), f32),
        "norm_out_a": gain(ks[5], (DEPTH, WIDTH_A)),
        "norm_out_b": gain(ks[6], (DEPTH, WIDTH_B)),
        "w_out": jax.random.normal(ks[7], (DEPTH, MIX_WIDTH, D_MODEL), f32) * MIX_WIDTH ** -0.5,
        "norm_ffn": gain(ks[8], (DEPTH, D_MODEL)),
        "w_gate": jax.random.normal(ks[9], (DEPTH, D_MODEL, D_FF), f32) * D_MODEL ** -0.5,
        "w_up": jax.random.normal(ks[10], (DEPTH, D_MODEL, D_FF), f32) * D_MODEL ** -0.5,
        "w_down": jax.random.normal(ks[11], (DEPTH, D_FF, D_MODEL), f32) * D_FF ** -0.5,
        "norm_final": gain(ks[12], (D_MODEL,)),
    }


def reference(x, meta_tokens, norm_mix, w_in, sinks, norm_out_a, norm_out_b, w_out,
              norm_ffn, w_gate, w_up, w_down, norm_final):
    B = x.shape[0]
    pad = jnp.zeros((B, PAD, D_MODEL), x.dtype)
    meta = jnp.broadcast_to(meta_tokens.astype(x.dtype)[None], (B, N_META, D_MODEL))
    h = jnp.concatenate([pad, meta, x], axis=1)
    P = h.shape[1]
    pos = jnp.arange(P) - PAD

    for l in range(DEPTH):
        u = rmsnorm(h, norm_mix[l])
        proj = u @ w_in[l]
        qa, ka, va, qb, kb, vb = jnp.split(proj, SPLITS, axis=-1)
        qa = rope(qa.reshape(B, P, N_HEADS_A, HEAD_DIM), pos)
        ka = rope(ka.reshape(B, P, N_KV_A, HEAD_DIM), pos)
        va = va.reshape(B, P, N_KV_A, HEAD_DIM)
        oa = swa_sink_attention(qa, ka, va, sinks[l])
        ob = stick_breaking_attention(
            qb.reshape(B, P, N_HEADS_B, HEAD_DIM),
            kb.reshape(B, P, N_HEADS_B, HEAD_DIM),
            vb.reshape(B, P, N_HEADS_B, HEAD_DIM))
        mixed = jnp.concatenate([rmsnorm(oa, norm_out_a[l]), rmsnorm(ob, norm_out_b[l])], axis=-1)
        h = h + mixed @ w_out[l]
        u = rmsnorm(h, norm_ffn[l])
        h = h + (jax.nn.silu(u @ w_gate[l]) * (u @ w_up[l])) @ w_down[l]

    return rmsnorm(h, norm_final)[:, BLOCK:]
```

```python
import contextlib
import numpy as np
import concourse.bass as bass
import concourse.mybir as mybir
from concourse.bass_utils import run_bass_kernel_spmd

F32 = mybir.dt.float32
BF16 = mybir.dt.bfloat16
AF = mybir.ActivationFunctionType
ALU = mybir.AluOpType

ENGS = ("pe", "act", "dve", "pool", "sp")

D = 1024
NCOL = 2304
DFF = 2816
NSC = DFF // 256
EPS = 1e-6
STOP = 99
import os as _os
DBG_TI0 = int(_os.environ.get('DBG_TI0', '0'))


class Sched:
    def __init__(self):
        self.ops = {e: [] for e in ENGS}
        self.cnt = {e: 0 for e in ENGS}
        self.dma_cnt = {}
        self.last_w = {}
        self.readers = {}
        self.seen = {e: {} for e in ENGS}
        self.final_tokens = []
        self.pe_pending = False

    def _deps(self, eng, reads, writes):
        deps = {}

        def add(tok):
            if tok is None:
                return
            k, v = tok
            if eng == "pe" and k == "pe":
                return
            if deps.get(k, 0) < v:
                deps[k] = v

        for r in reads:
            add(self.last_w.get(r))
        for w in writes:
            add(self.last_w.get(w))
            for t in self.readers.get(w, ()):
                add(t)
        out = []
        for k, v in deps.items():
            if self.seen[eng].get(k, 0) >= v:
                continue
            self.seen[eng][k] = v
            out.append((k, v))
        return out

    def _commit(self, tok, reads, writes):
        for r in reads:
            self.readers.setdefault(r, []).append(tok)
        for w in writes:
            self.last_w[w] = tok
            self.readers[w] = []

    def op(self, eng, fn, reads=(), writes=(), sig=True, pre=False):
        waits = self._deps(eng, reads, writes)
        if eng == "pe" and not sig:
            tok = ("pe", self.cnt["pe"] + 1)
            incs = []
            self.pe_pending = True
        else:
            self.cnt[eng] += 1
            tok = (eng, self.cnt[eng])
            incs = [(eng, 1)]
            if eng == "pe":
                self.pe_pending = False
        self.ops[eng].append((fn, waits, incs, False))
        self._commit(tok, reads, writes)
        return tok

    def dma(self, q, fn, semkey, reads=(), writes=(), final=False):
        waits = self._deps(q, reads, writes)
        self.dma_cnt[semkey] = self.dma_cnt.get(semkey, 0) + 16
        tok = (semkey, self.dma_cnt[semkey])
        self.ops[q].append((fn, waits, [(semkey, 16)], True))
        self._commit(tok, reads, writes)
        if final:
            self.final_tokens.append(tok)
        return tok

    def emit(self, nc):
        assert not self.pe_pending
        semkeys = list(ENGS[:4]) + list(self.dma_cnt.keys())
        with contextlib.ExitStack() as st:
            sems = {}
            for k in semkeys:
                sems[k] = st.enter_context(nc.semaphore("s_" + str(k)))
            block = st.enter_context(nc.Block())
            finals = {}
            for k, v in self.final_tokens:
                finals[k] = max(finals.get(k, 0), v)

            def run(engname):
                def body(e):
                    for fn, waits, incs, pre_wait in self.ops[engname]:
                        if pre_wait:
                            for k, v in waits:
                                e.wait_ge(sems[k], v)
                            ins = fn(e)
                        else:
                            for k, v in waits[:-1]:
                                e.wait_ge(sems[k], v)
                            ins = fn(e)
                            if waits:
                                k, v = waits[-1]
                                ins._wait_ge(sems[k], v)
                        for k, a in incs:
                            ins.then_inc(sems[k], a)
                    if engname == "sp":
                        for k, v in finals.items():
                            e.wait_ge(sems[k], v)
                return body

            block.tensor(run("pe"))
            block.scalar(run("act"))
            block.vector(run("dve"))
            block.gpsimd(run("pool"))
            block.sync(run("sp"))


def build(NSEQ, SEQ):
    NG = SEQ // 512
    NT = SEQ // 128
    CL = 16 + SEQ
    nc = bass.Bass("TRN2", target_bir_lowering=False)

    def din(name, shape, dt=F32):
        return nc.dram_tensor(name, shape, dt, kind="ExternalInput").ap()

    x_d = din("x", [NSEQ, SEQ, D])
    meta_d = din("meta", [16, D])
    w_in_d = din("w_in", [D, NCOL])
    w_out_d = din("w_out", [D, D])
    w_gate_d = din("w_gate", [D, DFF])
    w_up_d = din("w_up", [D, DFF])
    w_down_d = din("w_down", [DFF, D])
    gmix_d = din("gmix", [128, 8])
    gffn_d = din("gffn", [128, 8])
    gouta_d = din("gouta", [128, 4])
    goutb_d = din("goutb", [128, 4])
    sink_d = din("sinkl", [128, 4])
    gfin_d = din("gfin", [128, D])
    cos_d = din("cost", [128, NT + 1, 32])
    sin_d = din("sint", [128, NT + 1, 32])
    out_d = nc.dram_tensor("out", [NSEQ, SEQ, D], F32, kind="ExternalOutput").ap()
    wg_s = nc.dram_tensor("wg_s", [D, DFF], BF16, kind="Internal").ap()
    wu_s = nc.dram_tensor("wu_s", [D, DFF], BF16, kind="Internal").ap()
    wd_s = nc.dram_tensor("wd_s", [DFF, D], BF16, kind="Internal").ap()

    st = contextlib.ExitStack()

    def sb(name, shape, dt):
        return st.enter_context(nc.sbuf_tensor(name, shape, dt))

    def ps(name, shape, dt):
        return st.enter_context(nc.psum_tensor(name, shape, dt))

    with st:
        w_in_sb = sb("w_in_sb", [128, 8, NCOL], BF16)
        w_out_sb = sb("w_out_sb", [128, 8, D], BF16)
        kaT = sb("kaT", [128, CL], BF16)
        va = sb("va", [128, NT + 1, 128], BF16)
        kbT = sb("kbT", [128, 4, CL], BF16)
        vb = sb("vb", [128, NT + 1, 512], BF16)
        xg = sb("xg", [128, 4, D], F32)
        uT = sb("uT", [128, 8, 512], BF16)
        xs2 = [sb(f"xs{i}", [128, D], BF16) for i in range(2)]
        qaT = sb("qaT", [128, 4, 512], BF16)
        qbT = sb("qbT", [128, 4, 512], BF16)
        rq = sb("rq", [128, 640], BF16)
        kq = sb("kq", [128, 640], F32)
        scrA = sb("scrA", [128, 2048], BF16)
        scrB = sb("scrB", [128, 512], F32)
        e_sb = [sb(f"e_sb{i}", [128, 512], F32) for i in range(2)]
        sp_sb = [sb(f"sp_sb{i}", [128, 512], BF16) for i in range(2)]
        a_sb = [sb(f"a_sb{i}", [128, 512], BF16) for i in range(2)]
        acc_sb = [sb(f"acc_sb{i}", [128, 512], BF16) for i in range(2)]
        pT = sb("pT", [128, 3, 512], BF16)
        o_raw = sb("o_raw", [128, 4, 512], F32)
        mixT = sb("mixT", [128, 8, 512], BF16)
        hT = [sb(f"hT{i}", [128, 2, 512], BF16) for i in range(2)]
        wg_sl = [sb(f"wg_sl{i}", [128, 8, 256], BF16) for i in range(2)]
        wu_sl = [sb(f"wu_sl{i}", [128, 8, 256], BF16) for i in range(2)]
        wd_sl = [sb(f"wd_sl{i}", [128, 2, D], BF16) for i in range(2)]
        cos_sb = sb("cos_sb", [128, NT + 1, 32], F32)
        sin_sb = sb("sin_sb", [128, NT + 1, 32], F32)
        nsin_sb = sb("nsin_sb", [128, NT + 1, 32], F32)
        gmix = sb("gmix_sb", [128, 8], F32)
        gffn = sb("gffn_sb", [128, 8], F32)
        gouta = sb("gouta_sb", [128, 4], F32)
        goutb = sb("goutb_sb", [128, 4], F32)
        esink = sb("esink_sb", [128, 4], F32)
        gfin = sb("gfin_sb", [128, D], F32)
        ident = sb("ident", [128, 128], BF16)
        negtri = sb("negtri", [128, 128], BF16)
        negones = sb("negones", [128, 128], BF16)
        ones_bf = sb("ones_bf", [128, 128], BF16)
        stat = sb("stat", [128, 16], F32)
        banks = [ps(f"bank{i}", [128, 512], F32) for i in range(7)] + [ps("bank7", [128, 1024], BF16)]

        S = Sched()
        scrA_f = scrA[:].bitcast(F32)

        S.op("pool", lambda e: e.memset(scrB[:, 0:128], 1.0), writes=["scrB"])
        S.op("pool", lambda e: e.affine_select(out=ident[:], in_=scrB[:, 0:128], pattern=[[-1, 128]], compare_op=ALU.is_equal,
                                               fill=0.0, base=0, channel_multiplier=1), reads=["scrB"], writes=["ident"])
        S.op("pool", lambda e: e.memset(ones_bf[:], 1.0), writes=["ones_bf"])
        S.op("pool", lambda e: e.memset(scrB[:, 0:128], -1.0), reads=[], writes=["scrB"])
        S.op("pool", lambda e: e.memset(negones[:], -1.0), writes=["negones"])
        S.op("pool", lambda e: e.affine_select(out=negtri[:], in_=scrB[:, 0:128], pattern=[[-1, 128]], compare_op=ALU.is_ge,
                                               fill=0.0, base=0, channel_multiplier=1), reads=["scrB"], writes=["negtri"])
        S.op("pool", lambda e: e.memset(xg[:, 0, :], 0.0), writes=["xg0"])
        cast_n = [0]

        def ld(q, dst, src, key, wkeys):
            if q == "pool":
                ck = f"castchain{cast_n[0] % 2}"
                cast_n[0] += 1
                S.dma(q, lambda e: e.dma_start(out=dst, in_=src), key, reads=[ck], writes=wkeys + [ck])
            else:
                S.dma(q, lambda e: e.dma_start(out=dst, in_=src), key, writes=wkeys)

        w_in_v = w_in_d.rearrange("(kc p) n -> p kc n", p=128)
        for h in range(2):
            ld("pool", w_in_sb[:, :, h * 1152:(h + 1) * 1152], w_in_v[:, :, h * 1152:(h + 1) * 1152], f"ld_win{h}", [f"w_in{h}"])
        W_IN = ["w_in0", "w_in1"]
        ld("sp", gmix[:], gmix_d, "ld_c0", ["gmix"])
        ld("sp", gffn[:], gffn_d, "ld_c1", ["gffn"])
        ld("sp", gouta[:], gouta_d, "ld_c2", ["gouta"])
        ld("sp", goutb[:], goutb_d, "ld_c3", ["goutb"])
        ld("sp", esink[:], sink_d, "ld_c4", ["esink"])
        ld("sp", gfin[:], gfin_d, "ld_c5", ["gfin"])
        ld("sp", cos_sb[:], cos_d, "ld_c6", ["cos"])
        ld("sp", sin_sb[:], sin_d, "ld_c7", ["sin"])
        ld("pool", w_out_sb[:], w_out_d.rearrange("(kc p) n -> p kc n", p=128), "ld_wout", ["w_out"])
        def rows2(ap):
            return ap.rearrange("r (t c) -> (r t) c", t=2)
        for h in range(2):
            ld("pool", rows2(wg_s)[h * 1024:(h + 1) * 1024, :], rows2(w_gate_d)[h * 1024:(h + 1) * 1024, :], f"ld_wg{h}", [f"wg_s{h}"])
            ld("pool", rows2(wu_s)[h * 1024:(h + 1) * 1024, :], rows2(w_up_d)[h * 1024:(h + 1) * 1024, :], f"ld_wu{h}", [f"wu_s{h}"])
        for h in range(2):
            ld("pool", wd_s[h * 1408:(h + 1) * 1408, :], w_down_d[h * 1408:(h + 1) * 1408, :], f"ld_wd{h}", [f"wd_s{h}"])

        S.op("act", lambda e: e.activation(out=esink[:], in_=esink[:], func=AF.Exp), reads=["esink"], writes=["esink"])
        S.op("dve", lambda e: e.tensor_scalar(out=nsin_sb[:], in0=sin_sb[:], scalar1=-1.0, scalar2=None, op0=ALU.mult),
             reads=["sin"], writes=["nsin"])
        S.dma("sp", lambda e: e.dma_start(out=xg[0:16, 0, :], in_=meta_d), "ld_x0", reads=[], writes=["xg0"])

        bank_bf = [b[:].bitcast(BF16) for b in banks[:7]] + [banks[7][:]]

        def BK(b):
            return [f"bank{b}_0", f"bank{b}_1"] if b in (3, 4) else [f"bank{b}"]

        def rms_transpose(i, gvec, gkey, tb=None):
            xk = f"xg{i}"
            p = i % 2
            xs = xs2[p]
            xsk, sk = f"xs{p}", f"stat{p}"
            c0 = 4 * p
            tb = 5 + p
            S.op("act", lambda e: e.activation(out=xs[:], in_=xg[:, i, :], func=AF.Square, accum_out=stat[:, c0:c0 + 1]),
                 reads=[xk], writes=[xsk, sk])
            S.op("dve", lambda e: e.tensor_scalar(out=stat[:, c0 + 1:c0 + 2], in0=stat[:, c0:c0 + 1], scalar1=1.0 / D, scalar2=EPS,
                                                  op0=ALU.mult, op1=ALU.add), reads=[sk], writes=[sk])
            S.op("act", lambda e: e.activation(out=stat[:, c0 + 2:c0 + 3], in_=stat[:, c0 + 1:c0 + 2], func=AF.Ln), reads=[sk], writes=[sk])
            S.op("act", lambda e: e.activation(out=stat[:, c0 + 3:c0 + 4], in_=stat[:, c0 + 2:c0 + 3], func=AF.Exp, scale=-0.5),
                 reads=[sk], writes=[sk])
            S.op("dve", lambda e: e.tensor_scalar(out=xs[:], in0=xg[:, i, :], scalar1=stat[:, c0 + 3:c0 + 4], scalar2=None, op0=ALU.mult),
                 reads=[xk, sk], writes=[xsk])
            for kc in range(8):
                S.op("pe", lambda e, kc=kc: e.transpose(out=bank_bf[tb][:, kc * 128:(kc + 1) * 128],
                                                        in_=xs[:, kc * 128:(kc + 1) * 128], identity=ident[:]),
                     reads=[xsk, "ident"], writes=BK(tb), sig=(kc == 7))
            S.op("dve", lambda e: e.tensor_tensor(out=uT[:, :, i * 128:(i + 1) * 128],
                                                  in0=bank_bf[tb][:, :].rearrange("p (k t) -> p k t", k=8),
                                                  in1=gvec[:, :].unsqueeze(2).to_broadcast([128, 8, 128]), op=ALU.mult),
                 reads=BK(tb) + [gkey], writes=[f"uT{i}"])

        def rope(src_ps, nh, dsts, ti, skeys, dkey):
            n = nh * 64
            if DBG_TI0:
                ti = 0
            tcv = scrA_f[:, 0:n].rearrange("p (h t d) -> p h t d", h=nh, t=2)
            tsv = scrA_f[:, 512:512 + n].rearrange("p (h t d) -> p h t d", h=nh, t=2)
            srcv = src_ps.rearrange("p (h t d) -> p h t d", h=nh, t=2)
            cosb = cos_sb[:, ti, :].unsqueeze(1).to_broadcast([128, nh, 32])
            sinb = sin_sb[:, ti, :].unsqueeze(1).to_broadcast([128, nh, 32])
            nsinb = nsin_sb[:, ti, :].unsqueeze(1).to_broadcast([128, nh, 32])
            for t in range(2):
                S.op("dve", lambda e, t=t: e.tensor_tensor(out=tcv[:, :, t, :], in0=srcv[:, :, t, :], in1=cosb, op=ALU.mult),
                     reads=skeys + ["cos"], writes=["scrA"])
            S.op("dve", lambda e: e.tensor_tensor(out=tsv[:, :, 0, :], in0=srcv[:, :, 1, :], in1=nsinb, op=ALU.mult),
                 reads=skeys + ["nsin"], writes=["scrA"])
            S.op("dve", lambda e: e.tensor_tensor(out=tsv[:, :, 1, :], in0=srcv[:, :, 0, :], in1=sinb, op=ALU.mult),
                 reads=skeys + ["sin"], writes=["scrA"])
            for h0, nhh, dst in dsts:
                S.op("dve", lambda e, h0=h0, nhh=nhh, dst=dst: e.tensor_tensor(
                    out=dst, in0=scrA_f[:, h0 * 64:(h0 + nhh) * 64].rearrange("p (h d) -> p h d", h=nhh),
                    in1=scrA_f[:, 512 + h0 * 64:512 + (h0 + nhh) * 64].rearrange("p (h d) -> p h d", h=nhh), op=ALU.add),
                     reads=["scrA"], writes=[dkey])

        def mm(out, lhsT, rhs, start, stop, reads, writes, sig, skip=False):
            S.op("pe", lambda e: e.matmul(out, lhsT=lhsT, rhs=rhs, start=start, stop=stop, skip_group_check=skip),
                 reads=reads, writes=writes, sig=sig)

        rq_q = rq[:, 0:512].rearrange("p (j hh d) -> p hh j d", j=4, hh=2)

        def tok_proj(i, ct, meta=False):
            uk = f"uT{i}"
            lts = [uT[:, kc, i * 128:(i + 1) * 128] for kc in range(8)]
            for kc in range(8):
                if not meta:
                    mm(banks[2][:, :], lts[kc], w_in_sb[:, kc, 0:512], kc == 0, kc == 7, [uk, "w_in0"], BK(2), False)
                mm(banks[3][:, 0:256], lts[kc], w_in_sb[:, kc, 512:768], kc == 0, kc == 7, [uk, "w_in0"], BK(3), False)
                mm(banks[4][:, :], lts[kc], w_in_sb[:, kc, 1792:2304], kc == 0, kc == 7, [uk, "w_in1"], BK(4), kc == 7)
            rows = slice(0, 16) if meta else slice(0, 128)
            vk = "vbm" if meta else f"vb{ct}"
            S.op("act", lambda e: e.copy(out=vb[rows, ct, :], in_=banks[4][rows, :]), reads=BK(4), writes=[vk])
            vak = "vam" if meta else f"va{ct}"
            S.op("act", lambda e: e.copy(out=va[rows, ct, :], in_=banks[3][rows, 128:256]), reads=BK(3), writes=[vak])
            if STOP in (13, 23):
                return
            S.op("act", lambda e: e.copy(out=kq[:, 512:640], in_=banks[3][:, 0:128]), reads=BK(3), writes=["kq_k"])
            rope(kq[:, 512:640], 2, [(0, 2, rq[:, 512:640].rearrange("p (h d) -> p h d", h=2))], ct, ["kq_k"], "rqk")
            if not meta and STOP != 25:
                S.op("act", lambda e: e.copy(out=kq[:, 0:512], in_=banks[2][:, :]), reads=BK(2), writes=["kq_q"])
                rope(kq[:, 0:512], 8, [(0, 4, rq_q[:, 0, :, :]), (4, 4, rq_q[:, 1, :, :])], ct, ["kq_q"], "rqq")
            if STOP in (14, 24, 25):
                return
            tb = 7
            if not meta:
                for j in range(4):
                    S.op("pe", lambda e, j=j: e.transpose(out=bank_bf[tb][:, j * 128:(j + 1) * 128],
                                                          in_=rq[:, j * 128:(j + 1) * 128], identity=ident[:]),
                         reads=["rqq", "ident"], writes=[f"bank{tb}"], sig=False)
            S.op("pe", lambda e: e.transpose(out=bank_bf[tb][:, 512:640], in_=rq[:, 512:640], identity=ident[:]),
                 reads=["rqk", "ident"], writes=[f"bank{tb}"], sig=True)
            if meta:
                S.op("dve", lambda e: e.tensor_copy(out=kaT[:, 0:16], in_=bank_bf[tb][:, 512:528]),
                     reads=[f"bank{tb}"], writes=["kaTm"])
            else:
                c0 = 16 + (ct - 1) * 128
                S.op("dve", lambda e: e.tensor_copy(out=kaT[:, c0:c0 + 128], in_=bank_bf[tb][:, 512:640]),
                     reads=[f"bank{tb}"], writes=[f"kaT{ct}"])
                S.op("dve", lambda e: e.tensor_copy(out=qaT[:, :, i * 128:(i + 1) * 128],
                                                    in_=bank_bf[tb][:, 0:512].rearrange("p (j t) -> p j t", j=4)),
                     reads=[f"bank{tb}"], writes=["qaT"])

        def feat_proj(ntok, gt0, meta=False):
            uks = [f"uT{i}" for i in range((ntok + 127) // 128)]
            nb = 0
            for c in range(8):
                if meta and c < 4:
                    continue
                col0 = 768 + c * 128
                bk = nb % 2
                nb += 1
                for kc in range(8):
                    mm(banks[bk][:, 0:ntok], w_in_sb[:, kc, col0:col0 + 128], uT[:, kc, 0:ntok], kc == 0, kc == 7,
                       uks + ["w_in0", "w_in1"], [f"bank{bk}"], kc == 7)
                if c < 4:
                    S.op("act", lambda e, c=c, bk=bk: e.activation(out=qbT[:, c, :], in_=banks[bk][:, :], func=AF.Copy, scale=0.125),
                         reads=[f"bank{bk}"], writes=["qbT"])
                elif meta:
                    S.op("dve", lambda e, c=c, bk=bk: e.tensor_copy(out=kbT[:, c - 4, 0:16], in_=banks[bk][:, 0:16]),
                         reads=[f"bank{bk}"], writes=["kbTm"])
                else:
                    c0 = 16 + gt0 * 128
                    S.op("dve", lambda e, c=c, bk=bk, c0=c0: e.tensor_copy(out=kbT[:, c - 4, c0:c0 + 512], in_=banks[bk][:, :]),
                         reads=[f"bank{bk}"], writes=[f"kbT{gt0 + 1 + t}" for t in range(4)])

        rms_transpose(0, gmix, "gmix")
        tok_proj(0, 0, meta=True)
        feat_proj(128, 0, meta=True)

        ffn_loads = []

        def ffn_load(sc, slot):
            S.dma("sp", lambda e: e.dma_start(out=wg_sl[slot][:], in_=wg_s.rearrange("(kc p) f -> p kc f", p=128)[:, :, sc * 256:(sc + 1) * 256]),
                  f"ld_wgs{slot}", reads=["wg_s0", "wg_s1"], writes=[f"wg_sl{slot}"])
            S.dma("sp", lambda e: e.dma_start(out=wu_sl[slot][:], in_=wu_s.rearrange("(kc p) f -> p kc f", p=128)[:, :, sc * 256:(sc + 1) * 256]),
                  f"ld_wus{slot}", reads=["wu_s0", "wu_s1"], writes=[f"wu_sl{slot}"])
            S.dma("sp", lambda e: e.dma_start(out=wd_sl[slot][:], in_=wd_s.rearrange("(c p) n -> p c n", p=128)[:, 2 * sc:2 * sc + 2, :]),
                  f"ld_wds{slot}", reads=["wd_s0", "wd_s1"], writes=[f"wd_sl{slot}"])

        def attn_a(gi):
            for i in range(4):
                gt = gi * 4 + i
                ct = gt + 1
                for g in range(2):
                    rows = slice(64 * g, 64 * g + 64)
                    rhs_q = qaT[rows, :, i * 128:(i + 1) * 128]
                    c0 = 16 + gt * 128
                    segs = [("cur", 0, kaT[rows, c0:c0 + 128], 128, f"kaT{ct}", f"va{ct}", va[:, ct, 64 * g:64 * g + 64])]
                    if gt >= 1:
                        segs.append(("prev", 1, kaT[rows, c0 - 128:c0], 128, f"kaT{ct - 1}", f"va{ct - 1}", va[:, ct - 1, 64 * g:64 * g + 64]))
                    segs.append(("meta", 2, kaT[rows, 0:16], 16, "kaTm", "vam", va[0:16, 0, 64 * g:64 * g + 64]))
                    for nm, bk, kap, nk, kkey, vkey, vap in segs:
                        mm(banks[bk][0:nk, :].rearrange("p (j t) -> p j t", j=4), kap, rhs_q, True, True,
                           [kkey, "qaT"], [f"bank{bk}"], True)
                        S.op("act", lambda e, bk=bk, nk=nk: e.activation(out=pT[0:nk, bk, :], in_=banks[bk][0:nk, :], func=AF.Exp, scale=0.125),
                             reads=[f"bank{bk}"], writes=[f"pT{bk}"])
                        if nm == "cur":
                            S.op("pool", lambda e: e.affine_select(out=pT[:, 0, :].rearrange("p (j t) -> p j t", j=4),
                                                                   in_=pT[:, 0, :].rearrange("p (j t) -> p j t", j=4),
                                                                   pattern=[[0, 4], [1, 128]], compare_op=ALU.is_ge, fill=0.0,
                                                                   base=0, channel_multiplier=-1), reads=["pT0"], writes=["pT0"])
                        elif nm == "prev":
                            S.op("pool", lambda e: e.affine_select(out=pT[:, 1, :].rearrange("p (j t) -> p j t", j=4),
                                                                   in_=pT[:, 1, :].rearrange("p (j t) -> p j t", j=4),
                                                                   pattern=[[0, 4], [-1, 128]], compare_op=ALU.is_gt, fill=0.0,
                                                                   base=0, channel_multiplier=1), reads=["pT1"], writes=["pT1"])
                    for par in range(2):
                        prow = slice(64 * par, 64 * par + 64)
                        for si, (nm, bk, kap, nk, kkey, vkey, vap) in enumerate(segs):
                            rhs_p = pT[0:nk, bk, :].rearrange("p (c two t) -> p c two t", c=2, two=2)[:, :, par, :]
                            first, last = si == 0, si == len(segs) - 1
                            mm(banks[3][prow, 0:256].rearrange("p (c t) -> p c t", c=2), vap, rhs_p, first, last,
                               [vkey, f"pT{bk}"], [f"bank3_{par}"], False, skip=True)
                            mm(banks[4][prow, 0:256].rearrange("p (c t) -> p c t", c=2), ones_bf[0:nk, 0:64], rhs_p, first, last,
                               ["ones_bf", f"pT{bk}"], [f"bank4_{par}"], last and par == 1, skip=True)
                    tden = scrB[:, 0:256].rearrange("p (c t) -> p c t", c=2)
                    S.op("dve", lambda e, g=g: e.tensor_tensor(out=tden, in0=banks[4][:, 0:256].rearrange("p (c t) -> p c t", c=2),
                                                               in1=esink[:, 2 * g:2 * g + 2].unsqueeze(2).to_broadcast([128, 2, 128]), op=ALU.add),
                         reads=BK(4) + ["esink"], writes=["scrB"])
                    S.op("dve", lambda e: e.reciprocal(out=scrB[:, 0:256], in_=scrB[:, 0:256]), reads=["scrB"], writes=["scrB"])
                    S.op("dve", lambda e, g=g, i=i: e.tensor_tensor(out=o_raw[:, 2 * g:2 * g + 2, i * 128:(i + 1) * 128],
                                                                   in0=banks[3][:, 0:256].rearrange("p (c t) -> p c t", c=2),
                                                                   in1=tden, op=ALU.mult),
                         reads=BK(3) + ["scrB"], writes=["o_raw"])

        def out_norm(gvec, gkey, c_off):
            sq = scrA[:, :].rearrange("p (c t) -> p c t", c=4)
            S.op("act", lambda e: e.activation(out=sq, in_=o_raw[:], func=AF.Square), reads=["o_raw"], writes=["scrA"])
            for c in range(4):
                mm(banks[5][:, :], ones_bf[:], sq[:, c, :], c == 0, c == 3, ["scrA", "ones_bf"], ["bank5"], c == 3)
            S.op("dve", lambda e: e.tensor_scalar(out=scrB[:], in0=banks[5][:], scalar1=1.0 / 512, scalar2=EPS, op0=ALU.mult, op1=ALU.add),
                 reads=["bank5"], writes=["scrB"])
            S.op("act", lambda e: e.activation(out=scrB[:], in_=scrB[:], func=AF.Ln), reads=["scrB"], writes=["scrB"])
            S.op("act", lambda e: e.activation(out=scrB[:], in_=scrB[:], func=AF.Exp, scale=-0.5), reads=["scrB"], writes=["scrB"])
            for c in range(4):
                S.op("dve", lambda e, c=c: e.scalar_tensor_tensor(out=mixT[:, c_off + c, :], in0=o_raw[:, c, :], scalar=gvec[:, c:c + 1],
                                                                 in1=scrB[:], op0=ALU.mult, op1=ALU.mult),
                     reads=["o_raw", "scrB", gkey], writes=[f"mix{c_off + c}"])

        def attn_b(gi, mid=None):
            blist = []
            for hp in range(4):
                for par in range(2):
                    blks = [("diag", 4 * gi + d, d) for d in (3, 2, 1, 0)] + [("full", kb, 0) for kb in range(4 * gi - 1, -1, -1)] + [("meta", -1, 0)]
                    for bi, (kind, kb, d) in enumerate(blks):
                        blist.append(dict(hp=hp, par=par, kind=kind, kb=kb, d=d, first=(bi == 0), last=(bi == len(blks) - 1)))
            n = len(blist)

            def geom(b):
                q0 = 128 * b["d"] if b["kind"] == "diag" else 0
                nk = 16 if b["kind"] == "meta" else 128
                return q0, nk

            def s0(i):
                b = blist[i]
                q0, nk = geom(b)
                rows = slice(64 * b["par"], 64 * b["par"] + 64)
                zb = i % 3
                if b["kind"] == "meta":
                    kap, kkey = kbT[rows, b["hp"], 0:16], "kbTm"
                else:
                    c0 = 16 + b["kb"] * 128
                    kap, kkey = kbT[rows, b["hp"], c0:c0 + 128], f"kbT{b['kb'] + 1}"
                mm(banks[zb][0:nk, q0:512], kap, qbT[rows, b["hp"], q0:512], True, True, [kkey, "qbT"], [f"bank{zb}"], True)

            def s1a(i):
                b = blist[i]
                q0, nk = geom(b)
                zb, sl = i % 3, i % 2
                S.op("act", lambda e: e.activation(out=e_sb[sl][0:nk, q0:512], in_=banks[zb][0:nk, q0:512], func=AF.Exp),
                     reads=[f"bank{zb}"], writes=[f"e{sl}"])

            def s1(i):
                b = blist[i]
                q0, nk = geom(b)
                zb, sl = i % 3, i % 2
                accp, accn = acc_sb[(i - 1) % 2], acc_sb[i % 2]
                kp, kn = f"acc{(i - 1) % 2}", f"acc{i % 2}"
                S.op("act", lambda e: e.activation(out=sp_sb[sl][0:nk, q0:512], in_=e_sb[sl][0:nk, q0:512], func=AF.Ln, bias=1.0),
                     reads=[f"e{sl}"], writes=[f"sp{sl}"])
                if b["kind"] == "diag":
                    S.op("pool", lambda e: e.affine_select(out=sp_sb[sl][:, q0:q0 + 128], in_=sp_sb[sl][:, q0:q0 + 128], pattern=[[1, 128]],
                                                           compare_op=ALU.is_gt, fill=0.0, base=0, channel_multiplier=-1),
                         reads=[f"sp{sl}"], writes=[f"sp{sl}"])
                q1 = q0 + 128 if b["kind"] == "diag" else 0
                has_acc = (q1 < 512) and not (b["kind"] == "diag" and b["d"] == 3)
                mm(banks[zb][0:nk, q0:512], negtri[0:nk, 0:nk], sp_sb[sl][0:nk, q0:512], False, True,
                   ["negtri", f"sp{sl}"], [f"bank{zb}"], not has_acc, skip=True)
                if has_acc:
                    mm(banks[zb][0:nk, q1:512], negones[:, 0:nk], accp[:, q1:512], False, True,
                       ["negones", kp], [f"bank{zb}"], True, skip=True)
                if b["kind"] == "diag":
                    S.op("dve", lambda e: e.tensor_copy(out=accn[:, q0:q0 + 128], in_=sp_sb[sl][:, q0:q0 + 128]),
                         reads=[f"sp{sl}"], writes=[kn])
                    if has_acc:
                        S.op("dve", lambda e: e.tensor_tensor(out=accn[:, q1:512], in0=accp[:, q1:512], in1=sp_sb[sl][:, q1:512], op=ALU.add),
                             reads=[f"sp{sl}", kp], writes=[kn])
                elif b["kind"] == "full":
                    S.op("dve", lambda e: e.tensor_tensor(out=accn[:, :], in0=accp[:, :], in1=sp_sb[sl][:, :], op=ALU.add),
                         reads=[f"sp{sl}", kp], writes=[kn])

            def s2(i):
                b = blist[i]
                q0, nk = geom(b)
                zb, sl = i % 3, i % 2
                h = 2 * b["hp"] + b["par"]
                ob = 3 + (b["hp"] % 2)
                prow = slice(64 * b["par"], 64 * b["par"] + 64)
                S.op("act", lambda e: e.activation(out=a_sb[sl][0:nk, q0:512], in_=banks[zb][0:nk, q0:512], func=AF.Exp),
                     reads=[f"bank{zb}"], writes=[f"a{sl}"])
                if b["kind"] == "diag":
                    S.op("pool", lambda e: e.affine_select(out=a_sb[sl][:, q0:q0 + 128], in_=a_sb[sl][:, q0:q0 + 128], pattern=[[1, 128]],
                                                           compare_op=ALU.is_gt, fill=0.0, base=0, channel_multiplier=-1),
                         reads=[f"a{sl}"], writes=[f"a{sl}"])
                if b["kind"] == "meta":
                    vap, vkey = vb[0:16, 0, h * 64:(h + 1) * 64], "vbm"
                else:
                    vap, vkey = vb[:, b["kb"] + 1, h * 64:(h + 1) * 64], f"vb{b['kb'] + 1}"
                okey = f"bank{ob}_{b['par']}"
                mm(banks[ob][prow, q0:512], vap, a_sb[sl][0:nk, q0:512], b["first"], b["last"], [vkey, f"a{sl}"], [okey], b["last"], skip=True)
                if b["last"]:
                    S.op("dve", lambda e: e.tensor_copy(out=o_raw[prow, b["hp"], :], in_=banks[ob][prow, :]),
                         reads=[okey], writes=["o_raw"])

            for t in range(-2, n):
                if t + 2 < n:
                    s0(t + 2)
                if 0 <= t + 1 < n:
                    s1a(t + 1)
                if t >= 0:
                    s2(t)
                if 0 <= t + 1 < n:
                    s1(t + 1)
                if t == 0 and mid is not None:
                    mid()

        def outproj(i):
            for kc in range(8):
                for hf in range(2):
                    mm(banks[5 + hf][:, :], mixT[:, kc, i * 128:(i + 1) * 128], w_out_sb[:, kc, hf * 512:(hf + 1) * 512], kc == 0, kc == 7,
                       [f"mix{kc}", "w_out"], [f"bank{5 + hf}"], kc == 7 and hf == 1)
            for hf in range(2):
                S.op("dve", lambda e, hf=hf: e.tensor_tensor(out=xg[:, i, hf * 512:(hf + 1) * 512], in0=xg[:, i, hf * 512:(hf + 1) * 512],
                                                             in1=banks[5 + hf][:, :], op=ALU.add),
                     reads=[f"bank{5 + hf}", f"xg{i}"], writes=[f"xg{i}"])

        def ffn(state):
            uks = [f"uT{i}" for i in range(4)]
            n0 = state["n"]
            dcount = [0]

            def gu(sc):
                slot, hb = (n0 + sc) % 2, sc % 2
                for fc in range(2):
                    gb, ub = 2 * fc, 2 * fc + 1
                    for kc in range(8):
                        mm(banks[gb][:, :], wg_sl[slot][:, kc, fc * 128:(fc + 1) * 128], uT[:, kc, :], kc == 0, kc == 7,
                           uks + [f"wg_sl{slot}"], BK(gb), False)
                    for kc in range(8):
                        mm(banks[ub][:, :], wu_sl[slot][:, kc, fc * 128:(fc + 1) * 128], uT[:, kc, :], kc == 0, kc == 7,
                           uks + [f"wu_sl{slot}"], BK(ub), kc == 7)
                    S.op("act", lambda e, gb=gb: e.activation(out=scrB[:], in_=banks[gb][:], func=AF.Silu), reads=BK(gb), writes=["scrB"])
                    S.op("dve", lambda e, ub=ub, fc=fc, hb=hb: e.tensor_tensor(out=hT[hb][:, fc, :], in0=scrB[:], in1=banks[ub][:], op=ALU.mult),
                         reads=["scrB"] + BK(ub), writes=[f"hT{hb}"])

            def down(sc):
                slot, hb = (n0 + sc) % 2, sc % 2
                for i in range(4):
                    for hf in range(2):
                        db = 4 + (dcount[0] % 3)
                        dcount[0] += 1
                        for fc in range(2):
                            mm(banks[db][:, :], hT[hb][:, fc, i * 128:(i + 1) * 128], wd_sl[slot][:, fc, hf * 512:(hf + 1) * 512], fc == 0, fc == 1,
                               [f"hT{hb}", f"wd_sl{slot}"], BK(db), fc == 1)
                        S.op("dve", lambda e, i=i, hf=hf, db=db: e.tensor_tensor(out=xg[:, i, hf * 512:(hf + 1) * 512],
                                                                                 in0=xg[:, i, hf * 512:(hf + 1) * 512], in1=banks[db][:, :], op=ALU.add),
                             reads=BK(db) + [f"xg{i}"], writes=[f"xg{i}"])
                nxt = state["loads"]
                if nxt < state["total"]:
                    ffn_load(nxt % NSC, slot)
                    state["loads"] += 1

            gu(0)
            for sc in range(NSC):
                if sc + 1 < NSC:
                    gu(sc + 1)
                down(sc)
            state["n"] += NSC

        def final_norm(i):
            xk = f"xg{i}"
            p = i % 2
            xs = xs2[p]
            xsk, sk = f"xs{p}", f"statf{p}"
            c0 = 8 + 4 * p
            S.op("act", lambda e: e.activation(out=xs[:], in_=xg[:, i, :], func=AF.Square, accum_out=stat[:, c0:c0 + 1]),
                 reads=[xk], writes=[xsk, sk])
            S.op("dve", lambda e: e.tensor_scalar(out=stat[:, c0 + 1:c0 + 2], in0=stat[:, c0:c0 + 1], scalar1=1.0 / D, scalar2=EPS,
                                                  op0=ALU.mult, op1=ALU.add), reads=[sk], writes=[sk])
            S.op("act", lambda e: e.activation(out=stat[:, c0 + 2:c0 + 3], in_=stat[:, c0 + 1:c0 + 2], func=AF.Ln), reads=[sk], writes=[sk])
            S.op("act", lambda e: e.activation(out=stat[:, c0 + 3:c0 + 4], in_=stat[:, c0 + 2:c0 + 3], func=AF.Exp, scale=-0.5),
                 reads=[sk], writes=[sk])
            S.op("dve", lambda e: e.scalar_tensor_tensor(out=xg[:, i, :], in0=xg[:, i, :], scalar=stat[:, c0 + 3:c0 + 4], in1=gfin[:],
                                                         op0=ALU.mult, op1=ALU.mult),
                 reads=[xk, sk, "gfin"], writes=[xk])

        total_groups = NSEQ * NG
        fstate = dict(n=0, loads=0, total=total_groups * NSC)
        for _ in range(2):
            ffn_load(fstate["loads"] % NSC, fstate["loads"] % 2)
            fstate["loads"] += 1
        groups = [(sq, gi) for sq in range(NSEQ) for gi in range(NG)]

        def load_x(sq, gi, i):
            xv = x_d[sq, gi * 512 + i * 128:gi * 512 + (i + 1) * 128, :]
            S.dma("sp", lambda e: e.dma_start(out=xg[:, i, :], in_=xv), f"ld_x{i}", writes=[f"xg{i}"])

        def store_x(sq, gi, i):
            ov = out_d[sq, gi * 512 + i * 128:gi * 512 + (i + 1) * 128, :]
            S.dma("sp", lambda e: e.dma_start(out=ov, in_=xg[:, i, :]), f"st_out{i}", reads=[f"xg{i}"], final=True)

        for i in range(4):
            load_x(0, 0, i)
        for gidx, (sq, gi) in enumerate(groups):
            rms_transpose(0, gmix, "gmix")
            for i in range(4):
                if i + 1 < 4:
                    rms_transpose(i + 1, gmix, "gmix")
                tok_proj(i, gi * 4 + i + 1)
            feat_proj(512, gi * 4)
            attn_a(gi)
            attn_b(gi, mid=lambda: out_norm(gouta, "gouta", 0))
            out_norm(goutb, "goutb", 4)
            for i in range(4):
                outproj(i)
                if i >= 1:
                    rms_transpose(i - 1, gffn, "gffn")
            rms_transpose(3, gffn, "gffn")
            ffn(fstate)
            for i in range(4):
                final_norm(i)
                store_x(sq, gi, i)
                if gidx + 1 < len(groups):
                    load_x(groups[gidx + 1][0], groups[gidx + 1][1], i)
        S.emit(nc)
    return nc


def _tables(SEQ):
    NT = SEQ // 128
    half = 32
    inv_freq = (10000.0 ** (-np.arange(half, dtype=np.float32) / half)).astype(np.float32)
    cos = np.zeros((128, NT + 1, 32), np.float32)
    sin = np.zeros((128, NT + 1, 32), np.float32)
    r = np.arange(128)
    for t in range(NT + 1):
        pos = (r if t == 0 else 16 + 128 * (t - 1) + r).astype(np.float32)
        ang = (pos[:, None] * inv_freq[None, :]).astype(np.float32)
        cos[:, t, :] = np.cos(ang)
        sin[:, t, :] = np.sin(ang)
    return cos, sin


def _in_maps(x, meta_tokens, norm_mix, w_in, sinks, norm_out_a, norm_out_b, w_out, norm_ffn, w_gate, w_up, w_down,
             norm_final, n_cores, nseq):
    SEQ = x.shape[1]
    cos, sin = _tables(SEQ)
    c = np.ascontiguousarray
    f = lambda a: np.asarray(a, dtype=np.float32)
    sk = f(sinks).reshape(8)
    sinkl = np.zeros((128, 4), np.float32)
    for ch in range(4):
        sinkl[0:64, ch] = sk[2 * ch]
        sinkl[64:128, ch] = sk[2 * ch + 1]
    common = {
        "meta": c(f(meta_tokens)),
        "w_in": c(f(w_in)[0]), "w_out": c(f(w_out)[0]), "w_gate": c(f(w_gate)[0]), "w_up": c(f(w_up)[0]), "w_down": c(f(w_down)[0]),
        "gmix": c(f(norm_mix).reshape(8, 128).T), "gffn": c(f(norm_ffn).reshape(8, 128).T),
        "gouta": c(f(norm_out_a).reshape(4, 128).T), "goutb": c(f(norm_out_b).reshape(4, 128).T),
        "sinkl": sinkl, "gfin": c(np.broadcast_to(f(norm_final).reshape(1, D), (128, D))),
        "cost": cos, "sint": sin,
    }
    xs = f(x)
    maps = []
    for i in range(n_cores):
        m = dict(common)
        m["x"] = c(xs[i * nseq:(i + 1) * nseq])
        maps.append(m)
    return maps


def kernel(x, meta_tokens, norm_mix, w_in, sinks, norm_out_a, norm_out_b, w_out, norm_ffn, w_gate, w_up, w_down, norm_final):
    n_cores = 8
    B, SEQ, _ = x.shape
    nseq = B // n_cores
    nc = build(nseq, SEQ)
    maps = _in_maps(x, meta_tokens, norm_mix, w_in, sinks, norm_out_a, norm_out_b, w_out, norm_ffn, w_gate, w_up, w_down,
                    norm_final, n_cores, nseq)
    res = run_bass_kernel_spmd(nc, maps, core_ids=list(range(n_cores)))
    return np.concatenate([np.asarray(r["out"], dtype=np.float32) for r in res.results], axis=0)
```

```python
import contextlib
import numpy as np
import concourse.bass as bass
import concourse.mybir as mybir
from concourse.bass_utils import run_bass_kernel_spmd

F32 = mybir.dt.float32
BF16 = mybir.dt.bfloat16
AF = mybir.ActivationFunctionType
ALU = mybir.AluOpType

ENGS = ("pe", "act", "dve", "pool", "sp")

D = 1024
NCOL = 2304
DFF = 2816
NSC = DFF // 256
EPS = 1e-6
STOP = 99
import os as _os
DBG_TI0 = int(_os.environ.get('DBG_TI0', '0'))


class Sched:
    def __init__(self):
        self.ops = {e: [] for e in ENGS}
        self.cnt = {e: 0 for e in ENGS}
        self.dma_cnt = {}
        self.last_w = {}
        self.readers = {}
        self.seen = {e: {} for e in ENGS}
        self.final_tokens = []
        self.pe_pending = False

    def _deps(self, eng, reads, writes):
        deps = {}

        def add(tok):
            if tok is None:
                return
            k, v = tok
            if eng == "pe" and k == "pe":
                return
            if deps.get(k, 0) < v:
                deps[k] = v

        for r in reads:
            add(self.last_w.get(r))
        for w in writes:
            add(self.last_w.get(w))
            for t in self.readers.get(w, ()):
                add(t)
        out = []
        for k, v in deps.items():
            if self.seen[eng].get(k, 0) >= v:
                continue
            self.seen[eng][k] = v
            out.append((k, v))
        return out

    def _commit(self, tok, reads, writes):
        for r in reads:
            self.readers.setdefault(r, []).append(tok)
        for w in writes:
            self.last_w[w] = tok
            self.readers[w] = []

    def op(self, eng, fn, reads=(), writes=(), sig=True, pre=False):
        waits = self._deps(eng, reads, writes)
        if eng == "pe" and not sig:
            tok = ("pe", self.cnt["pe"] + 1)
            incs = []
            self.pe_pending = True
        else:
            self.cnt[eng] += 1
            tok = (eng, self.cnt[eng])
            incs = [(eng, 1)]
            if eng == "pe":
                self.pe_pending = False
        self.ops[eng].append((fn, waits, incs, False))
        self._commit(tok, reads, writes)
        return tok

    def dma(self, q, fn, semkey, reads=(), writes=(), final=False):
        waits = self._deps(q, reads, writes)
        self.dma_cnt[semkey] = self.dma_cnt.get(semkey, 0) + 16
        tok = (semkey, self.dma_cnt[semkey])
        self.ops[q].append((fn, waits, [(semkey, 16)], True))
        self._commit(tok, reads, writes)
        if final:
            self.final_tokens.append(tok)
        return tok

    def emit(self, nc):
        assert not self.pe_pending
        semkeys = list(ENGS[:4]) + list(self.dma_cnt.keys())
        with contextlib.ExitStack() as st:
            sems = {}
            for k in semkeys:
                sems[k] = st.enter_context(nc.semaphore("s_" + str(k)))
            block = st.enter_context(nc.Block())
            finals = {}
            for k, v in self.final_tokens:
                finals[k] = max(finals.get(k, 0), v)

            def run(engname):
                def body(e):
                    for fn, waits, incs, pre_wait in self.ops[engname]:
                        if pre_wait:
                            for k, v in waits:
                                e.wait_ge(sems[k], v)
                            ins = fn(e)
                        else:
                            for k, v in waits[:-1]:
                                e.wait_ge(sems[k], v)
                            ins = fn(e)
                            if waits:
                                k, v = waits[-1]
                                ins._wait_ge(sems[k], v)
                        for k, a in incs:
                            ins.then_inc(sems[k], a)
                    if engname == "sp":
                        for k, v in finals.items():
                            e.wait_ge(sems[k], v)
                return body

            block.tensor(run("pe"))
            block.scalar(run("act"))
            block.vector(run("dve"))
            block.gpsimd(run("pool"))
            block.sync(run("sp"))


def build(NSEQ, SEQ):
    NG = SEQ // 512
    NT = SEQ // 128
    CL = 16 + SEQ
    nc = bass.Bass("TRN2", target_bir_lowering=False)

    def din(name, shape, dt=F32):
        return nc.dram_tensor(name, shape, dt, kind="ExternalInput").ap()

    x_d = din("x", [NSEQ, SEQ, D])
    meta_d = din("meta", [16, D])
    w_in_d = din("w_in", [D, NCOL])
    w_out_d = din("w_out", [D, D])
    w_gate_d = din("w_gate", [D, DFF])
    w_up_d = din("w_up", [D, DFF])
    w_down_d = din("w_down", [DFF, D])
    gmix_d = din("gmix", [128, 8])
    gffn_d = din("gffn", [128, 8])
    gouta_d = din("gouta", [128, 4])
    goutb_d = din("goutb", [128, 4])
    sink_d = din("sinkl", [128, 4])
    gfin_d = din("gfin", [128, D])
    cos_d = din("cost", [128, NT + 1, 32])
    sin_d = din("sint", [128, NT + 1, 32])
    out_d = nc.dram_tensor("out", [NSEQ, SEQ, D], F32, kind="ExternalOutput").ap()
    wg_s = nc.dram_tensor("wg_s", [D, DFF], BF16, kind="Internal").ap()
    wu_s = nc.dram_tensor("wu_s", [D, DFF], BF16, kind="Internal").ap()
    wd_s = nc.dram_tensor("wd_s", [DFF, D], BF16, kind="Internal").ap()

    st = contextlib.ExitStack()

    def sb(name, shape, dt):
        return st.enter_context(nc.sbuf_tensor(name, shape, dt))

    def ps(name, shape, dt):
        return st.enter_context(nc.psum_tensor(name, shape, dt))

    with st:
        w_in_sb = sb("w_in_sb", [128, 8, NCOL], BF16)
        w_out_sb = sb("w_out_sb", [128, 8, D], BF16)
        kaT = sb("kaT", [128, CL], BF16)
        va = sb("va", [128, NT + 1, 128], BF16)
        kbT = sb("kbT", [128, 4, CL], BF16)
        vb = sb("vb", [128, NT + 1, 512], BF16)
        xg = sb("xg", [128, 4, D], F32)
        uT = sb("uT", [128, 8, 512], BF16)
        xs2 = [sb(f"xs{i}", [128, D], BF16) for i in range(2)]
        qaT = sb("qaT", [128, 4, 512], BF16)
        qbT = sb("qbT", [128, 4, 512], BF16)
        rq = sb("rq", [128, 640], BF16)
        kq = sb("kq", [128, 640], F32)
        scrA = sb("scrA", [128, 2048], BF16)
        scrB = sb("scrB", [128, 512], F32)
        e_sb = [sb(f"e_sb{i}", [128, 512], F32) for i in range(2)]
        sp_sb = [sb(f"sp_sb{i}", [128, 512], BF16) for i in range(2)]
        a_sb = [sb(f"a_sb{i}", [128, 512], BF16) for i in range(2)]
        acc_sb = [sb(f"acc_sb{i}", [128, 512], BF16) for i in range(2)]
        pT = sb("pT", [128, 3, 512], BF16)
        o_raw = sb("o_raw", [128, 4, 512], F32)
        mixT = sb("mixT", [128, 8, 512], BF16)
        hT = [sb(f"hT{i}", [128, 2, 512], BF16) for i in range(2)]
        wg_sl = [sb(f"wg_sl{i}", [128, 8, 256], BF16) for i in range(2)]
        wu_sl = [sb(f"wu_sl{i}", [128, 8, 256], BF16) for i in range(2)]
        wd_sl = [sb(f"wd_sl{i}", [128, 2, D], BF16) for i in range(2)]
        cos_sb = sb("cos_sb", [128, NT + 1, 32], F32)
        sin_sb = sb("sin_sb", [128, NT + 1, 32], F32)
        nsin_sb = sb("nsin_sb", [128, NT + 1, 32], F32)
        gmix = sb("gmix_sb", [128, 8], F32)
        gffn = sb("gffn_sb", [128, 8], F32)
        gouta = sb("gouta_sb", [128, 4], F32)
        goutb = sb("goutb_sb", [128, 4], F32)
        esink = sb("esink_sb", [128, 4], F32)
        gfin = sb("gfin_sb", [128, D], F32)
        ident = sb("ident", [128, 128], BF16)
        negtri = sb("negtri", [128, 128], BF16)
        negones = sb("negones", [128, 128], BF16)
        ones_bf = sb("ones_bf", [128, 128], BF16)
        stat = sb("stat", [128, 16], F32)
        banks = [ps(f"bank{i}", [128, 512], F32) for i in range(7)] + [ps("bank7", [128, 1024], BF16)]

        S = Sched()
        scrA_f = scrA[:].bitcast(F32)

        S.op("pool", lambda e: e.memset(scrB[:, 0:128], 1.0), writes=["scrB"])
        S.op("pool", lambda e: e.affine_select(out=ident[:], in_=scrB[:, 0:128], pattern=[[-1, 128]], compare_op=ALU.is_equal,
                                               fill=0.0, base=0, channel_multiplier=1), reads=["scrB"], writes=["ident"])
        S.op("pool", lambda e: e.memset(ones_bf[:], 1.0), writes=["ones_bf"])
        S.op("pool", lambda e: e.memset(scrB[:, 0:128], -1.0), reads=[], writes=["scrB"])
        S.op("pool", lambda e: e.memset(negones[:], -1.0), writes=["negones"])
        S.op("pool", lambda e: e.affine_select(out=negtri[:], in_=scrB[:, 0:128], pattern=[[-1, 128]], compare_op=ALU.is_ge,
                                               fill=0.0, base=0, channel_multiplier=1), reads=["scrB"], writes=["negtri"])
        S.op("pool", lambda e: e.memset(xg[:, 0, :], 0.0), writes=["xg0"])
        cast_n = [0]

        def ld(q, dst, src, key, wkeys):
            if q == "pool":
                ck = f"castchain{cast_n[0] % 2}"
                cast_n[0] += 1
                S.dma(q, lambda e: e.dma_start(out=dst, in_=src), key, reads=[ck], writes=wkeys + [ck])
            else:
                S.dma(q, lambda e: e.dma_start(out=dst, in_=src), key, writes=wkeys)

        w_in_v = w_in_d.rearrange("(kc p) n -> p kc n", p=128)
        for h in range(2):
            ld("pool", w_in_sb[:, :, h * 1152:(h + 1) * 1152], w_in_v[:, :, h * 1152:(h + 1) * 1152], f"ld_win{h}", [f"w_in{h}"])
        W_IN = ["w_in0", "w_in1"]
        ld("sp", gmix[:], gmix_d, "ld_c0", ["gmix"])
        ld("sp", gffn[:], gffn_d, "ld_c1", ["gffn"])
        ld("sp", gouta[:], gouta_d, "ld_c2", ["gouta"])
        ld("sp", goutb[:], goutb_d, "ld_c3", ["goutb"])
        ld("sp", esink[:], sink_d, "ld_c4", ["esink"])
        ld("sp", gfin[:], gfin_d, "ld_c5", ["gfin"])
        ld("sp", cos_sb[:], cos_d, "ld_c6", ["cos"])
        ld("sp", sin_sb[:], sin_d, "ld_c7", ["sin"])
        ld("pool", w_out_sb[:], w_out_d.rearrange("(kc p) n -> p kc n", p=128), "ld_wout", ["w_out"])
        def rows2(ap):
            return ap.rearrange("r (t c) -> (r t) c", t=2)
        for h in range(2):
            ld("pool", rows2(wg_s)[h * 1024:(h + 1) * 1024, :], rows2(w_gate_d)[h * 1024:(h + 1) * 1024, :], f"ld_wg{h}", [f"wg_s{h}"])
            ld("pool", rows2(wu_s)[h * 1024:(h + 1) * 1024, :], rows2(w_up_d)[h * 1024:(h + 1) * 1024, :], f"ld_wu{h}", [f"wu_s{h}"])
        for h in range(2):
            ld("pool", wd_s[h * 1408:(h + 1) * 1408, :], w_down_d[h * 1408:(h + 1) * 1408, :], f"ld_wd{h}", [f"wd_s{h}"])

        S.op("act", lambda e: e.activation(out=esink[:], in_=esink[:], func=AF.Exp), reads=["esink"], writes=["esink"])
        S.op("dve", lambda e: e.tensor_scalar(out=nsin_sb[:], in0=sin_sb[:], scalar1=-1.0, scalar2=None, op0=ALU.mult),
             reads=["sin"], writes=["nsin"])
        S.dma("sp", lambda e: e.dma_start(out=xg[0:16, 0, :], in_=meta_d), "ld_x0", reads=[], writes=["xg0"])

        bank_bf = [b[:].bitcast(BF16) for b in banks[:7]] + [banks[7][:]]

        def BK(b):
            return [f"bank{b}_0", f"bank{b}_1"] if b in (3, 4) else [f"bank{b}"]

        def rms_transpose(i, gvec, gkey, tb=None):
            xk = f"xg{i}"
            p = i % 2
            xs = xs2[p]
            xsk, sk = f"xs{p}", f"stat{p}"
            c0 = 4 * p
            tb = 5 + p
            S.op("act", lambda e: e.activation(out=xs[:], in_=xg[:, i, :], func=AF.Square, accum_out=stat[:, c0:c0 + 1]),
                 reads=[xk], writes=[xsk, sk])
            S.op("dve", lambda e: e.tensor_scalar(out=stat[:, c0 + 1:c0 + 2], in0=stat[:, c0:c0 + 1], scalar1=1.0 / D, scalar2=EPS,
                                                  op0=ALU.mult, op1=ALU.add), reads=[sk], writes=[sk])
            S.op("act", lambda e: e.activation(out=stat[:, c0 + 2:c0 + 3], in_=stat[:, c0 + 1:c0 + 2], func=AF.Ln), reads=[sk], writes=[sk])
            S.op("act", lambda e: e.activation(out=stat[:, c0 + 3:c0 + 4], in_=stat[:, c0 + 2:c0 + 3], func=AF.Exp, scale=-0.5),
                 reads=[sk], writes=[sk])
            S.op("dve", lambda e: e.tensor_scalar(out=xs[:], in0=xg[:, i, :], scalar1=stat[:, c0 + 3:c0 + 4], scalar2=None, op0=ALU.mult),
                 reads=[xk, sk], writes=[xsk])
            for kc in range(8):
                S.op("pe", lambda e, kc=kc: e.transpose(out=bank_bf[tb][:, kc * 128:(kc + 1) * 128],
                                                        in_=xs[:, kc * 128:(kc + 1) * 128], identity=ident[:]),
                     reads=[xsk, "ident"], writes=BK(tb), sig=(kc == 7))
            S.op("dve", lambda e: e.tensor_tensor(out=uT[:, :, i * 128:(i + 1) * 128],
                                                  in0=bank_bf[tb][:, :].rearrange("p (k t) -> p k t", k=8),
                                                  in1=gvec[:, :].unsqueeze(2).to_broadcast([128, 8, 128]), op=ALU.mult),
                 reads=BK(tb) + [gkey], writes=[f"uT{i}"])

        def rope(src_ps, nh, dsts, ti, skeys, dkey):
            n = nh * 64
            if DBG_TI0:
                ti = 0
            tcv = scrA_f[:, 0:n].rearrange("p (h t d) -> p h t d", h=nh, t=2)
            tsv = scrA_f[:, 512:512 + n].rearrange("p (h t d) -> p h t d", h=nh, t=2)
            srcv = src_ps.rearrange("p (h t d) -> p h t d", h=nh, t=2)
            cosb = cos_sb[:, ti, :].unsqueeze(1).to_broadcast([128, nh, 32])
            sinb = sin_sb[:, ti, :].unsqueeze(1).to_broadcast([128, nh, 32])
            nsinb = nsin_sb[:, ti, :].unsqueeze(1).to_broadcast([128, nh, 32])
            for t in range(2):
                S.op("dve", lambda e, t=t: e.tensor_tensor(out=tcv[:, :, t, :], in0=srcv[:, :, t, :], in1=cosb, op=ALU.mult),
                     reads=skeys + ["cos"], writes=["scrA"])
            S.op("dve", lambda e: e.tensor_tensor(out=tsv[:, :, 0, :], in0=srcv[:, :, 1, :], in1=nsinb, op=ALU.mult),
                 reads=skeys + ["nsin"], writes=["scrA"])
            S.op("dve", lambda e: e.tensor_tensor(out=tsv[:, :, 1, :], in0=srcv[:, :, 0, :], in1=sinb, op=ALU.mult),
                 reads=skeys + ["sin"], writes=["scrA"])
            for h0, nhh, dst in dsts:
                S.op("dve", lambda e, h0=h0, nhh=nhh, dst=dst: e.tensor_tensor(
                    out=dst, in0=scrA_f[:, h0 * 64:(h0 + nhh) * 64].rearrange("p (h d) -> p h d", h=nhh),
                    in1=scrA_f[:, 512 + h0 * 64:512 + (h0 + nhh) * 64].rearrange("p (h d) -> p h d", h=nhh), op=ALU.add),
                     reads=["scrA"], writes=[dkey])

        def mm(out, lhsT, rhs, start, stop, reads, writes, sig, skip=False):
            S.op("pe", lambda e: e.matmul(out, lhsT=lhsT, rhs=rhs, start=start, stop=stop, skip_group_check=skip),
                 reads=reads, writes=writes, sig=sig)

        rq_q = rq[:, 0:512].rearrange("p (j hh d) -> p hh j d", j=4, hh=2)

        def tok_proj(i, ct, meta=False):
            uk = f"uT{i}"
            lts = [uT[:, kc, i * 128:(i + 1) * 128] for kc in range(8)]
            for kc in range(8):
                if not meta:
                    mm(banks[2][:, :], lts[kc], w_in_sb[:, kc, 0:512], kc == 0, kc == 7, [uk, "w_in0"], BK(2), False)
                mm(banks[3][:, 0:256], lts[kc], w_in_sb[:, kc, 512:768], kc == 0, kc == 7, [uk, "w_in0"], BK(3), False)
                mm(banks[4][:, :], lts[kc], w_in_sb[:, kc, 1792:2304], kc == 0, kc == 7, [uk, "w_in1"], BK(4), kc == 7)
            rows = slice(0, 16) if meta else slice(0, 128)
            vk = "vbm" if meta else f"vb{ct}"
            S.op("act", lambda e: e.copy(out=vb[rows, ct, :], in_=banks[4][rows, :]), reads=BK(4), writes=[vk])
            vak = "vam" if meta else f"va{ct}"
            S.op("act", lambda e: e.copy(out=va[rows, ct, :], in_=banks[3][rows, 128:256]), reads=BK(3), writes=[vak])
            if STOP in (13, 23):
                return
            S.op("act", lambda e: e.copy(out=kq[:, 512:640], in_=banks[3][:, 0:128]), reads=BK(3), writes=["kq_k"])
            rope(kq[:, 512:640], 2, [(0, 2, rq[:, 512:640].rearrange("p (h d) -> p h d", h=2))], ct, ["kq_k"], "rqk")
            if not meta and STOP != 25:
                S.op("act", lambda e: e.copy(out=kq[:, 0:512], in_=banks[2][:, :]), reads=BK(2), writes=["kq_q"])
                rope(kq[:, 0:512], 8, [(0, 4, rq_q[:, 0, :, :]), (4, 4, rq_q[:, 1, :, :])], ct, ["kq_q"], "rqq")
            if STOP in (14, 24, 25):
                return
            tb = 7
            if not meta:
                for j in range(4):
                    S.op("pe", lambda e, j=j: e.transpose(out=bank_bf[tb][:, j * 128:(j + 1) * 128],
                                                          in_=rq[:, j * 128:(j + 1) * 128], identity=ident[:]),
                         reads=["rqq", "ident"], writes=[f"bank{tb}"], sig=False)
            S.op("pe", lambda e: e.transpose(out=bank_bf[tb][:, 512:640], in_=rq[:, 512:640], identity=ident[:]),
                 reads=["rqk", "ident"], writes=[f"bank{tb}"], sig=True)
            if meta:
                S.op("dve", lambda e: e.tensor_copy(out=kaT[:, 0:16], in_=bank_bf[tb][:, 512:528]),
                     reads=[f"bank{tb}"], writes=["kaTm"])
            else:
                c0 = 16 + (ct - 1) * 128
                S.op("dve", lambda e: e.tensor_copy(out=kaT[:, c0:c0 + 128], in_=bank_bf[tb][:, 512:640]),
                     reads=[f"bank{tb}"], writes=[f"kaT{ct}"])
                S.op("dve", lambda e: e.tensor_copy(out=qaT[:, :, i * 128:(i + 1) * 128],
                                                    in_=bank_bf[tb][:, 0:512].rearrange("p (j t) -> p j t", j=4)),
                     reads=[f"bank{tb}"], writes=["qaT"])

        def feat_proj(ntok, gt0, meta=False):
            uks = [f"uT{i}" for i in range((ntok + 127) // 128)]
            nb = 0
            for c in range(8):
                if meta and c < 4:
                    continue
                col0 = 768 + c * 128
                bk = nb % 2
                nb += 1
                for kc in range(8):
                    mm(banks[bk][:, 0:ntok], w_in_sb[:, kc, col0:col0 + 128], uT[:, kc, 0:ntok], kc == 0, kc == 7,
                       uks + ["w_in0", "w_in1"], [f"bank{bk}"], kc == 7)
                if c < 4:
                    S.op("act", lambda e, c=c, bk=bk: e.activation(out=qbT[:, c, :], in_=banks[bk][:, :], func=AF.Copy, scale=0.125),
                         reads=[f"bank{bk}"], writes=["qbT"])
                elif meta:
                    S.op("dve", lambda e, c=c, bk=bk: e.tensor_copy(out=kbT[:, c - 4, 0:16], in_=banks[bk][:, 0:16]),
                         reads=[f"bank{bk}"], writes=["kbTm"])
                else:
                    c0 = 16 + gt0 * 128
                    S.op("dve", lambda e, c=c, bk=bk, c0=c0: e.tensor_copy(out=kbT[:, c - 4, c0:c0 + 512], in_=banks[bk][:, :]),
                         reads=[f"bank{bk}"], writes=[f"kbT{gt0 + 1 + t}" for t in range(4)])

        rms_transpose(0, gmix, "gmix")
        tok_proj(0, 0, meta=True)
        feat_proj(128, 0, meta=True)

        ffn_loads = []

        def ffn_load(sc, slot):
            S.dma("sp", lambda e: e.dma_start(out=wg_sl[slot][:], in_=wg_s.rearrange("(kc p) f -> p kc f", p=128)[:, :, sc * 256:(sc + 1) * 256]),
                  f"ld_wgs{slot}", reads=["wg_s0", "wg_s1"], writes=[f"wg_sl{slot}"])
            S.dma("sp", lambda e: e.dma_start(out=wu_sl[slot][:], in_=wu_s.rearrange("(kc p) f -> p kc f", p=128)[:, :, sc * 256:(sc + 1) * 256]),
                  f"ld_wus{slot}", reads=["wu_s0", "wu_s1"], writes=[f"wu_sl{slot}"])
            S.dma("sp", lambda e: e.dma_start(out=wd_sl[slot][:], in_=wd_s.rearrange("(c p) n -> p c n", p=128)[:, 2 * sc:2 * sc + 2, :]),
                  f"ld_wds{slot}", reads=["wd_s0", "wd_s1"], writes=[f"wd_sl{slot}"])

        def attn_a(gi):
            for i in range(4):
                gt = gi * 4 + i
                ct = gt + 1
                for g in range(2):
                    rows = slice(64 * g, 64 * g + 64)
                    rhs_q = qaT[rows, :, i * 128:(i + 1) * 128]
                    c0 = 16 + gt * 128
                    segs = [("cur", 0, kaT[rows, c0:c0 + 128], 128, f"kaT{ct}", f"va{ct}", va[:, ct, 64 * g:64 * g + 64])]
                    if gt >= 1:
                        segs.append(("prev", 1, kaT[rows, c0 - 128:c0], 128, f"kaT{ct - 1}", f"va{ct - 1}", va[:, ct - 1, 64 * g:64 * g + 64]))
                    segs.append(("meta", 2, kaT[rows, 0:16], 16, "kaTm", "vam", va[0:16, 0, 64 * g:64 * g + 64]))
                    for nm, bk, kap, nk, kkey, vkey, vap in segs:
                        mm(banks[bk][0:nk, :].rearrange("p (j t) -> p j t", j=4), kap, rhs_q, True, True,
                           [kkey, "qaT"], [f"bank{bk}"], True)
                        S.op("act", lambda e, bk=bk, nk=nk: e.activation(out=pT[0:nk, bk, :], in_=banks[bk][0:nk, :], func=AF.Exp, scale=0.125),
                             reads=[f"bank{bk}"], writes=[f"pT{bk}"])
                        if nm == "cur":
                            S.op("pool", lambda e: e.affine_select(out=pT[:, 0, :].rearrange("p (j t) -> p j t", j=4),
                                                                   in_=pT[:, 0, :].rearrange("p (j t) -> p j t", j=4),
                                                                   pattern=[[0, 4], [1, 128]], compare_op=ALU.is_ge, fill=0.0,
                                                                   base=0, channel_multiplier=-1), reads=["pT0"], writes=["pT0"])
                        elif nm == "prev":
                            S.op("pool", lambda e: e.affine_select(out=pT[:, 1, :].rearrange("p (j t) -> p j t", j=4),
                                                                   in_=pT[:, 1, :].rearrange("p (j t) -> p j t", j=4),
                                                                   pattern=[[0, 4], [-1, 128]], compare_op=ALU.is_gt, fill=0.0,
                                                                   base=0, channel_multiplier=1), reads=["pT1"], writes=["pT1"])
                    for par in range(2):
                        prow = slice(64 * par, 64 * par + 64)
                        for si, (nm, bk, kap, nk, kkey, vkey, vap) in enumerate(segs):
                            rhs_p = pT[0:nk, bk, :].rearrange("p (c two t) -> p c two t", c=2, two=2)[:, :, par, :]
                            first, last = si == 0, si == len(segs) - 1
                            mm(banks[3][prow, 0:256].rearrange("p (c t) -> p c t", c=2), vap, rhs_p, first, last,
                               [vkey, f"pT{bk}"], [f"bank3_{par}"], False, skip=True)
                            mm(banks[4][prow, 0:256].rearrange("p (c t) -> p c t", c=2), ones_bf[0:nk, 0:64], rhs_p, first, last,
                               ["ones_bf", f"pT{bk}"], [f"bank4_{par}"], last and par == 1, skip=True)
                    tden = scrB[:, 0:256].rearrange("p (c t) -> p c t", c=2)
                    S.op("dve", lambda e, g=g: e.tensor_tensor(out=tden, in0=banks[4][:, 0:256].rearrange("p (c t) -> p c t", c=2),
                                                               in1=esink[:, 2 * g:2 * g + 2].unsqueeze(2).to_broadcast([128, 2, 128]), op=ALU.add),
                         reads=BK(4) + ["esink"], writes=["scrB"])
                    S.op("dve", lambda e: e.reciprocal(out=scrB[:, 0:256], in_=scrB[:, 0:256]), reads=["scrB"], writes=["scrB"])
                    S.op("dve", lambda e, g=g, i=i: e.tensor_tensor(out=o_raw[:, 2 * g:2 * g + 2, i * 128:(i + 1) * 128],
                                                                   in0=banks[3][:, 0:256].rearrange("p (c t) -> p c t", c=2),
                                                                   in1=tden, op=ALU.mult),
                         reads=BK(3) + ["scrB"], writes=["o_raw"])

        def out_norm(gvec, gkey, c_off):
            sq = scrA[:, :].rearrange("p (c t) -> p c t", c=4)
            S.op("act", lambda e: e.activation(out=sq, in_=o_raw[:], func=AF.Square), reads=["o_raw"], writes=["scrA"])
            for c in range(4):
                mm(banks[5][:, :], ones_bf[:], sq[:, c, :], c == 0, c == 3, ["scrA", "ones_bf"], ["bank5"], c == 3)
            S.op("dve", lambda e: e.tensor_scalar(out=scrB[:], in0=banks[5][:], scalar1=1.0 / 512, scalar2=EPS, op0=ALU.mult, op1=ALU.add),
                 reads=["bank5"], writes=["scrB"])
            S.op("act", lambda e: e.activation(out=scrB[:], in_=scrB[:], func=AF.Ln), reads=["scrB"], writes=["scrB"])
            S.op("act", lambda e: e.activation(out=scrB[:], in_=scrB[:], func=AF.Exp, scale=-0.5), reads=["scrB"], writes=["scrB"])
            for c in range(4):
                S.op("dve", lambda e, c=c: e.scalar_tensor_tensor(out=mixT[:, c_off + c, :], in0=o_raw[:, c, :], scalar=gvec[:, c:c + 1],
                                                                 in1=scrB[:], op0=ALU.mult, op1=ALU.mult),
                     reads=["o_raw", "scrB", gkey], writes=[f"mix{c_off + c}"])

        ZB = (0, 1, 2, 5)

        def attn_b(gi):
            blist = []
            for hp in range(4):
                for par in range(2):
                    blks = [("diag", 4 * gi + d, d) for d in (3, 2, 1, 0)] + [("full", kb, 0) for kb in range(4 * gi - 1, -1, -1)] + [("meta", -1, 0)]
                    for bi, (kind, kb, d) in enumerate(blks):
                        blist.append(dict(hp=hp, par=par, kind=kind, kb=kb, d=d, first=(bi == 0), last=(bi == len(blks) - 1)))
            n = len(blist)

            def geom(b):
                q0 = 128 * b["d"] if b["kind"] == "diag" else 0
                nk = 16 if b["kind"] == "meta" else 128
                return q0, nk

            def s0(i):
                b = blist[i]
                q0, nk = geom(b)
                rows = slice(64 * b["par"], 64 * b["par"] + 64)
                zb = ZB[i % 4]
                if b["kind"] == "meta":
                    kap, kkey = kbT[rows, b["hp"], 0:16], "kbTm"
                else:
                    c0 = 16 + b["kb"] * 128
                    kap, kkey = kbT[rows, b["hp"], c0:c0 + 128], f"kbT{b['kb'] + 1}"
                mm(banks[zb][0:nk, q0:512], kap, qbT[rows, b["hp"], q0:512], True, True, [kkey, "qbT"], [f"bank{zb}"], True)

            def s1a(i):
                b = blist[i]
                q0, nk = geom(b)
                zb, sl = ZB[i % 4], i % 2
                S.op("act", lambda e: e.activation(out=e_sb[sl][0:nk, q0:512], in_=banks[zb][0:nk, q0:512], func=AF.Exp),
                     reads=[f"bank{zb}"], writes=[f"e{sl}"])

            def s1(i):
                b = blist[i]
                q0, nk = geom(b)
                zb, sl = ZB[i % 4], i % 2
                accp, accn = acc_sb[(i - 1) % 2], acc_sb[i % 2]
                kp, kn = f"acc{(i - 1) % 2}", f"acc{i % 2}"
                S.op("act", lambda e: e.activation(out=sp_sb[sl][0:nk, q0:512], in_=e_sb[sl][0:nk, q0:512], func=AF.Ln, bias=1.0),
                     reads=[f"e{sl}"], writes=[f"sp{sl}"])
                if b["kind"] == "diag":
                    S.op("pool", lambda e: e.affine_select(out=sp_sb[sl][:, q0:q0 + 128], in_=sp_sb[sl][:, q0:q0 + 128], pattern=[[1, 128]],
                                                           compare_op=ALU.is_gt, fill=0.0, base=0, channel_multiplier=-1),
                         reads=[f"sp{sl}"], writes=[f"sp{sl}"])
                q1 = q0 + 128 if b["kind"] == "diag" else 0
                has_acc = (q1 < 512) and not (b["kind"] == "diag" and b["d"] == 3)
                mm(banks[zb][0:nk, q0:512], negtri[0:nk, 0:nk], sp_sb[sl][0:nk, q0:512], False, True,
                   ["negtri", f"sp{sl}"], [f"bank{zb}"], not has_acc, skip=True)
                if has_acc:
                    mm(banks[zb][0:nk, q1:512], negones[:, 0:nk], accp[:, q1:512], False, True,
                       ["negones", kp], [f"bank{zb}"], True, skip=True)
                if b["kind"] == "diag":
                    S.op("dve", lambda e: e.tensor_copy(out=accn[:, q0:q0 + 128], in_=sp_sb[sl][:, q0:q0 + 128]),
                         reads=[f"sp{sl}"], writes=[kn])
                    if has_acc:
                        S.op("dve", lambda e: e.tensor_tensor(out=accn[:, q1:512], in0=accp[:, q1:512], in1=sp_sb[sl][:, q1:512], op=ALU.add),
                             reads=[f"sp{sl}", kp], writes=[kn])
                elif b["kind"] == "full":
                    S.op("dve", lambda e: e.tensor_tensor(out=accn[:, :], in0=accp[:, :], in1=sp_sb[sl][:, :], op=ALU.add),
                         reads=[f"sp{sl}", kp], writes=[kn])

            def s2(i):
                b = blist[i]
                q0, nk = geom(b)
                zb, sl = ZB[i % 4], i % 2
                h = 2 * b["hp"] + b["par"]
                ob = 3 + (b["hp"] % 2)
                prow = slice(64 * b["par"], 64 * b["par"] + 64)
                S.op("act", lambda e: e.activation(out=a_sb[sl][0:nk, q0:512], in_=banks[zb][0:nk, q0:512], func=AF.Exp),
                     reads=[f"bank{zb}"], writes=[f"a{sl}"])
                if b["kind"] == "diag":
                    S.op("pool", lambda e: e.affine_select(out=a_sb[sl][:, q0:q0 + 128], in_=a_sb[sl][:, q0:q0 + 128], pattern=[[1, 128]],
                                                           compare_op=ALU.is_gt, fill=0.0, base=0, channel_multiplier=-1),
                         reads=[f"a{sl}"], writes=[f"a{sl}"])
                if b["kind"] == "meta":
                    vap, vkey = vb[0:16, 0, h * 64:(h + 1) * 64], "vbm"
                else:
                    vap, vkey = vb[:, b["kb"] + 1, h * 64:(h + 1) * 64], f"vb{b['kb'] + 1}"
                okey = f"bank{ob}_{b['par']}"
                mm(banks[ob][prow, q0:512], vap, a_sb[sl][0:nk, q0:512], b["first"], b["last"], [vkey, f"a{sl}"], [okey], b["last"], skip=True)
                if b["last"]:
                    S.op("dve", lambda e: e.tensor_copy(out=o_raw[prow, b["hp"], :], in_=banks[ob][prow, :]),
                         reads=[okey], writes=["o_raw"])

            for t in range(-3, n):
                if t + 3 < n:
                    s0(t + 3)
                if 0 <= t + 2 < n:
                    s1a(t + 2)
                if 0 <= t + 1 < n:
                    s1(t + 1)
                if t >= 0:
                    s2(t)

        def outproj(i):
            for kc in range(8):
                for hf in range(2):
                    mm(banks[5 + hf][:, :], mixT[:, kc, i * 128:(i + 1) * 128], w_out_sb[:, kc, hf * 512:(hf + 1) * 512], kc == 0, kc == 7,
                       [f"mix{kc}", "w_out"], [f"bank{5 + hf}"], kc == 7 and hf == 1)
            for hf in range(2):
                S.op("dve", lambda e, hf=hf: e.tensor_tensor(out=xg[:, i, hf * 512:(hf + 1) * 512], in0=xg[:, i, hf * 512:(hf + 1) * 512],
                                                             in1=banks[5 + hf][:, :], op=ALU.add),
                     reads=[f"bank{5 + hf}", f"xg{i}"], writes=[f"xg{i}"])

        def ffn(state):
            uks = [f"uT{i}" for i in range(4)]
            n0 = state["n"]
            dcount = [0]

            def gu(sc):
                slot, hb = (n0 + sc) % 2, sc % 2
                for fc in range(2):
                    gb, ub = 2 * fc, 2 * fc + 1
                    for kc in range(8):
                        mm(banks[gb][:, :], wg_sl[slot][:, kc, fc * 128:(fc + 1) * 128], uT[:, kc, :], kc == 0, kc == 7,
                           uks + [f"wg_sl{slot}"], BK(gb), False)
                    for kc in range(8):
                        mm(banks[ub][:, :], wu_sl[slot][:, kc, fc * 128:(fc + 1) * 128], uT[:, kc, :], kc == 0, kc == 7,
                           uks + [f"wu_sl{slot}"], BK(ub), kc == 7)
                    S.op("act", lambda e, gb=gb: e.activation(out=scrB[:], in_=banks[gb][:], func=AF.Silu), reads=BK(gb), writes=["scrB"])
                    S.op("dve", lambda e, ub=ub, fc=fc, hb=hb: e.tensor_tensor(out=hT[hb][:, fc, :], in0=scrB[:], in1=banks[ub][:], op=ALU.mult),
                         reads=["scrB"] + BK(ub), writes=[f"hT{hb}"])

            def down(sc):
                slot, hb = (n0 + sc) % 2, sc % 2
                for i in range(4):
                    for hf in range(2):
                        db = 4 + (dcount[0] % 3)
                        dcount[0] += 1
                        for fc in range(2):
                            mm(banks[db][:, :], hT[hb][:, fc, i * 128:(i + 1) * 128], wd_sl[slot][:, fc, hf * 512:(hf + 1) * 512], fc == 0, fc == 1,
                               [f"hT{hb}", f"wd_sl{slot}"], BK(db), fc == 1)
                        S.op("dve", lambda e, i=i, hf=hf, db=db: e.tensor_tensor(out=xg[:, i, hf * 512:(hf + 1) * 512],
                                                                                 in0=xg[:, i, hf * 512:(hf + 1) * 512], in1=banks[db][:, :], op=ALU.add),
                             reads=BK(db) + [f"xg{i}"], writes=[f"xg{i}"])
                nxt = state["loads"]
                if nxt < state["total"]:
                    ffn_load(nxt % NSC, slot)
                    state["loads"] += 1

            gu(0)
            for sc in range(NSC):
                if sc + 1 < NSC:
                    gu(sc + 1)
                down(sc)
            state["n"] += NSC

        def final_norm(i):
            xk = f"xg{i}"
            p = i % 2
            xs = xs2[p]
            xsk, sk = f"xs{p}", f"statf{p}"
            c0 = 8 + 4 * p
            S.op("act", lambda e: e.activation(out=xs[:], in_=xg[:, i, :], func=AF.Square, accum_out=stat[:, c0:c0 + 1]),
                 reads=[xk], writes=[xsk, sk])
            S.op("dve", lambda e: e.tensor_scalar(out=stat[:, c0 + 1:c0 + 2], in0=stat[:, c0:c0 + 1], scalar1=1.0 / D, scalar2=EPS,
                                                  op0=ALU.mult, op1=ALU.add), reads=[sk], writes=[sk])
            S.op("act", lambda e: e.activation(out=stat[:, c0 + 2:c0 + 3], in_=stat[:, c0 + 1:c0 + 2], func=AF.Ln), reads=[sk], writes=[sk])
            S.op("act", lambda e: e.activation(out=stat[:, c0 + 3:c0 + 4], in_=stat[:, c0 + 2:c0 + 3], func=AF.Exp, scale=-0.5),
                 reads=[sk], writes=[sk])
            S.op("dve", lambda e: e.scalar_tensor_tensor(out=xg[:, i, :], in0=xg[:, i, :], scalar=stat[:, c0 + 3:c0 + 4], in1=gfin[:],
                                                         op0=ALU.mult, op1=ALU.mult),
                 reads=[xk, sk, "gfin"], writes=[xk])

        total_groups = NSEQ * NG
        fstate = dict(n=0, loads=0, total=total_groups * NSC)
        for _ in range(2):
            ffn_load(fstate["loads"] % NSC, fstate["loads"] % 2)
            fstate["loads"] += 1
        groups = [(sq, gi) for sq in range(NSEQ) for gi in range(NG)]

        def load_x(sq, gi, i):
            xv = x_d[sq, gi * 512 + i * 128:gi * 512 + (i + 1) * 128, :]
            S.dma("sp", lambda e: e.dma_start(out=xg[:, i, :], in_=xv), f"ld_x{i}", writes=[f"xg{i}"])

        def store_x(sq, gi, i):
            ov = out_d[sq, gi * 512 + i * 128:gi * 512 + (i + 1) * 128, :]
            S.dma("sp", lambda e: e.dma_start(out=ov, in_=xg[:, i, :]), f"st_out{i}", reads=[f"xg{i}"], final=True)

        for i in range(4):
            load_x(0, 0, i)
        for gidx, (sq, gi) in enumerate(groups):
            rms_transpose(0, gmix, "gmix")
            for i in range(4):
                if i + 1 < 4:
                    rms_transpose(i + 1, gmix, "gmix")
                tok_proj(i, gi * 4 + i + 1)
            feat_proj(512, gi * 4)
            attn_a(gi)
            out_norm(gouta, "gouta", 0)
            attn_b(gi)
            out_norm(goutb, "goutb", 4)
            for i in range(4):
                outproj(i)
                if i >= 1:
                    rms_transpose(i - 1, gffn, "gffn")
            rms_transpose(3, gffn, "gffn")
            ffn(fstate)
            for i in range(4):
                final_norm(i)
                store_x(sq, gi, i)
                if gidx + 1 < len(groups):
                    load_x(groups[gidx + 1][0], groups[gidx + 1][1], i)
        S.emit(nc)
    return nc


def _tables(SEQ):
    NT = SEQ // 128
    half = 32
    inv_freq = (10000.0 ** (-np.arange(half, dtype=np.float32) / half)).astype(np.float32)
    cos = np.zeros((128, NT + 1, 32), np.float32)
    sin = np.zeros((128, NT + 1, 32), np.float32)
    r = np.arange(128)
    for t in range(NT + 1):
        pos = (r if t == 0 else 16 + 128 * (t - 1) + r).astype(np.float32)
        ang = (pos[:, None] * inv_freq[None, :]).astype(np.float32)
        cos[:, t, :] = np.cos(ang)
        sin[:, t, :] = np.sin(ang)
    return cos, sin


def _in_maps(x, meta_tokens, norm_mix, w_in, sinks, norm_out_a, norm_out_b, w_out, norm_ffn, w_gate, w_up, w_down,
             norm_final, n_cores, nseq):
    SEQ = x.shape[1]
    cos, sin = _tables(SEQ)
    c = np.ascontiguousarray
    f = lambda a: np.asarray(a, dtype=np.float32)
    sk = f(sinks).reshape(8)
    sinkl = np.zeros((128, 4), np.float32)
    for ch in range(4):
        sinkl[0:64, ch] = sk[2 * ch]
        sinkl[64:128, ch] = sk[2 * ch + 1]
    common = {
        "meta": c(f(meta_tokens)),
        "w_in": c(f(w_in)[0]), "w_out": c(f(w_out)[0]), "w_gate": c(f(w_gate)[0]), "w_up": c(f(w_up)[0]), "w_down": c(f(w_down)[0]),
        "gmix": c(f(norm_mix).reshape(8, 128).T), "gffn": c(f(norm_ffn).reshape(8, 128).T),
        "gouta": c(f(norm_out_a).reshape(4, 128).T), "goutb": c(f(norm_out_b).reshape(4, 128).T),
        "sinkl": sinkl, "gfin": c(np.broadcast_to(f(norm_final).reshape(1, D), (128, D))),
        "cost": cos, "sint": sin,
    }
    xs = f(x)
    maps = []
    for i in range(n_cores):
        m = dict(common)
        m["x"] = c(xs[i * nseq:(i + 1) * nseq])
        maps.append(m)
    return maps


def kernel(x, meta_tokens, norm_mix, w_in, sinks, norm_out_a, norm_out_b, w_out, norm_ffn, w_gate, w_up, w_down, norm_final):
    n_cores = 8
    B, SEQ, _ = x.shape
    nseq = B // n_cores
    nc = build(nseq, SEQ)
    maps = _in_maps(x, meta_tokens, norm_mix, w_in, sinks, norm_out_a, norm_out_b, w_out, norm_ffn, w_gate, w_up, w_down,
                    norm_final, n_cores, nseq)
    res = run_bass_kernel_spmd(nc, maps, core_ids=list(range(n_cores)))
    return np.concatenate([np.asarray(r["out"], dtype=np.float32) for r in res.results], axis=0)
```

```python
import contextlib
import numpy as np
import concourse.bass as bass
import concourse.mybir as mybir
from concourse.bass_utils import run_bass_kernel_spmd

F32 = mybir.dt.float32
BF16 = mybir.dt.bfloat16
AF = mybir.ActivationFunctionType
ALU = mybir.AluOpType

ENGS = ("pe", "act", "dve", "pool", "sp")

D = 1024
NCOL = 2304
DFF = 2816
NSC = DFF // 256
EPS = 1e-6
STOP = 99
import os as _os
DBG_TI0 = int(_os.environ.get('DBG_TI0', '0'))


class Sched:
    def __init__(self):
        self.ops = {e: [] for e in ENGS}
        self.cnt = {e: 0 for e in ENGS}
        self.dma_cnt = {}
        self.last_w = {}
        self.readers = {}
        self.seen = {e: {} for e in ENGS}
        self.final_tokens = []
        self.pe_pending = False

    def _deps(self, eng, reads, writes):
        deps = {}

        def add(tok):
            if tok is None:
                return
            k, v = tok
            if eng == "pe" and k == "pe":
                return
            if deps.get(k, 0) < v:
                deps[k] = v

        for r in reads:
            add(self.last_w.get(r))
        for w in writes:
            add(self.last_w.get(w))
            for t in self.readers.get(w, ()):
                add(t)
        out = []
        for k, v in deps.items():
            if self.seen[eng].get(k, 0) >= v:
                continue
            self.seen[eng][k] = v
            out.append((k, v))
        return out

    def _commit(self, tok, reads, writes):
        for r in reads:
            self.readers.setdefault(r, []).append(tok)
        for w in writes:
            self.last_w[w] = tok
            self.readers[w] = []

    def op(self, eng, fn, reads=(), writes=(), sig=True, pre=False):
        waits = self._deps(eng, reads, writes)
        if eng == "pe" and not sig:
            tok = ("pe", self.cnt["pe"] + 1)
            incs = []
            self.pe_pending = True
        else:
            self.cnt[eng] += 1
            tok = (eng, self.cnt[eng])
            incs = [(eng, 1)]
            if eng == "pe":
                self.pe_pending = False
        self.ops[eng].append((fn, waits, incs, False))
        self._commit(tok, reads, writes)
        return tok

    def dma(self, q, fn, semkey, reads=(), writes=(), final=False):
        waits = self._deps(q, reads, writes)
        self.dma_cnt[semkey] = self.dma_cnt.get(semkey, 0) + 16
        tok = (semkey, self.dma_cnt[semkey])
        self.ops[q].append((fn, waits, [(semkey, 16)], True))
        self._commit(tok, reads, writes)
        if final:
            self.final_tokens.append(tok)
        return tok

    def emit(self, nc):
        assert not self.pe_pending
        semkeys = list(ENGS[:4]) + list(self.dma_cnt.keys())
        with contextlib.ExitStack() as st:
            sems = {}
            for k in semkeys:
                sems[k] = st.enter_context(nc.semaphore("s_" + str(k)))
            block = st.enter_context(nc.Block())
            finals = {}
            for k, v in self.final_tokens:
                finals[k] = max(finals.get(k, 0), v)

            def run(engname):
                def body(e):
                    for fn, waits, incs, pre_wait in self.ops[engname]:
                        if pre_wait:
                            for k, v in waits:
                                e.wait_ge(sems[k], v)
                            ins = fn(e)
                        else:
                            for k, v in waits[:-1]:
                                e.wait_ge(sems[k], v)
                            ins = fn(e)
                            if waits:
                                k, v = waits[-1]
                                ins._wait_ge(sems[k], v)
                        for k, a in incs:
                            ins.then_inc(sems[k], a)
                    if engname == "sp":
                        for k, v in finals.items():
                            e.wait_ge(sems[k], v)
                return body

            block.tensor(run("pe"))
            block.scalar(run("act"))
            block.vector(run("dve"))
            block.gpsimd(run("pool"))
            block.sync(run("sp"))


def build(NSEQ, SEQ):
    NG = SEQ // 512
    NT = SEQ // 128
    CL = 16 + SEQ
    nc = bass.Bass("TRN2", target_bir_lowering=False)

    def din(name, shape, dt=F32):
        return nc.dram_tensor(name, shape, dt, kind="ExternalInput").ap()

    x_d = din("x", [NSEQ, SEQ, D])
    meta_d = din("meta", [16, D])
    w_in_d = din("w_in", [D, NCOL])
    w_out_d = din("w_out", [D, D])
    w_gate_d = din("w_gate", [D, DFF])
    w_up_d = din("w_up", [D, DFF])
    w_down_d = din("w_down", [DFF, D])
    gmix_d = din("gmix", [128, 8])
    gffn_d = din("gffn", [128, 8])
    gouta_d = din("gouta", [128, 4])
    goutb_d = din("goutb", [128, 4])
    sink_d = din("sinkl", [128, 4])
    gfin_d = din("gfin", [128, D])
    cos_d = din("cost", [128, NT + 1, 32])
    sin_d = din("sint", [128, NT + 1, 32])
    out_d = nc.dram_tensor("out", [NSEQ, SEQ, D], F32, kind="ExternalOutput").ap()
    wg_s = nc.dram_tensor("wg_s", [D, DFF], BF16, kind="Internal").ap()
    wu_s = nc.dram_tensor("wu_s", [D, DFF], BF16, kind="Internal").ap()
    wd_s = nc.dram_tensor("wd_s", [DFF, D], BF16, kind="Internal").ap()

    st = contextlib.ExitStack()

    def sb(name, shape, dt):
        return st.enter_context(nc.sbuf_tensor(name, shape, dt))

    def ps(name, shape, dt):
        return st.enter_context(nc.psum_tensor(name, shape, dt))

    with st:
        w_in_sb = sb("w_in_sb", [128, 8, NCOL], BF16)
        w_out_sb = sb("w_out_sb", [128, 8, D], BF16)
        kaT = sb("kaT", [128, CL], BF16)
        va = sb("va", [128, NT + 1, 128], BF16)
        kbT = sb("kbT", [128, 4, CL], BF16)
        vb = sb("vb", [128, NT + 1, 512], BF16)
        xg = sb("xg", [128, 4, D], F32)
        uT = sb("uT", [128, 8, 512], BF16)
        xs2 = [sb(f"xs{i}", [128, D], BF16) for i in range(2)]
        qaT = sb("qaT", [128, 4, 512], BF16)
        qbT = sb("qbT", [128, 4, 512], BF16)
        rq = sb("rq", [128, 640], BF16)
        kq = sb("kq", [128, 640], F32)
        scrA = sb("scrA", [128, 2048], BF16)
        scrB = sb("scrB", [128, 512], F32)
        e_sb = [sb(f"e_sb{i}", [128, 512], F32) for i in range(2)]
        sp_sb = [sb(f"sp_sb{i}", [128, 512], BF16) for i in range(2)]
        a_sb = [sb(f"a_sb{i}", [128, 512], BF16) for i in range(2)]
        acc_sb = [sb(f"acc_sb{i}", [128, 512], BF16) for i in range(2)]
        pT = sb("pT", [128, 3, 512], BF16)
        o_raw = sb("o_raw", [128, 4, 512], F32)
        mixT = sb("mixT", [128, 8, 512], BF16)
        hT = [sb(f"hT{i}", [128, 2, 512], BF16) for i in range(2)]
        wg_sl = [sb(f"wg_sl{i}", [128, 8, 256], BF16) for i in range(2)]
        wu_sl = [sb(f"wu_sl{i}", [128, 8, 256], BF16) for i in range(2)]
        wd_sl = [sb(f"wd_sl{i}", [128, 2, D], BF16) for i in range(2)]
        cos_sb = sb("cos_sb", [128, NT + 1, 32], F32)
        sin_sb = sb("sin_sb", [128, NT + 1, 32], F32)
        nsin_sb = sb("nsin_sb", [128, NT + 1, 32], F32)
        gmix = sb("gmix_sb", [128, 8], F32)
        gffn = sb("gffn_sb", [128, 8], F32)
        gouta = sb("gouta_sb", [128, 4], F32)
        goutb = sb("goutb_sb", [128, 4], F32)
        esink = sb("esink_sb", [128, 4], F32)
        gfin = sb("gfin_sb", [128, D], F32)
        ident = sb("ident", [128, 128], BF16)
        negtri = sb("negtri", [128, 128], BF16)
        negones = sb("negones", [128, 128], BF16)
        ones_bf = sb("ones_bf", [128, 128], BF16)
        stat = sb("stat", [128, 16], F32)
        banks = [ps(f"bank{i}", [128, 512], F32) for i in range(7)] + [ps("bank7", [128, 1024], BF16)]

        S = Sched()
        scrA_f = scrA[:].bitcast(F32)

        S.op("pool", lambda e: e.memset(scrB[:, 0:128], 1.0), writes=["scrB"])
        S.op("pool", lambda e: e.affine_select(out=ident[:], in_=scrB[:, 0:128], pattern=[[-1, 128]], compare_op=ALU.is_equal,
                                               fill=0.0, base=0, channel_multiplier=1), reads=["scrB"], writes=["ident"])
        S.op("pool", lambda e: e.memset(ones_bf[:], 1.0), writes=["ones_bf"])
        S.op("pool", lambda e: e.memset(scrB[:, 0:128], -1.0), reads=[], writes=["scrB"])
        S.op("pool", lambda e: e.memset(negones[:], -1.0), writes=["negones"])
        S.op("pool", lambda e: e.affine_select(out=negtri[:], in_=scrB[:, 0:128], pattern=[[-1, 128]], compare_op=ALU.is_ge,
                                               fill=0.0, base=0, channel_multiplier=1), reads=["scrB"], writes=["negtri"])
        S.op("pool", lambda e: e.memset(xg[:, 0, :], 0.0), writes=["xg0"])
        cast_n = [0]

        def ld(q, dst, src, key, wkeys):
            if q == "pool":
                ck = f"castchain{cast_n[0] % 2}"
                cast_n[0] += 1
                S.dma(q, lambda e: e.dma_start(out=dst, in_=src), key, reads=[ck], writes=wkeys + [ck])
            else:
                S.dma(q, lambda e: e.dma_start(out=dst, in_=src), key, writes=wkeys)

        w_in_v = w_in_d.rearrange("(kc p) n -> p kc n", p=128)
        for h in range(2):
            ld("pool", w_in_sb[:, :, h * 1152:(h + 1) * 1152], w_in_v[:, :, h * 1152:(h + 1) * 1152], f"ld_win{h}", [f"w_in{h}"])
        W_IN = ["w_in0", "w_in1"]
        ld("sp", gmix[:], gmix_d, "ld_c0", ["gmix"])
        ld("sp", gffn[:], gffn_d, "ld_c1", ["gffn"])
        ld("sp", gouta[:], gouta_d, "ld_c2", ["gouta"])
        ld("sp", goutb[:], goutb_d, "ld_c3", ["goutb"])
        ld("sp", esink[:], sink_d, "ld_c4", ["esink"])
        ld("sp", gfin[:], gfin_d, "ld_c5", ["gfin"])
        ld("sp", cos_sb[:], cos_d, "ld_c6", ["cos"])
        ld("sp", sin_sb[:], sin_d, "ld_c7", ["sin"])
        ld("pool", w_out_sb[:], w_out_d.rearrange("(kc p) n -> p kc n", p=128), "ld_wout", ["w_out"])
        def rows2(ap):
            return ap.rearrange("r (t c) -> (r t) c", t=2)
        for h in range(2):
            ld("pool", rows2(wg_s)[h * 1024:(h + 1) * 1024, :], rows2(w_gate_d)[h * 1024:(h + 1) * 1024, :], f"ld_wg{h}", [f"wg_s{h}"])
            ld("pool", rows2(wu_s)[h * 1024:(h + 1) * 1024, :], rows2(w_up_d)[h * 1024:(h + 1) * 1024, :], f"ld_wu{h}", [f"wu_s{h}"])
        for h in range(2):
            ld("pool", wd_s[h * 1408:(h + 1) * 1408, :], w_down_d[h * 1408:(h + 1) * 1408, :], f"ld_wd{h}", [f"wd_s{h}"])

        S.op("act", lambda e: e.activation(out=esink[:], in_=esink[:], func=AF.Exp), reads=["esink"], writes=["esink"])
        S.op("dve", lambda e: e.tensor_scalar(out=nsin_sb[:], in0=sin_sb[:], scalar1=-1.0, scalar2=None, op0=ALU.mult),
             reads=["sin"], writes=["nsin"])
        S.dma("sp", lambda e: e.dma_start(out=xg[0:16, 0, :], in_=meta_d), "ld_x0", reads=[], writes=["xg0"])

        bank_bf = [b[:].bitcast(BF16) for b in banks[:7]] + [banks[7][:]]

        def BK(b):
            return [f"bank{b}_0", f"bank{b}_1"] if b in (3, 4) else [f"bank{b}"]

        def rms_transpose(i, gvec, gkey, tb=None):
            xk = f"xg{i}"
            p = i % 2
            xs = xs2[p]
            xsk, sk = f"xs{p}", f"stat{p}"
            c0 = 4 * p
            tb = 5 + p
            S.op("act", lambda e: e.activation(out=xs[:], in_=xg[:, i, :], func=AF.Square, accum_out=stat[:, c0:c0 + 1]),
                 reads=[xk], writes=[xsk, sk])
            S.op("dve", lambda e: e.tensor_scalar(out=stat[:, c0 + 1:c0 + 2], in0=stat[:, c0:c0 + 1], scalar1=1.0 / D, scalar2=EPS,
                                                  op0=ALU.mult, op1=ALU.add), reads=[sk], writes=[sk])
            S.op("act", lambda e: e.activation(out=stat[:, c0 + 2:c0 + 3], in_=stat[:, c0 + 1:c0 + 2], func=AF.Ln), reads=[sk], writes=[sk])
            S.op("act", lambda e: e.activation(out=stat[:, c0 + 3:c0 + 4], in_=stat[:, c0 + 2:c0 + 3], func=AF.Exp, scale=-0.5),
                 reads=[sk], writes=[sk])
            S.op("dve", lambda e: e.tensor_scalar(out=xs[:], in0=xg[:, i, :], scalar1=stat[:, c0 + 3:c0 + 4], scalar2=None, op0=ALU.mult),
                 reads=[xk, sk], writes=[xsk])
            for kc in range(8):
                S.op("pe", lambda e, kc=kc: e.transpose(out=bank_bf[tb][:, kc * 128:(kc + 1) * 128],
                                                        in_=xs[:, kc * 128:(kc + 1) * 128], identity=ident[:]),
                     reads=[xsk, "ident"], writes=BK(tb), sig=(kc == 7))
            S.op("dve", lambda e: e.tensor_tensor(out=uT[:, :, i * 128:(i + 1) * 128],
                                                  in0=bank_bf[tb][:, :].rearrange("p (k t) -> p k t", k=8),
                                                  in1=gvec[:, :].unsqueeze(2).to_broadcast([128, 8, 128]), op=ALU.mult),
                 reads=BK(tb) + [gkey], writes=[f"uT{i}"])

        def rope(src_ps, nh, dsts, ti, skeys, dkey):
            n = nh * 64
            if DBG_TI0:
                ti = 0
            tcv = scrA_f[:, 0:n].rearrange("p (h t d) -> p h t d", h=nh, t=2)
            tsv = scrA_f[:, 512:512 + n].rearrange("p (h t d) -> p h t d", h=nh, t=2)
            srcv = src_ps.rearrange("p (h t d) -> p h t d", h=nh, t=2)
            cosb = cos_sb[:, ti, :].unsqueeze(1).to_broadcast([128, nh, 32])
            sinb = sin_sb[:, ti, :].unsqueeze(1).to_broadcast([128, nh, 32])
            nsinb = nsin_sb[:, ti, :].unsqueeze(1).to_broadcast([128, nh, 32])
            for t in range(2):
                S.op("dve", lambda e, t=t: e.tensor_tensor(out=tcv[:, :, t, :], in0=srcv[:, :, t, :], in1=cosb, op=ALU.mult),
                     reads=skeys + ["cos"], writes=["scrA"])
            S.op("dve", lambda e: e.tensor_tensor(out=tsv[:, :, 0, :], in0=srcv[:, :, 1, :], in1=nsinb, op=ALU.mult),
                 reads=skeys + ["nsin"], writes=["scrA"])
            S.op("dve", lambda e: e.tensor_tensor(out=tsv[:, :, 1, :], in0=srcv[:, :, 0, :], in1=sinb, op=ALU.mult),
                 reads=skeys + ["sin"], writes=["scrA"])
            for h0, nhh, dst in dsts:
                S.op("dve", lambda e, h0=h0, nhh=nhh, dst=dst: e.tensor_tensor(
                    out=dst, in0=scrA_f[:, h0 * 64:(h0 + nhh) * 64].rearrange("p (h d) -> p h d", h=nhh),
                    in1=scrA_f[:, 512 + h0 * 64:512 + (h0 + nhh) * 64].rearrange("p (h d) -> p h d", h=nhh), op=ALU.add),
                     reads=["scrA"], writes=[dkey])

        def mm(out, lhsT, rhs, start, stop, reads, writes, sig, skip=False):
            S.op("pe", lambda e: e.matmul(out, lhsT=lhsT, rhs=rhs, start=start, stop=stop, skip_group_check=skip),
                 reads=reads, writes=writes, sig=sig)

        rq_q = rq[:, 0:512].rearrange("p (j hh d) -> p hh j d", j=4, hh=2)

        def tok_proj(i, ct, meta=False):
            uk = f"uT{i}"
            lts = [uT[:, kc, i * 128:(i + 1) * 128] for kc in range(8)]
            for kc in range(8):
                if not meta:
                    mm(banks[2][:, :], lts[kc], w_in_sb[:, kc, 0:512], kc == 0, kc == 7, [uk, "w_in0"], BK(2), False)
                mm(banks[3][:, 0:256], lts[kc], w_in_sb[:, kc, 512:768], kc == 0, kc == 7, [uk, "w_in0"], BK(3), False)
                mm(banks[4][:, :], lts[kc], w_in_sb[:, kc, 1792:2304], kc == 0, kc == 7, [uk, "w_in1"], BK(4), kc == 7)
            rows = slice(0, 16) if meta else slice(0, 128)
            vk = "vbm" if meta else f"vb{ct}"
            S.op("act", lambda e: e.copy(out=vb[rows, ct, :], in_=banks[4][rows, :]), reads=BK(4), writes=[vk])
            vak = "vam" if meta else f"va{ct}"
            S.op("act", lambda e: e.copy(out=va[rows, ct, :], in_=banks[3][rows, 128:256]), reads=BK(3), writes=[vak])
            if STOP in (13, 23):
                return
            S.op("act", lambda e: e.copy(out=kq[:, 512:640], in_=banks[3][:, 0:128]), reads=BK(3), writes=["kq_k"])
            rope(kq[:, 512:640], 2, [(0, 2, rq[:, 512:640].rearrange("p (h d) -> p h d", h=2))], ct, ["kq_k"], "rqk")
            if not meta and STOP != 25:
                S.op("act", lambda e: e.copy(out=kq[:, 0:512], in_=banks[2][:, :]), reads=BK(2), writes=["kq_q"])
                rope(kq[:, 0:512], 8, [(0, 4, rq_q[:, 0, :, :]), (4, 4, rq_q[:, 1, :, :])], ct, ["kq_q"], "rqq")
            if STOP in (14, 24, 25):
                return
            tb = 7
            if not meta:
                for j in range(4):
                    S.op("pe", lambda e, j=j: e.transpose(out=bank_bf[tb][:, j * 128:(j + 1) * 128],
                                                          in_=rq[:, j * 128:(j + 1) * 128], identity=ident[:]),
                         reads=["rqq", "ident"], writes=[f"bank{tb}"], sig=False)
            S.op("pe", lambda e: e.transpose(out=bank_bf[tb][:, 512:640], in_=rq[:, 512:640], identity=ident[:]),
                 reads=["rqk", "ident"], writes=[f"bank{tb}"], sig=True)
            if meta:
                S.op("dve", lambda e: e.tensor_copy(out=kaT[:, 0:16], in_=bank_bf[tb][:, 512:528]),
                     reads=[f"bank{tb}"], writes=["kaTm"])
            else:
                c0 = 16 + (ct - 1) * 128
                S.op("dve", lambda e: e.tensor_copy(out=kaT[:, c0:c0 + 128], in_=bank_bf[tb][:, 512:640]),
                     reads=[f"bank{tb}"], writes=[f"kaT{ct}"])
                S.op("dve", lambda e: e.tensor_copy(out=qaT[:, :, i * 128:(i + 1) * 128],
                                                    in_=bank_bf[tb][:, 0:512].rearrange("p (j t) -> p j t", j=4)),
                     reads=[f"bank{tb}"], writes=["qaT"])

        def feat_proj(ntok, gt0, meta=False):
            uks = [f"uT{i}" for i in range((ntok + 127) // 128)]
            nb = 0
            for c in range(8):
                if meta and c < 4:
                    continue
                col0 = 768 + c * 128
                bk = nb % 2
                nb += 1
                for kc in range(8):
                    mm(banks[bk][:, 0:ntok], w_in_sb[:, kc, col0:col0 + 128], uT[:, kc, 0:ntok], kc == 0, kc == 7,
                       uks + ["w_in0", "w_in1"], [f"bank{bk}"], kc == 7)
                if c < 4:
                    S.op("act", lambda e, c=c, bk=bk: e.activation(out=qbT[:, c, :], in_=banks[bk][:, :], func=AF.Copy, scale=0.125),
                         reads=[f"bank{bk}"], writes=["qbT"])
                elif meta:
                    S.op("dve", lambda e, c=c, bk=bk: e.tensor_copy(out=kbT[:, c - 4, 0:16], in_=banks[bk][:, 0:16]),
                         reads=[f"bank{bk}"], writes=["kbTm"])
                else:
                    c0 = 16 + gt0 * 128
                    S.op("dve", lambda e, c=c, bk=bk, c0=c0: e.tensor_copy(out=kbT[:, c - 4, c0:c0 + 512], in_=banks[bk][:, :]),
                         reads=[f"bank{bk}"], writes=[f"kbT{gt0 + 1 + t}" for t in range(4)])

        rms_transpose(0, gmix, "gmix")
        tok_proj(0, 0, meta=True)
        feat_proj(128, 0, meta=True)

        ffn_loads = []

        def ffn_load(sc, slot):
            S.dma("sp", lambda e: e.dma_start(out=wg_sl[slot][:], in_=wg_s.rearrange("(kc p) f -> p kc f", p=128)[:, :, sc * 256:(sc + 1) * 256]),
                  f"ld_wgs{slot}", reads=["wg_s0", "wg_s1"], writes=[f"wg_sl{slot}"])
            S.dma("sp", lambda e: e.dma_start(out=wu_sl[slot][:], in_=wu_s.rearrange("(kc p) f -> p kc f", p=128)[:, :, sc * 256:(sc + 1) * 256]),
                  f"ld_wus{slot}", reads=["wu_s0", "wu_s1"], writes=[f"wu_sl{slot}"])
            S.dma("sp", lambda e: e.dma_start(out=wd_sl[slot][:], in_=wd_s.rearrange("(c p) n -> p c n", p=128)[:, 2 * sc:2 * sc + 2, :]),
                  f"ld_wds{slot}", reads=["wd_s0", "wd_s1"], writes=[f"wd_sl{slot}"])

        def attn_a(gi):
            for i in range(4):
                gt = gi * 4 + i
                ct = gt + 1
                for g in range(2):
                    rows = slice(64 * g, 64 * g + 64)
                    rhs_q = qaT[rows, :, i * 128:(i + 1) * 128]
                    c0 = 16 + gt * 128
                    segs = [("cur", 0, kaT[rows, c0:c0 + 128], 128, f"kaT{ct}", f"va{ct}", va[:, ct, 64 * g:64 * g + 64])]
                    if gt >= 1:
                        segs.append(("prev", 1, kaT[rows, c0 - 128:c0], 128, f"kaT{ct - 1}", f"va{ct - 1}", va[:, ct - 1, 64 * g:64 * g + 64]))
                    segs.append(("meta", 2, kaT[rows, 0:16], 16, "kaTm", "vam", va[0:16, 0, 64 * g:64 * g + 64]))
                    for nm, bk, kap, nk, kkey, vkey, vap in segs:
                        mm(banks[bk][0:nk, :].rearrange("p (j t) -> p j t", j=4), kap, rhs_q, True, True,
                           [kkey, "qaT"], [f"bank{bk}"], True)
                        S.op("act", lambda e, bk=bk, nk=nk: e.activation(out=pT[0:nk, bk, :], in_=banks[bk][0:nk, :], func=AF.Exp, scale=0.125),
                             reads=[f"bank{bk}"], writes=[f"pT{bk}"])
                        if nm == "cur":
                            S.op("pool", lambda e: e.affine_select(out=pT[:, 0, :].rearrange("p (j t) -> p j t", j=4),
                                                                   in_=pT[:, 0, :].rearrange("p (j t) -> p j t", j=4),
                                                                   pattern=[[0, 4], [1, 128]], compare_op=ALU.is_ge, fill=0.0,
                                                                   base=0, channel_multiplier=-1), reads=["pT0"], writes=["pT0"])
                        elif nm == "prev":
                            S.op("pool", lambda e: e.affine_select(out=pT[:, 1, :].rearrange("p (j t) -> p j t", j=4),
                                                                   in_=pT[:, 1, :].rearrange("p (j t) -> p j t", j=4),
                                                                   pattern=[[0, 4], [-1, 128]], compare_op=ALU.is_gt, fill=0.0,
                                                                   base=0, channel_multiplier=1), reads=["pT1"], writes=["pT1"])
                    for par in range(2):
                        prow = slice(64 * par, 64 * par + 64)
                        for si, (nm, bk, kap, nk, kkey, vkey, vap) in enumerate(segs):
                            rhs_p = pT[0:nk, bk, :].rearrange("p (c two t) -> p c two t", c=2, two=2)[:, :, par, :]
                            first, last = si == 0, si == len(segs) - 1
                            mm(banks[3][prow, 0:256].rearrange("p (c t) -> p c t", c=2), vap, rhs_p, first, last,
                               [vkey, f"pT{bk}"], [f"bank3_{par}"], False, skip=True)
                            mm(banks[4][prow, 0:256].rearrange("p (c t) -> p c t", c=2), ones_bf[0:nk, 0:64], rhs_p, first, last,
                               ["ones_bf", f"pT{bk}"], [f"bank4_{par}"], last and par == 1, skip=True)
                    tden = scrB[:, 0:256].rearrange("p (c t) -> p c t", c=2)
                    S.op("dve", lambda e, g=g: e.tensor_tensor(out=tden, in0=banks[4][:, 0:256].rearrange("p (c t) -> p c t", c=2),
                                                               in1=esink[:, 2 * g:2 * g + 2].unsqueeze(2).to_broadcast([128, 2, 128]), op=ALU.add),
                         reads=BK(4) + ["esink"], writes=["scrB"])
                    S.op("dve", lambda e: e.reciprocal(out=scrB[:, 0:256], in_=scrB[:, 0:256]), reads=["scrB"], writes=["scrB"])
                    S.op("dve", lambda e, g=g, i=i: e.tensor_tensor(out=o_raw[:, 2 * g:2 * g + 2, i * 128:(i + 1) * 128],
                                                                   in0=banks[3][:, 0:256].rearrange("p (c t) -> p c t", c=2),
                                                                   in1=tden, op=ALU.mult),
                         reads=BK(3) + ["scrB"], writes=["o_raw"])

        def out_norm(gvec, gkey, c_off):
            sq = scrA[:, :].rearrange("p (c t) -> p c t", c=4)
            S.op("act", lambda e: e.activation(out=sq, in_=o_raw[:], func=AF.Square), reads=["o_raw"], writes=["scrA"])
            for c in range(4):
                mm(banks[5][:, :], ones_bf[:], sq[:, c, :], c == 0, c == 3, ["scrA", "ones_bf"], ["bank5"], c == 3)
            S.op("dve", lambda e: e.tensor_scalar(out=scrB[:], in0=banks[5][:], scalar1=1.0 / 512, scalar2=EPS, op0=ALU.mult, op1=ALU.add),
                 reads=["bank5"], writes=["scrB"])
            S.op("act", lambda e: e.activation(out=scrB[:], in_=scrB[:], func=AF.Ln), reads=["scrB"], writes=["scrB"])
            S.op("act", lambda e: e.activation(out=scrB[:], in_=scrB[:], func=AF.Exp, scale=-0.5), reads=["scrB"], writes=["scrB"])
            for c in range(4):
                S.op("dve", lambda e, c=c: e.scalar_tensor_tensor(out=mixT[:, c_off + c, :], in0=o_raw[:, c, :], scalar=gvec[:, c:c + 1],
                                                                 in1=scrB[:], op0=ALU.mult, op1=ALU.mult),
                     reads=["o_raw", "scrB", gkey], writes=[f"mix{c_off + c}"])

        b7f = banks[7][:].bitcast(F32)
        ZP = ((0, 1), (2, 5), (6, 7))

        def zap(bk):
            return b7f if bk == 7 else banks[bk][:, :]

        def attn_b(gi):
            plist = []
            for hp in range(4):
                blks = [("diag", 4 * gi + d, d) for d in (3, 2, 1, 0)] + [("full", kb, 0) for kb in range(4 * gi - 1, -1, -1)] + [("meta", -1, 0)]
                for bi, (kind, kb, d) in enumerate(blks):
                    plist.append(dict(hp=hp, kind=kind, kb=kb, d=d, first=(bi == 0), last=(bi == len(blks) - 1)))
            n = len(plist)

            def geom(b):
                q0 = 128 * b["d"] if b["kind"] == "diag" else 0
                nk = 16 if b["kind"] == "meta" else 128
                return q0, nk

            def s0(T):
                b = plist[T]
                q0, nk = geom(b)
                for par in range(2):
                    rows = slice(64 * par, 64 * par + 64)
                    zb = ZP[T % 3][par]
                    if b["kind"] == "meta":
                        kap, kkey = kbT[rows, b["hp"], 0:16], "kbTm"
                    else:
                        c0 = 16 + b["kb"] * 128
                        kap, kkey = kbT[rows, b["hp"], c0:c0 + 128], f"kbT{b['kb'] + 1}"
                    mm(zap(zb)[0:nk, q0:512], kap, qbT[rows, b["hp"], q0:512], True, True, [kkey, "qbT"], BK(zb), par == 1)

            def s1a(T):
                b = plist[T]
                q0, nk = geom(b)
                for par in range(2):
                    zb = ZP[T % 3][par]
                    S.op("act", lambda e, par=par, zb=zb: e.activation(out=e_sb[par][0:nk, q0:512], in_=zap(zb)[0:nk, q0:512], func=AF.Exp),
                         reads=BK(zb), writes=[f"e{par}"])

            def s1(T):
                b = plist[T]
                q0, nk = geom(b)
                q1 = q0 + 128 if b["kind"] == "diag" else 0
                has_acc = (q1 < 512) and not (b["kind"] == "diag" and b["d"] == 3)
                for par in range(2):
                    S.op("act", lambda e, par=par: e.activation(out=sp_sb[par][0:nk, q0:512], in_=e_sb[par][0:nk, q0:512], func=AF.Ln, bias=1.0),
                         reads=[f"e{par}"], writes=[f"sp{par}"])
                    if b["kind"] == "diag":
                        S.op("pool", lambda e, par=par: e.affine_select(out=sp_sb[par][:, q0:q0 + 128], in_=sp_sb[par][:, q0:q0 + 128], pattern=[[1, 128]],
                                                                        compare_op=ALU.is_gt, fill=0.0, base=0, channel_multiplier=-1),
                             reads=[f"sp{par}"], writes=[f"sp{par}"])
                for par in range(2):
                    zb = ZP[T % 3][par]
                    mm(zap(zb)[0:nk, q0:512], negtri[0:nk, 0:nk], sp_sb[par][0:nk, q0:512], False, True,
                       ["negtri", f"sp{par}"], BK(zb), not has_acc, skip=True)
                    if has_acc:
                        mm(zap(zb)[0:nk, q1:512], negones[:, 0:nk], acc_sb[par][:, q1:512], False, True,
                           ["negones", f"acc{par}"], BK(zb), True, skip=True)
                for par in range(2):
                    if b["kind"] == "diag":
                        S.op("dve", lambda e, par=par: e.tensor_copy(out=acc_sb[par][:, q0:q0 + 128], in_=sp_sb[par][:, q0:q0 + 128]),
                             reads=[f"sp{par}"], writes=[f"acc{par}"])
                        if has_acc:
                            S.op("dve", lambda e, par=par: e.tensor_tensor(out=acc_sb[par][:, q1:512], in0=acc_sb[par][:, q1:512],
                                                                           in1=sp_sb[par][:, q1:512], op=ALU.add),
                                 reads=[f"sp{par}", f"acc{par}"], writes=[f"acc{par}"])
                    elif b["kind"] == "full":
                        S.op("dve", lambda e, par=par: e.tensor_tensor(out=acc_sb[par][:, :], in0=acc_sb[par][:, :], in1=sp_sb[par][:, :], op=ALU.add),
                             reads=[f"sp{par}", f"acc{par}"], writes=[f"acc{par}"])

            def s2(T):
                b = plist[T]
                q0, nk = geom(b)
                ob = 3 + (b["hp"] % 2)
                for par in range(2):
                    zb = ZP[T % 3][par]
                    S.op("act", lambda e, par=par, zb=zb: e.activation(out=a_sb[par][0:nk, q0:512], in_=zap(zb)[0:nk, q0:512], func=AF.Exp),
                         reads=BK(zb), writes=[f"a{par}"])
                    if b["kind"] == "diag":
                        S.op("pool", lambda e, par=par: e.affine_select(out=a_sb[par][:, q0:q0 + 128], in_=a_sb[par][:, q0:q0 + 128], pattern=[[1, 128]],
                                                                        compare_op=ALU.is_gt, fill=0.0, base=0, channel_multiplier=-1),
                             reads=[f"a{par}"], writes=[f"a{par}"])
                for par in range(2):
                    h = 2 * b["hp"] + par
                    prow = slice(64 * par, 64 * par + 64)
                    if b["kind"] == "meta":
                        vap, vkey = vb[0:16, 0, h * 64:(h + 1) * 64], "vbm"
                    else:
                        vap, vkey = vb[:, b["kb"] + 1, h * 64:(h + 1) * 64], f"vb{b['kb'] + 1}"
                    mm(banks[ob][prow, q0:512], vap, a_sb[par][0:nk, q0:512], b["first"], b["last"], [vkey, f"a{par}"],
                       [f"bank{ob}_{par}"], par == 1, skip=True)
                if b["last"]:
                    S.op("dve", lambda e: e.tensor_copy(out=o_raw[:, b["hp"], :], in_=banks[ob][:, :]),
                         reads=BK(ob), writes=["o_raw"])

            for T in range(-2, n):
                if T + 2 < n:
                    s0(T + 2)
                if 0 <= T + 1 < n:
                    s1a(T + 1)
                    s1(T + 1)
                if T >= 0:
                    s2(T)

        def outproj(i):
            for kc in range(8):
                for hf in range(2):
                    mm(banks[5 + hf][:, :], mixT[:, kc, i * 128:(i + 1) * 128], w_out_sb[:, kc, hf * 512:(hf + 1) * 512], kc == 0, kc == 7,
                       [f"mix{kc}", "w_out"], [f"bank{5 + hf}"], kc == 7 and hf == 1)
            for hf in range(2):
                S.op("dve", lambda e, hf=hf: e.tensor_tensor(out=xg[:, i, hf * 512:(hf + 1) * 512], in0=xg[:, i, hf * 512:(hf + 1) * 512],
                                                             in1=banks[5 + hf][:, :], op=ALU.add),
                     reads=[f"bank{5 + hf}", f"xg{i}"], writes=[f"xg{i}"])

        def ffn(state):
            uks = [f"uT{i}" for i in range(4)]
            n0 = state["n"]
            dcount = [0]

            def gu(sc):
                slot, hb = (n0 + sc) % 2, sc % 2
                for fc in range(2):
                    gb, ub = 2 * fc, 2 * fc + 1
                    for kc in range(8):
                        mm(banks[gb][:, :], wg_sl[slot][:, kc, fc * 128:(fc + 1) * 128], uT[:, kc, :], kc == 0, kc == 7,
                           uks + [f"wg_sl{slot}"], BK(gb), False)
                    for kc in range(8):
                        mm(banks[ub][:, :], wu_sl[slot][:, kc, fc * 128:(fc + 1) * 128], uT[:, kc, :], kc == 0, kc == 7,
                           uks + [f"wu_sl{slot}"], BK(ub), kc == 7)
                    S.op("act", lambda e, gb=gb: e.activation(out=scrB[:], in_=banks[gb][:], func=AF.Silu), reads=BK(gb), writes=["scrB"])
                    S.op("dve", lambda e, ub=ub, fc=fc, hb=hb: e.tensor_tensor(out=hT[hb][:, fc, :], in0=scrB[:], in1=banks[ub][:], op=ALU.mult),
                         reads=["scrB"] + BK(ub), writes=[f"hT{hb}"])

            def down(sc):
                slot, hb = (n0 + sc) % 2, sc % 2
                for i in range(4):
                    for hf in range(2):
                        db = 4 + (dcount[0] % 3)
                        dcount[0] += 1
                        for fc in range(2):
                            mm(banks[db][:, :], hT[hb][:, fc, i * 128:(i + 1) * 128], wd_sl[slot][:, fc, hf * 512:(hf + 1) * 512], fc == 0, fc == 1,
                               [f"hT{hb}", f"wd_sl{slot}"], BK(db), fc == 1)
                        S.op("dve", lambda e, i=i, hf=hf, db=db: e.tensor_tensor(out=xg[:, i, hf * 512:(hf + 1) * 512],
                                                                                 in0=xg[:, i, hf * 512:(hf + 1) * 512], in1=banks[db][:, :], op=ALU.add),
                             reads=BK(db) + [f"xg{i}"], writes=[f"xg{i}"])
                nxt = state["loads"]
                if nxt < state["total"]:
                    ffn_load(nxt % NSC, slot)
                    state["loads"] += 1

            gu(0)
            for sc in range(NSC):
                if sc + 1 < NSC:
                    gu(sc + 1)
                down(sc)
            state["n"] += NSC

        def final_norm(i):
            xk = f"xg{i}"
            p = i % 2
            xs = xs2[p]
            xsk, sk = f"xs{p}", f"statf{p}"
            c0 = 8 + 4 * p
            S.op("act", lambda e: e.activation(out=xs[:], in_=xg[:, i, :], func=AF.Square, accum_out=stat[:, c0:c0 + 1]),
                 reads=[xk], writes=[xsk, sk])
            S.op("dve", lambda e: e.tensor_scalar(out=stat[:, c0 + 1:c0 + 2], in0=stat[:, c0:c0 + 1], scalar1=1.0 / D, scalar2=EPS,
                                                  op0=ALU.mult, op1=ALU.add), reads=[sk], writes=[sk])
            S.op("act", lambda e: e.activation(out=stat[:, c0 + 2:c0 + 3], in_=stat[:, c0 + 1:c0 + 2], func=AF.Ln), reads=[sk], writes=[sk])
            S.op("act", lambda e: e.activation(out=stat[:, c0 + 3:c0 + 4], in_=stat[:, c0 + 2:c0 + 3], func=AF.Exp, scale=-0.5),
                 reads=[sk], writes=[sk])
            S.op("dve", lambda e: e.scalar_tensor_tensor(out=xg[:, i, :], in0=xg[:, i, :], scalar=stat[:, c0 + 3:c0 + 4], in1=gfin[:],
                                                         op0=ALU.mult, op1=ALU.mult),
                 reads=[xk, sk, "gfin"], writes=[xk])

        total_groups = NSEQ * NG
        fstate = dict(n=0, loads=0, total=total_groups * NSC)
        for _ in range(2):
            ffn_load(fstate["loads"] % NSC, fstate["loads"] % 2)
            fstate["loads"] += 1
        groups = [(sq, gi) for sq in range(NSEQ) for gi in range(NG)]

        def load_x(sq, gi, i):
            xv = x_d[sq, gi * 512 + i * 128:gi * 512 + (i + 1) * 128, :]
            S.dma("sp", lambda e: e.dma_start(out=xg[:, i, :], in_=xv), f"ld_x{i}", writes=[f"xg{i}"])

        def store_x(sq, gi, i):
            ov = out_d[sq, gi * 512 + i * 128:gi * 512 + (i + 1) * 128, :]
            S.dma("sp", lambda e: e.dma_start(out=ov, in_=xg[:, i, :]), f"st_out{i}", reads=[f"xg{i}"], final=True)

        for i in range(4):
            load_x(0, 0, i)
        for gidx, (sq, gi) in enumerate(groups):
            rms_transpose(0, gmix, "gmix")
            for i in range(4):
                if i + 1 < 4:
                    rms_transpose(i + 1, gmix, "gmix")
                tok_proj(i, gi * 4 + i + 1)
            feat_proj(512, gi * 4)
            attn_a(gi)
            out_norm(gouta, "gouta", 0)
            attn_b(gi)
            out_norm(goutb, "goutb", 4)
            for i in range(4):
                outproj(i)
                if i >= 1:
                    rms_transpose(i - 1, gffn, "gffn")
            rms_transpose(3, gffn, "gffn")
            ffn(fstate)
            for i in range(4):
                final_norm(i)
                store_x(sq, gi, i)
                if gidx + 1 < len(groups):
                    load_x(groups[gidx + 1][0], groups[gidx + 1][1], i)
        S.emit(nc)
    return nc


def _tables(SEQ):
    NT = SEQ // 128
    half = 32
    inv_freq = (10000.0 ** (-np.arange(half, dtype=np.float32) / half)).astype(np.float32)
    cos = np.zeros((128, NT + 1, 32), np.float32)
    sin = np.zeros((128, NT + 1, 32), np.float32)
    r = np.arange(128)
    for t in range(NT + 1):
        pos = (r if t == 0 else 16 + 128 * (t - 1) + r).astype(np.float32)
        ang = (pos[:, None] * inv_freq[None, :]).astype(np.float32)
        cos[:, t, :] = np.cos(ang)
        sin[:, t, :] = np.sin(ang)
    return cos, sin


def _in_maps(x, meta_tokens, norm_mix, w_in, sinks, norm_out_a, norm_out_b, w_out, norm_ffn, w_gate, w_up, w_down,
             norm_final, n_cores, nseq):
    SEQ = x.shape[1]
    cos, sin = _tables(SEQ)
    c = np.ascontiguousarray
    f = lambda a: np.asarray(a, dtype=np.float32)
    sk = f(sinks).reshape(8)
    sinkl = np.zeros((128, 4), np.float32)
    for ch in range(4):
        sinkl[0:64, ch] = sk[2 * ch]
        sinkl[64:128, ch] = sk[2 * ch + 1]
    common = {
        "meta": c(f(meta_tokens)),
        "w_in": c(f(w_in)[0]), "w_out": c(f(w_out)[0]), "w_gate": c(f(w_gate)[0]), "w_up": c(f(w_up)[0]), "w_down": c(f(w_down)[0]),
        "gmix": c(f(norm_mix).reshape(8, 128).T), "gffn": c(f(norm_ffn).reshape(8, 128).T),
        "gouta": c(f(norm_out_a).reshape(4, 128).T), "goutb": c(f(norm_out_b).reshape(4, 128).T),
        "sinkl": sinkl, "gfin": c(np.broadcast_to(f(norm_final).reshape(1, D), (128, D))),
        "cost": cos, "sint": sin,
    }
    xs = f(x)
    maps = []
    for i in range(n_cores):
        m = dict(common)
        m["x"] = c(xs[i * nseq:(i + 1) * nseq])
        maps.append(m)
    return maps


def kernel(x, meta_tokens, norm_mix, w_in, sinks, norm_out_a, norm_out_b, w_out, norm_ffn, w_gate, w_up, w_down, norm_final):
    n_cores = 8
    B, SEQ, _ = x.shape
    nseq = B // n_cores
    nc = build(nseq, SEQ)
    maps = _in_maps(x, meta_tokens, norm_mix, w_in, sinks, norm_out_a, norm_out_b, w_out, norm_ffn, w_gate, w_up, w_down,
                    norm_final, n_cores, nseq)
    res = run_bass_kernel_spmd(nc, maps, core_ids=list(range(n_cores)))
    return np.concatenate([np.asarray(r["out"], dtype=np.float32) for r in res.results], axis=0)
```

```python
import contextlib
import numpy as np
import concourse.bass as bass
import concourse.mybir as mybir
from concourse.bass_utils import run_bass_kernel_spmd

F32 = mybir.dt.float32
BF16 = mybir.dt.bfloat16
AF = mybir.ActivationFunctionType
ALU = mybir.AluOpType

ENGS = ("pe", "act", "dve", "pool", "sp")

D = 1024
NCOL = 2304
DFF = 2816
NSC = DFF // 256
EPS = 1e-6
STOP = 99
import os as _os
DBG_TI0 = int(_os.environ.get('DBG_TI0', '0'))


class Sched:
    def __init__(self):
        self.ops = {e: [] for e in ENGS}
        self.cnt = {e: 0 for e in ENGS}
        self.dma_cnt = {}
        self.last_w = {}
        self.readers = {}
        self.seen = {e: {} for e in ENGS}
        self.final_tokens = []
        self.pe_pending = False

    def _deps(self, eng, reads, writes):
        deps = {}

        def add(tok):
            if tok is None:
                return
            k, v = tok
            if eng == "pe" and k == "pe":
                return
            if deps.get(k, 0) < v:
                deps[k] = v

        for r in reads:
            add(self.last_w.get(r))
        for w in writes:
            add(self.last_w.get(w))
            for t in self.readers.get(w, ()):
                add(t)
        out = []
        for k, v in deps.items():
            if self.seen[eng].get(k, 0) >= v:
                continue
            self.seen[eng][k] = v
            out.append((k, v))
        return out

    def _commit(self, tok, reads, writes):
        for r in reads:
            self.readers.setdefault(r, []).append(tok)
        for w in writes:
            self.last_w[w] = tok
            self.readers[w] = []

    def op(self, eng, fn, reads=(), writes=(), sig=True, pre=False):
        waits = self._deps(eng, reads, writes)
        if eng == "pe" and not sig:
            tok = ("pe", self.cnt["pe"] + 1)
            incs = []
            self.pe_pending = True
        else:
            self.cnt[eng] += 1
            tok = (eng, self.cnt[eng])
            incs = [(eng, 1)]
            if eng == "pe":
                self.pe_pending = False
        self.ops[eng].append((fn, waits, incs, False))
        self._commit(tok, reads, writes)
        return tok

    def dma(self, q, fn, semkey, reads=(), writes=(), final=False):
        waits = self._deps(q, reads, writes)
        self.dma_cnt[semkey] = self.dma_cnt.get(semkey, 0) + 16
        tok = (semkey, self.dma_cnt[semkey])
        self.ops[q].append((fn, waits, [(semkey, 16)], True))
        self._commit(tok, reads, writes)
        if final:
            self.final_tokens.append(tok)
        return tok

    def emit(self, nc):
        assert not self.pe_pending
        semkeys = list(ENGS[:4]) + list(self.dma_cnt.keys())
        with contextlib.ExitStack() as st:
            sems = {}
            for k in semkeys:
                sems[k] = st.enter_context(nc.semaphore("s_" + str(k)))
            block = st.enter_context(nc.Block())
            finals = {}
            for k, v in self.final_tokens:
                finals[k] = max(finals.get(k, 0), v)

            def run(engname):
                def body(e):
                    for fn, waits, incs, pre_wait in self.ops[engname]:
                        if pre_wait:
                            for k, v in waits:
                                e.wait_ge(sems[k], v)
                            ins = fn(e)
                        else:
                            for k, v in waits[:-1]:
                                e.wait_ge(sems[k], v)
                            ins = fn(e)
                            if waits:
                                k, v = waits[-1]
                                ins._wait_ge(sems[k], v)
                        for k, a in incs:
                            ins.then_inc(sems[k], a)
                    if engname == "sp":
                        for k, v in finals.items():
                            e.wait_ge(sems[k], v)
                return body

            block.tensor(run("pe"))
            block.scalar(run("act"))
            block.vector(run("dve"))
            block.gpsimd(run("pool"))
            block.sync(run("sp"))


def build(NSEQ, SEQ):
    NG = SEQ // 512
    NT = SEQ // 128
    CL = 16 + SEQ
    nc = bass.Bass("TRN2", target_bir_lowering=False)

    def din(name, shape, dt=F32):
        return nc.dram_tensor(name, shape, dt, kind="ExternalInput").ap()

    x_d = din("x", [NSEQ, SEQ, D])
    meta_d = din("meta", [16, D])
    w_in_d = din("w_in", [D, NCOL])
    w_out_d = din("w_out", [D, D])
    w_gate_d = din("w_gate", [D, DFF])
    w_up_d = din("w_up", [D, DFF])
    w_down_d = din("w_down", [DFF, D])
    gmix_d = din("gmix", [128, 8])
    gffn_d = din("gffn", [128, 8])
    gouta_d = din("gouta", [128, 4])
    goutb_d = din("goutb", [128, 4])
    sink_d = din("sinkl", [128, 4])
    gfin_d = din("gfin", [128, D])
    cos_d = din("cost", [128, NT + 1, 32])
    sin_d = din("sint", [128, NT + 1, 32])
    out_d = nc.dram_tensor("out", [NSEQ, SEQ, D], F32, kind="ExternalOutput").ap()
    wg_s = nc.dram_tensor("wg_s", [D, DFF], BF16, kind="Internal").ap()
    wu_s = nc.dram_tensor("wu_s", [D, DFF], BF16, kind="Internal").ap()
    wd_s = nc.dram_tensor("wd_s", [DFF, D], BF16, kind="Internal").ap()

    st = contextlib.ExitStack()

    def sb(name, shape, dt):
        return st.enter_context(nc.sbuf_tensor(name, shape, dt))

    def ps(name, shape, dt):
        return st.enter_context(nc.psum_tensor(name, shape, dt))

    with st:
        w_in_sb = sb("w_in_sb", [128, 8, NCOL], BF16)
        w_out_sb = sb("w_out_sb", [128, 8, D], BF16)
        kaT = sb("kaT", [128, CL], BF16)
        va = sb("va", [128, NT + 1, 128], BF16)
        kbT = sb("kbT", [128, 4, CL], BF16)
        vb = sb("vb", [128, NT + 1, 512], BF16)
        xg = sb("xg", [128, 4, D], F32)
        uT = sb("uT", [128, 8, 512], BF16)
        xs2 = [sb(f"xs{i}", [128, D], BF16) for i in range(2)]
        qaT = sb("qaT", [128, 4, 512], BF16)
        qbT = sb("qbT", [128, 4, 512], BF16)
        rq = sb("rq", [128, 640], BF16)
        kq = sb("kq", [128, 640], F32)
        scrA = sb("scrA", [128, 2048], BF16)
        scrB = sb("scrB", [128, 512], F32)
        e_sb = [sb(f"e_sb{i}", [128, 512], F32) for i in range(2)]
        sp_sb = [sb(f"sp_sb{i}", [128, 512], BF16) for i in range(2)]
        a_sb = [sb(f"a_sb{i}", [128, 512], BF16) for i in range(2)]
        acc_sb = [sb(f"acc_sb{i}", [128, 512], BF16) for i in range(2)]
        pT = sb("pT", [128, 3, 512], BF16)
        o_raw = sb("o_raw", [128, 4, 512], F32)
        mixT = sb("mixT", [128, 8, 512], BF16)
        hT = [sb(f"hT{i}", [128, 2, 512], BF16) for i in range(2)]
        wg_sl = [sb(f"wg_sl{i}", [128, 8, 256], BF16) for i in range(2)]
        wu_sl = [sb(f"wu_sl{i}", [128, 8, 256], BF16) for i in range(2)]
        wd_sl = [sb(f"wd_sl{i}", [128, 2, D], BF16) for i in range(2)]
        cos_sb = sb("cos_sb", [128, NT + 1, 32], F32)
        sin_sb = sb("sin_sb", [128, NT + 1, 32], F32)
        nsin_sb = sb("nsin_sb", [128, NT + 1, 32], F32)
        gmix = sb("gmix_sb", [128, 8], F32)
        gffn = sb("gffn_sb", [128, 8], F32)
        gouta = sb("gouta_sb", [128, 4], F32)
        goutb = sb("goutb_sb", [128, 4], F32)
        esink = sb("esink_sb", [128, 4], F32)
        gfin = sb("gfin_sb", [128, D], F32)
        ident = sb("ident", [128, 128], BF16)
        negtri = sb("negtri", [128, 128], BF16)
        negones = sb("negones", [128, 128], BF16)
        ones_bf = sb("ones_bf", [128, 128], BF16)
        stat = sb("stat", [128, 16], F32)
        banks = [ps(f"bank{i}", [128, 512], F32) for i in range(7)] + [ps("bank7", [128, 1024], BF16)]

        S = Sched()
        scrA_f = scrA[:].bitcast(F32)

        S.op("pool", lambda e: e.memset(scrB[:, 0:128], 1.0), writes=["scrB"])
        S.op("pool", lambda e: e.affine_select(out=ident[:], in_=scrB[:, 0:128], pattern=[[-1, 128]], compare_op=ALU.is_equal,
                                               fill=0.0, base=0, channel_multiplier=1), reads=["scrB"], writes=["ident"])
        S.op("pool", lambda e: e.memset(ones_bf[:], 1.0), writes=["ones_bf"])
        S.op("pool", lambda e: e.memset(scrB[:, 0:128], -1.0), reads=[], writes=["scrB"])
        S.op("pool", lambda e: e.memset(negones[:], -1.0), writes=["negones"])
        S.op("pool", lambda e: e.affine_select(out=negtri[:], in_=scrB[:, 0:128], pattern=[[-1, 128]], compare_op=ALU.is_ge,
                                               fill=0.0, base=0, channel_multiplier=1), reads=["scrB"], writes=["negtri"])
        S.op("pool", lambda e: e.memset(xg[:, 0, :], 0.0), writes=["xg0"])
        cast_n = [0]

        def ld(q, dst, src, key, wkeys):
            if q == "pool":
                ck = f"castchain{cast_n[0] % 2}"
                cast_n[0] += 1
                S.dma(q, lambda e: e.dma_start(out=dst, in_=src), key, reads=[ck], writes=wkeys + [ck])
            else:
                S.dma(q, lambda e: e.dma_start(out=dst, in_=src), key, writes=wkeys)

        w_in_v = w_in_d.rearrange("(kc p) n -> p kc n", p=128)
        for h in range(2):
            ld("pool", w_in_sb[:, :, h * 1152:(h + 1) * 1152], w_in_v[:, :, h * 1152:(h + 1) * 1152], f"ld_win{h}", [f"w_in{h}"])
        W_IN = ["w_in0", "w_in1"]
        ld("sp", gmix[:], gmix_d, "ld_c0", ["gmix"])
        ld("sp", gffn[:], gffn_d, "ld_c1", ["gffn"])
        ld("sp", gouta[:], gouta_d, "ld_c2", ["gouta"])
        ld("sp", goutb[:], goutb_d, "ld_c3", ["goutb"])
        ld("sp", esink[:], sink_d, "ld_c4", ["esink"])
        ld("sp", gfin[:], gfin_d, "ld_c5", ["gfin"])
        ld("sp", cos_sb[:], cos_d, "ld_c6", ["cos"])
        ld("sp", sin_sb[:], sin_d, "ld_c7", ["sin"])
        ld("pool", w_out_sb[:], w_out_d.rearrange("(kc p) n -> p kc n", p=128), "ld_wout", ["w_out"])
        def rows2(ap):
            return ap.rearrange("r (t c) -> (r t) c", t=2)
        for h in range(2):
            ld("pool", rows2(wg_s)[h * 1024:(h + 1) * 1024, :], rows2(w_gate_d)[h * 1024:(h + 1) * 1024, :], f"ld_wg{h}", [f"wg_s{h}"])
            ld("pool", rows2(wu_s)[h * 1024:(h + 1) * 1024, :], rows2(w_up_d)[h * 1024:(h + 1) * 1024, :], f"ld_wu{h}", [f"wu_s{h}"])
        for h in range(2):
            ld("pool", wd_s[h * 1408:(h + 1) * 1408, :], w_down_d[h * 1408:(h + 1) * 1408, :], f"ld_wd{h}", [f"wd_s{h}"])

        S.op("act", lambda e: e.activation(out=esink[:], in_=esink[:], func=AF.Exp), reads=["esink"], writes=["esink"])
        S.op("dve", lambda e: e.tensor_scalar(out=nsin_sb[:], in0=sin_sb[:], scalar1=-1.0, scalar2=None, op0=ALU.mult),
             reads=["sin"], writes=["nsin"])
        S.dma("sp", lambda e: e.dma_start(out=xg[0:16, 0, :], in_=meta_d), "ld_x0", reads=[], writes=["xg0"])

        bank_bf = [b[:].bitcast(BF16) for b in banks[:7]] + [banks[7][:]]

        def BK(b):
            return [f"bank{b}_0", f"bank{b}_1"] if b in (3, 4) else [f"bank{b}"]

        def rms_transpose(i, gvec, gkey, tb=None):
            xk = f"xg{i}"
            p = i % 2
            xs = xs2[p]
            xsk, sk = f"xs{p}", f"stat{p}"
            c0 = 4 * p
            tb = 5 + p
            S.op("act", lambda e: e.activation(out=xs[:], in_=xg[:, i, :], func=AF.Square, accum_out=stat[:, c0:c0 + 1]),
                 reads=[xk], writes=[xsk, sk])
            S.op("dve", lambda e: e.tensor_scalar(out=stat[:, c0 + 1:c0 + 2], in0=stat[:, c0:c0 + 1], scalar1=1.0 / D, scalar2=EPS,
                                                  op0=ALU.mult, op1=ALU.add), reads=[sk], writes=[sk])
            S.op("act", lambda e: e.activation(out=stat[:, c0 + 2:c0 + 3], in_=stat[:, c0 + 1:c0 + 2], func=AF.Ln), reads=[sk], writes=[sk])
            S.op("act", lambda e: e.activation(out=stat[:, c0 + 3:c0 + 4], in_=stat[:, c0 + 2:c0 + 3], func=AF.Exp, scale=-0.5),
                 reads=[sk], writes=[sk])
            S.op("dve", lambda e: e.tensor_scalar(out=xs[:], in0=xg[:, i, :], scalar1=stat[:, c0 + 3:c0 + 4], scalar2=None, op0=ALU.mult),
                 reads=[xk, sk], writes=[xsk])
            for kc in range(8):
                S.op("pe", lambda e, kc=kc: e.transpose(out=bank_bf[tb][:, kc * 128:(kc + 1) * 128],
                                                        in_=xs[:, kc * 128:(kc + 1) * 128], identity=ident[:]),
                     reads=[xsk, "ident"], writes=BK(tb), sig=(kc == 7))
            S.op("dve", lambda e: e.tensor_tensor(out=uT[:, :, i * 128:(i + 1) * 128],
                                                  in0=bank_bf[tb][:, :].rearrange("p (k t) -> p k t", k=8),
                                                  in1=gvec[:, :].unsqueeze(2).to_broadcast([128, 8, 128]), op=ALU.mult),
                 reads=BK(tb) + [gkey], writes=[f"uT{i}"])

        def rope(src_ps, nh, dsts, ti, skeys, dkey):
            n = nh * 64
            if DBG_TI0:
                ti = 0
            tcv = scrA_f[:, 0:n].rearrange("p (h t d) -> p h t d", h=nh, t=2)
            tsv = scrA_f[:, 512:512 + n].rearrange("p (h t d) -> p h t d", h=nh, t=2)
            srcv = src_ps.rearrange("p (h t d) -> p h t d", h=nh, t=2)
            cosb = cos_sb[:, ti, :].unsqueeze(1).to_broadcast([128, nh, 32])
            sinb = sin_sb[:, ti, :].unsqueeze(1).to_broadcast([128, nh, 32])
            nsinb = nsin_sb[:, ti, :].unsqueeze(1).to_broadcast([128, nh, 32])
            for t in range(2):
                S.op("dve", lambda e, t=t: e.tensor_tensor(out=tcv[:, :, t, :], in0=srcv[:, :, t, :], in1=cosb, op=ALU.mult),
                     reads=skeys + ["cos"], writes=["scrA"])
            S.op("dve", lambda e: e.tensor_tensor(out=tsv[:, :, 0, :], in0=srcv[:, :, 1, :], in1=nsinb, op=ALU.mult),
                 reads=skeys + ["nsin"], writes=["scrA"])
            S.op("dve", lambda e: e.tensor_tensor(out=tsv[:, :, 1, :], in0=srcv[:, :, 0, :], in1=sinb, op=ALU.mult),
                 reads=skeys + ["sin"], writes=["scrA"])
            for h0, nhh, dst in dsts:
                S.op("dve", lambda e, h0=h0, nhh=nhh, dst=dst: e.tensor_tensor(
                    out=dst, in0=scrA_f[:, h0 * 64:(h0 + nhh) * 64].rearrange("p (h d) -> p h d", h=nhh),
                    in1=scrA_f[:, 512 + h0 * 64:512 + (h0 + nhh) * 64].rearrange("p (h d) -> p h d", h=nhh), op=ALU.add),
                     reads=["scrA"], writes=[dkey])

        def mm(out, lhsT, rhs, start, stop, reads, writes, sig, skip=False):
            S.op("pe", lambda e: e.matmul(out, lhsT=lhsT, rhs=rhs, start=start, stop=stop, skip_group_check=skip),
                 reads=reads, writes=writes, sig=sig)

        rq_q = rq[:, 0:512].rearrange("p (j hh d) -> p hh j d", j=4, hh=2)

        def tok_proj(i, ct, meta=False):
            uk = f"uT{i}"
            lts = [uT[:, kc, i * 128:(i + 1) * 128] for kc in range(8)]
            for kc in range(8):
                if not meta:
                    mm(banks[2][:, :], lts[kc], w_in_sb[:, kc, 0:512], kc == 0, kc == 7, [uk, "w_in0"], BK(2), False)
                mm(banks[3][:, 0:256], lts[kc], w_in_sb[:, kc, 512:768], kc == 0, kc == 7, [uk, "w_in0"], BK(3), False)
                mm(banks[4][:, :], lts[kc], w_in_sb[:, kc, 1792:2304], kc == 0, kc == 7, [uk, "w_in1"], BK(4), kc == 7)
            rows = slice(0, 16) if meta else slice(0, 128)
            vk = "vbm" if meta else f"vb{ct}"
            S.op("act", lambda e: e.copy(out=vb[rows, ct, :], in_=banks[4][rows, :]), reads=BK(4), writes=[vk])
            vak = "vam" if meta else f"va{ct}"
            S.op("act", lambda e: e.copy(out=va[rows, ct, :], in_=banks[3][rows, 128:256]), reads=BK(3), writes=[vak])
            if STOP in (13, 23):
                return
            S.op("act", lambda e: e.copy(out=kq[:, 512:640], in_=banks[3][:, 0:128]), reads=BK(3), writes=["kq_k"])
            rope(kq[:, 512:640], 2, [(0, 2, rq[:, 512:640].rearrange("p (h d) -> p h d", h=2))], ct, ["kq_k"], "rqk")
            if not meta and STOP != 25:
                S.op("act", lambda e: e.copy(out=kq[:, 0:512], in_=banks[2][:, :]), reads=BK(2), writes=["kq_q"])
                rope(kq[:, 0:512], 8, [(0, 4, rq_q[:, 0, :, :]), (4, 4, rq_q[:, 1, :, :])], ct, ["kq_q"], "rqq")
            if STOP in (14, 24, 25):
                return
            tb = 7
            if not meta:
                for j in range(4):
                    S.op("pe", lambda e, j=j: e.transpose(out=bank_bf[tb][:, j * 128:(j + 1) * 128],
                                                          in_=rq[:, j * 128:(j + 1) * 128], identity=ident[:]),
                         reads=["rqq", "ident"], writes=[f"bank{tb}"], sig=False)
            S.op("pe", lambda e: e.transpose(out=bank_bf[tb][:, 512:640], in_=rq[:, 512:640], identity=ident[:]),
                 reads=["rqk", "ident"], writes=[f"bank{tb}"], sig=True)
            if meta:
                S.op("dve", lambda e: e.tensor_copy(out=kaT[:, 0:16], in_=bank_bf[tb][:, 512:528]),
                     reads=[f"bank{tb}"], writes=["kaTm"])
            else:
                c0 = 16 + (ct - 1) * 128
                S.op("dve", lambda e: e.tensor_copy(out=kaT[:, c0:c0 + 128], in_=bank_bf[tb][:, 512:640]),
                     reads=[f"bank{tb}"], writes=[f"kaT{ct}"])
                S.op("dve", lambda e: e.tensor_copy(out=qaT[:, :, i * 128:(i + 1) * 128],
                                                    in_=bank_bf[tb][:, 0:512].rearrange("p (j t) -> p j t", j=4)),
                     reads=[f"bank{tb}"], writes=["qaT"])

        def feat_proj(ntok, gt0, meta=False):
            uks = [f"uT{i}" for i in range((ntok + 127) // 128)]
            nb = 0
            for c in range(8):
                if meta and c < 4:
                    continue
                col0 = 768 + c * 128
                bk = nb % 2
                nb += 1
                for kc in range(8):
                    mm(banks[bk][:, 0:ntok], w_in_sb[:, kc, col0:col0 + 128], uT[:, kc, 0:ntok], kc == 0, kc == 7,
                       uks + ["w_in0", "w_in1"], [f"bank{bk}"], kc == 7)
                if c < 4:
                    S.op("act", lambda e, c=c, bk=bk: e.activation(out=qbT[:, c, :], in_=banks[bk][:, :], func=AF.Copy, scale=0.125),
                         reads=[f"bank{bk}"], writes=["qbT"])
                elif meta:
                    S.op("dve", lambda e, c=c, bk=bk: e.tensor_copy(out=kbT[:, c - 4, 0:16], in_=banks[bk][:, 0:16]),
                         reads=[f"bank{bk}"], writes=["kbTm"])
                else:
                    c0 = 16 + gt0 * 128
                    S.op("dve", lambda e, c=c, bk=bk, c0=c0: e.tensor_copy(out=kbT[:, c - 4, c0:c0 + 512], in_=banks[bk][:, :]),
                         reads=[f"bank{bk}"], writes=[f"kbT{gt0 + 1 + t}" for t in range(4)])

        rms_transpose(0, gmix, "gmix")
        tok_proj(0, 0, meta=True)
        feat_proj(128, 0, meta=True)

        ffn_loads = []

        def ffn_load(sc, slot):
            S.dma("sp", lambda e: e.dma_start(out=wg_sl[slot][:], in_=wg_s.rearrange("(kc p) f -> p kc f", p=128)[:, :, sc * 256:(sc + 1) * 256]),
                  f"ld_wgs{slot}", reads=["wg_s0", "wg_s1"], writes=[f"wg_sl{slot}"])
            S.dma("sp", lambda e: e.dma_start(out=wu_sl[slot][:], in_=wu_s.rearrange("(kc p) f -> p kc f", p=128)[:, :, sc * 256:(sc + 1) * 256]),
                  f"ld_wus{slot}", reads=["wu_s0", "wu_s1"], writes=[f"wu_sl{slot}"])
            S.dma("sp", lambda e: e.dma_start(out=wd_sl[slot][:], in_=wd_s.rearrange("(c p) n -> p c n", p=128)[:, 2 * sc:2 * sc + 2, :]),
                  f"ld_wds{slot}", reads=["wd_s0", "wd_s1"], writes=[f"wd_sl{slot}"])

        def attn_a(gi):
            units = [(i, g) for i in range(4) for g in range(2)]

            def res(u):
                if u % 2 == 0:
                    return {0: 0, 1: 1, 2: 2}, banks[3][:, :], BK(3), [pT[:, 0, :], pT[:, 1, :], pT[:, 2, :]], ["pT0", "pT1", "pT2"]
                return {0: 4, 1: 5, 2: 6}, b7f, ["bank7"], [a_sb[0][:, :], a_sb[1][:, :], sp_sb[0][:, :]], ["a0", "a1", "sp0"]

            def segs_of(u):
                i, g = units[u]
                gt = gi * 4 + i
                ct = gt + 1
                rows = slice(64 * g, 64 * g + 64)
                c0 = 16 + gt * 128
                segs = [("cur", 0, kaT[rows, c0:c0 + 128], 128, f"kaT{ct}", f"va{ct}", va[:, ct, 64 * g:64 * g + 64])]
                if gt >= 1:
                    segs.append(("prev", 1, kaT[rows, c0 - 128:c0], 128, f"kaT{ct - 1}", f"va{ct - 1}", va[:, ct - 1, 64 * g:64 * g + 64]))
                segs.append(("meta", 2, kaT[rows, 0:16], 16, "kaTm", "vam", va[0:16, 0, 64 * g:64 * g + 64]))
                return segs

            def front(u):
                i, g = units[u]
                rows = slice(64 * g, 64 * g + 64)
                sbm, od, odk, sbuf, sk = res(u)
                rhs_q = qaT[rows, :, i * 128:(i + 1) * 128]
                for nm, sl, kap, nk, kkey, vkey, vap in segs_of(u):
                    bk = sbm[sl]
                    mm(banks[bk][0:nk, :].rearrange("p (j t) -> p j t", j=4), kap, rhs_q, True, True, [kkey, "qaT"], BK(bk), True)
                    S.op("act", lambda e, bk=bk, nk=nk, sl=sl: e.activation(out=sbuf[sl][0:nk, :], in_=banks[bk][0:nk, :], func=AF.Exp, scale=0.125),
                         reads=BK(bk), writes=[sk[sl]])
                    if nm == "cur":
                        S.op("pool", lambda e: e.affine_select(out=sbuf[0].rearrange("p (j t) -> p j t", j=4),
                                                               in_=sbuf[0].rearrange("p (j t) -> p j t", j=4),
                                                               pattern=[[0, 4], [1, 128]], compare_op=ALU.is_ge, fill=0.0,
                                                               base=0, channel_multiplier=-1), reads=[sk[0]], writes=[sk[0]])
                    elif nm == "prev":
                        S.op("pool", lambda e: e.affine_select(out=sbuf[1].rearrange("p (j t) -> p j t", j=4),
                                                               in_=sbuf[1].rearrange("p (j t) -> p j t", j=4),
                                                               pattern=[[0, 4], [-1, 128]], compare_op=ALU.is_gt, fill=0.0,
                                                               base=0, channel_multiplier=1), reads=[sk[1]], writes=[sk[1]])

            def back(u):
                i, g = units[u]
                sbm, od, odk, sbuf, sk = res(u)
                segs = segs_of(u)
                for which in range(2):
                    for par in range(2):
                        prow = slice(64 * par, 64 * par + 64)
                        for si, (nm, sl, kap, nk, kkey, vkey, vap) in enumerate(segs):
                            rhs_p = sbuf[sl][0:nk, :].rearrange("p (c two t) -> p c two t", c=2, two=2)[:, :, par, :]
                            first, last = si == 0, si == len(segs) - 1
                            if which == 0:
                                mm(od[prow, 0:256].rearrange("p (c t) -> p c t", c=2), vap, rhs_p, first, last,
                                   [vkey, sk[sl]], odk, False, skip=True)
                            else:
                                mm(od[prow, 256:512].rearrange("p (c t) -> p c t", c=2), ones_bf[0:nk, 0:64], rhs_p, first, last,
                                   ["ones_bf", sk[sl]], odk, last and par == 1, skip=True)
                tden = scrB[:, 0:256].rearrange("p (c t) -> p c t", c=2)
                S.op("dve", lambda e: e.tensor_tensor(out=tden, in0=od[:, 256:512].rearrange("p (c t) -> p c t", c=2),
                                                      in1=esink[:, 2 * g:2 * g + 2].unsqueeze(2).to_broadcast([128, 2, 128]), op=ALU.add),
                     reads=odk + ["esink"], writes=["scrB"])
                S.op("dve", lambda e: e.reciprocal(out=scrB[:, 0:256], in_=scrB[:, 0:256]), reads=["scrB"], writes=["scrB"])
                S.op("dve", lambda e: e.tensor_tensor(out=o_raw[:, 2 * g:2 * g + 2, i * 128:(i + 1) * 128],
                                                      in0=od[:, 0:256].rearrange("p (c t) -> p c t", c=2), in1=tden, op=ALU.mult),
                     reads=odk + ["scrB"], writes=["o_raw"])

            front(0)
            for u in range(len(units)):
                if u + 1 < len(units):
                    front(u + 1)
                back(u)

        def out_norm(gvec, gkey, c_off):
            sq = scrA[:, :].rearrange("p (c t) -> p c t", c=4)
            S.op("act", lambda e: e.activation(out=sq, in_=o_raw[:], func=AF.Square), reads=["o_raw"], writes=["scrA"])
            for c in range(4):
                mm(banks[5][:, :], ones_bf[:], sq[:, c, :], c == 0, c == 3, ["scrA", "ones_bf"], ["bank5"], c == 3)
            S.op("dve", lambda e: e.tensor_scalar(out=scrB[:], in0=banks[5][:], scalar1=1.0 / 512, scalar2=EPS, op0=ALU.mult, op1=ALU.add),
                 reads=["bank5"], writes=["scrB"])
            S.op("act", lambda e: e.activation(out=scrB[:], in_=scrB[:], func=AF.Ln), reads=["scrB"], writes=["scrB"])
            S.op("act", lambda e: e.activation(out=scrB[:], in_=scrB[:], func=AF.Exp, scale=-0.5), reads=["scrB"], writes=["scrB"])
            for c in range(4):
                S.op("dve", lambda e, c=c: e.scalar_tensor_tensor(out=mixT[:, c_off + c, :], in0=o_raw[:, c, :], scalar=gvec[:, c:c + 1],
                                                                 in1=scrB[:], op0=ALU.mult, op1=ALU.mult),
                     reads=["o_raw", "scrB", gkey], writes=[f"mix{c_off + c}"])

        b7f = banks[7][:].bitcast(F32)
        ZP = ((0, 1), (2, 5), (6, 7))

        def zap(bk):
            return b7f if bk == 7 else banks[bk][:, :]

        def attn_b(gi):
            plist = []
            for hp in range(4):
                blks = [("diag", 4 * gi + d, d) for d in (3, 2, 1, 0)] + [("full", kb, 0) for kb in range(4 * gi - 1, -1, -1)] + [("meta", -1, 0)]
                for bi, (kind, kb, d) in enumerate(blks):
                    plist.append(dict(hp=hp, kind=kind, kb=kb, d=d, first=(bi == 0), last=(bi == len(blks) - 1)))
            n = len(plist)

            def geom(b):
                q0 = 128 * b["d"] if b["kind"] == "diag" else 0
                nk = 16 if b["kind"] == "meta" else 128
                return q0, nk

            def s0(T):
                b = plist[T]
                q0, nk = geom(b)
                for par in range(2):
                    rows = slice(64 * par, 64 * par + 64)
                    zb = ZP[T % 3][par]
                    if b["kind"] == "meta":
                        kap, kkey = kbT[rows, b["hp"], 0:16], "kbTm"
                    else:
                        c0 = 16 + b["kb"] * 128
                        kap, kkey = kbT[rows, b["hp"], c0:c0 + 128], f"kbT{b['kb'] + 1}"
                    mm(zap(zb)[0:nk, q0:512], kap, qbT[rows, b["hp"], q0:512], True, True, [kkey, "qbT"], BK(zb), par == 1)

            def s1a(T):
                b = plist[T]
                q0, nk = geom(b)
                for par in range(2):
                    zb = ZP[T % 3][par]
                    S.op("act", lambda e, par=par, zb=zb: e.activation(out=e_sb[par][0:nk, q0:512], in_=zap(zb)[0:nk, q0:512], func=AF.Exp),
                         reads=BK(zb), writes=[f"e{par}"])

            def s1(T):
                b = plist[T]
                q0, nk = geom(b)
                q1 = q0 + 128 if b["kind"] == "diag" else 0
                has_acc = (q1 < 512) and not (b["kind"] == "diag" and b["d"] == 3)
                for par in range(2):
                    S.op("act", lambda e, par=par: e.activation(out=sp_sb[par][0:nk, q0:512], in_=e_sb[par][0:nk, q0:512], func=AF.Ln, bias=1.0),
                         reads=[f"e{par}"], writes=[f"sp{par}"])
                    if b["kind"] == "diag":
                        S.op("pool", lambda e, par=par: e.affine_select(out=sp_sb[par][:, q0:q0 + 128], in_=sp_sb[par][:, q0:q0 + 128], pattern=[[1, 128]],
                                                                        compare_op=ALU.is_gt, fill=0.0, base=0, channel_multiplier=-1),
                             reads=[f"sp{par}"], writes=[f"sp{par}"])
                for par in range(2):
                    zb = ZP[T % 3][par]
                    mm(zap(zb)[0:nk, q0:512], negtri[0:nk, 0:nk], sp_sb[par][0:nk, q0:512], False, True,
                       ["negtri", f"sp{par}"], BK(zb), not has_acc, skip=True)
                    if has_acc:
                        mm(zap(zb)[0:nk, q1:512], negones[:, 0:nk], acc_sb[par][:, q1:512], False, True,
                           ["negones", f"acc{par}"], BK(zb), True, skip=True)
                for par in range(2):
                    if b["kind"] == "diag":
                        S.op("dve", lambda e, par=par: e.tensor_copy(out=acc_sb[par][:, q0:q0 + 128], in_=sp_sb[par][:, q0:q0 + 128]),
                             reads=[f"sp{par}"], writes=[f"acc{par}"])
                        if has_acc:
                            S.op("dve", lambda e, par=par: e.tensor_tensor(out=acc_sb[par][:, q1:512], in0=acc_sb[par][:, q1:512],
                                                                           in1=sp_sb[par][:, q1:512], op=ALU.add),
                                 reads=[f"sp{par}", f"acc{par}"], writes=[f"acc{par}"])
                    elif b["kind"] == "full":
                        S.op("dve", lambda e, par=par: e.tensor_tensor(out=acc_sb[par][:, :], in0=acc_sb[par][:, :], in1=sp_sb[par][:, :], op=ALU.add),
                             reads=[f"sp{par}", f"acc{par}"], writes=[f"acc{par}"])

            def s2(T):
                b = plist[T]
                q0, nk = geom(b)
                ob = 3 + (b["hp"] % 2)
                for par in range(2):
                    zb = ZP[T % 3][par]
                    S.op("act", lambda e, par=par, zb=zb: e.activation(out=a_sb[par][0:nk, q0:512], in_=zap(zb)[0:nk, q0:512], func=AF.Exp),
                         reads=BK(zb), writes=[f"a{par}"])
                    if b["kind"] == "diag":
                        S.op("pool", lambda e, par=par: e.affine_select(out=a_sb[par][:, q0:q0 + 128], in_=a_sb[par][:, q0:q0 + 128], pattern=[[1, 128]],
                                                                        compare_op=ALU.is_gt, fill=0.0, base=0, channel_multiplier=-1),
                             reads=[f"a{par}"], writes=[f"a{par}"])
                for par in range(2):
                    h = 2 * b["hp"] + par
                    prow = slice(64 * par, 64 * par + 64)
                    if b["kind"] == "meta":
                        vap, vkey = vb[0:16, 0, h * 64:(h + 1) * 64], "vbm"
                    else:
                        vap, vkey = vb[:, b["kb"] + 1, h * 64:(h + 1) * 64], f"vb{b['kb'] + 1}"
                    mm(banks[ob][prow, q0:512], vap, a_sb[par][0:nk, q0:512], b["first"], b["last"], [vkey, f"a{par}"],
                       [f"bank{ob}_{par}"], par == 1, skip=True)
                if b["last"]:
                    S.op("dve", lambda e: e.tensor_copy(out=o_raw[:, b["hp"], :], in_=banks[ob][:, :]),
                         reads=BK(ob), writes=["o_raw"])

            for T in range(-2, n):
                if T + 2 < n:
                    s0(T + 2)
                if 0 <= T + 1 < n:
                    s1a(T + 1)
                    s1(T + 1)
                if T >= 0:
                    s2(T)

        def outproj(i):
            for kc in range(8):
                for hf in range(2):
                    mm(banks[5 + hf][:, :], mixT[:, kc, i * 128:(i + 1) * 128], w_out_sb[:, kc, hf * 512:(hf + 1) * 512], kc == 0, kc == 7,
                       [f"mix{kc}", "w_out"], [f"bank{5 + hf}"], kc == 7 and hf == 1)
            for hf in range(2):
                S.op("dve", lambda e, hf=hf: e.tensor_tensor(out=xg[:, i, hf * 512:(hf + 1) * 512], in0=xg[:, i, hf * 512:(hf + 1) * 512],
                                                             in1=banks[5 + hf][:, :], op=ALU.add),
                     reads=[f"bank{5 + hf}", f"xg{i}"], writes=[f"xg{i}"])

        def ffn(state):
            uks = [f"uT{i}" for i in range(4)]
            n0 = state["n"]
            dcount = [0]

            def gu(sc):
                slot, hb = (n0 + sc) % 2, sc % 2
                for fc in range(2):
                    gb, ub = 2 * fc, 2 * fc + 1
                    for kc in range(8):
                        mm(banks[gb][:, :], wg_sl[slot][:, kc, fc * 128:(fc + 1) * 128], uT[:, kc, :], kc == 0, kc == 7,
                           uks + [f"wg_sl{slot}"], BK(gb), False)
                    for kc in range(8):
                        mm(banks[ub][:, :], wu_sl[slot][:, kc, fc * 128:(fc + 1) * 128], uT[:, kc, :], kc == 0, kc == 7,
                           uks + [f"wu_sl{slot}"], BK(ub), kc == 7)
                    S.op("act", lambda e, gb=gb: e.activation(out=scrB[:], in_=banks[gb][:], func=AF.Silu), reads=BK(gb), writes=["scrB"])
                    S.op("dve", lambda e, ub=ub, fc=fc, hb=hb: e.tensor_tensor(out=hT[hb][:, fc, :], in0=scrB[:], in1=banks[ub][:], op=ALU.mult),
                         reads=["scrB"] + BK(ub), writes=[f"hT{hb}"])

            def down(sc):
                slot, hb = (n0 + sc) % 2, sc % 2
                for i in range(4):
                    for hf in range(2):
                        db = 4 + (dcount[0] % 3)
                        dcount[0] += 1
                        for fc in range(2):
                            mm(banks[db][:, :], hT[hb][:, fc, i * 128:(i + 1) * 128], wd_sl[slot][:, fc, hf * 512:(hf + 1) * 512], fc == 0, fc == 1,
                               [f"hT{hb}", f"wd_sl{slot}"], BK(db), fc == 1)
                        S.op("dve", lambda e, i=i, hf=hf, db=db: e.tensor_tensor(out=xg[:, i, hf * 512:(hf + 1) * 512],
                                                                                 in0=xg[:, i, hf * 512:(hf + 1) * 512], in1=banks[db][:, :], op=ALU.add),
                             reads=BK(db) + [f"xg{i}"], writes=[f"xg{i}"])
                nxt = state["loads"]
                if nxt < state["total"]:
                    ffn_load(nxt % NSC, slot)
                    state["loads"] += 1

            gu(0)
            for sc in range(NSC):
                if sc + 1 < NSC:
                    gu(sc + 1)
                down(sc)
            state["n"] += NSC

        def final_norm(i):
            xk = f"xg{i}"
            p = i % 2
            xs = xs2[p]
            xsk, sk = f"xs{p}", f"statf{p}"
            c0 = 8 + 4 * p
            S.op("act", lambda e: e.activation(out=xs[:], in_=xg[:, i, :], func=AF.Square, accum_out=stat[:, c0:c0 + 1]),
                 reads=[xk], writes=[xsk, sk])
            S.op("dve", lambda e: e.tensor_scalar(out=stat[:, c0 + 1:c0 + 2], in0=stat[:, c0:c0 + 1], scalar1=1.0 / D, scalar2=EPS,
                                                  op0=ALU.mult, op1=ALU.add), reads=[sk], writes=[sk])
            S.op("act", lambda e: e.activation(out=stat[:, c0 + 2:c0 + 3], in_=stat[:, c0 + 1:c0 + 2], func=AF.Ln), reads=[sk], writes=[sk])
            S.op("act", lambda e: e.activation(out=stat[:, c0 + 3:c0 + 4], in_=stat[:, c0 + 2:c0 + 3], func=AF.Exp, scale=-0.5),
                 reads=[sk], writes=[sk])
            S.op("dve", lambda e: e.scalar_tensor_tensor(out=xg[:, i, :], in0=xg[:, i, :], scalar=stat[:, c0 + 3:c0 + 4], in1=gfin[:],
                                                         op0=ALU.mult, op1=ALU.mult),
                 reads=[xk, sk, "gfin"], writes=[xk])

        total_groups = NSEQ * NG
        fstate = dict(n=0, loads=0, total=total_groups * NSC)
        for _ in range(2):
            ffn_load(fstate["loads"] % NSC, fstate["loads"] % 2)
            fstate["loads"] += 1
        groups = [(sq, gi) for sq in range(NSEQ) for gi in range(NG)]

        def load_x(sq, gi, i):
            xv = x_d[sq, gi * 512 + i * 128:gi * 512 + (i + 1) * 128, :]
            S.dma("sp", lambda e: e.dma_start(out=xg[:, i, :], in_=xv), f"ld_x{i}", writes=[f"xg{i}"])

        def store_x(sq, gi, i):
            ov = out_d[sq, gi * 512 + i * 128:gi * 512 + (i + 1) * 128, :]
            S.dma("sp", lambda e: e.dma_start(out=ov, in_=xg[:, i, :]), f"st_out{i}", reads=[f"xg{i}"], final=True)

        for i in range(4):
            load_x(0, 0, i)
        for gidx, (sq, gi) in enumerate(groups):
            rms_transpose(0, gmix, "gmix")
            for i in range(4):
                if i + 1 < 4:
                    rms_transpose(i + 1, gmix, "gmix")
                tok_proj(i, gi * 4 + i + 1)
            feat_proj(512, gi * 4)
            attn_a(gi)
            out_norm(gouta, "gouta", 0)
            attn_b(gi)
            out_norm(goutb, "goutb", 4)
            for i in range(4):
                outproj(i)
                if i >= 1:
                    rms_transpose(i - 1, gffn, "gffn")
            rms_transpose(3, gffn, "gffn")
            ffn(fstate)
            for i in range(4):
                final_norm(i)
                store_x(sq, gi, i)
                if gidx + 1 < len(groups):
                    load_x(groups[gidx + 1][0], groups[gidx + 1][1], i)
        S.emit(nc)
    return nc


def _tables(SEQ):
    NT = SEQ // 128
    half = 32
    inv_freq = (10000.0 ** (-np.arange(half, dtype=np.float32) / half)).astype(np.float32)
    cos = np.zeros((128, NT + 1, 32), np.float32)
    sin = np.zeros((128, NT + 1, 32), np.float32)
    r = np.arange(128)
    for t in range(NT + 1):
        pos = (r if t == 0 else 16 + 128 * (t - 1) + r).astype(np.float32)
        ang = (pos[:, None] * inv_freq[None, :]).astype(np.float32)
        cos[:, t, :] = np.cos(ang)
        sin[:, t, :] = np.sin(ang)
    return cos, sin


def _in_maps(x, meta_tokens, norm_mix, w_in, sinks, norm_out_a, norm_out_b, w_out, norm_ffn, w_gate, w_up, w_down,
             norm_final, n_cores, nseq):
    SEQ = x.shape[1]
    cos, sin = _tables(SEQ)
    c = np.ascontiguousarray
    f = lambda a: np.asarray(a, dtype=np.float32)
    sk = f(sinks).reshape(8)
    sinkl = np.zeros((128, 4), np.float32)
    for ch in range(4):
        sinkl[0:64, ch] = sk[2 * ch]
        sinkl[64:128, ch] = sk[2 * ch + 1]
    common = {
        "meta": c(f(meta_tokens)),
        "w_in": c(f(w_in)[0]), "w_out": c(f(w_out)[0]), "w_gate": c(f(w_gate)[0]), "w_up": c(f(w_up)[0]), "w_down": c(f(w_down)[0]),
        "gmix": c(f(norm_mix).reshape(8, 128).T), "gffn": c(f(norm_ffn).reshape(8, 128).T),
        "gouta": c(f(norm_out_a).reshape(4, 128).T), "goutb": c(f(norm_out_b).reshape(4, 128).T),
        "sinkl": sinkl, "gfin": c(np.broadcast_to(f(norm_final).reshape(1, D), (128, D))),
        "cost": cos, "sint": sin,
    }
    xs = f(x)
    maps = []
    for i in range(n_cores):
        m = dict(common)
        m["x"] = c(xs[i * nseq:(i + 1) * nseq])
        maps.append(m)
    return maps


def kernel(x, meta_tokens, norm_mix, w_in, sinks, norm_out_a, norm_out_b, w_out, norm_ffn, w_gate, w_up, w_down, norm_final):
    n_cores = 8
    B, SEQ, _ = x.shape
    nseq = B // n_cores
    nc = build(nseq, SEQ)
    maps = _in_maps(x, meta_tokens, norm_mix, w_in, sinks, norm_out_a, norm_out_b, w_out, norm_ffn, w_gate, w_up, w_down,
                    norm_final, n_cores, nseq)
    res = run_bass_kernel_spmd(nc, maps, core_ids=list(range(n_cores)))
    return np.concatenate([np.asarray(r["out"], dtype=np.float32) for r in res.results], axis=0)
```

```python
import contextlib
import numpy as np
import concourse.bass as bass
import concourse.mybir as mybir
from concourse.bass_utils import run_bass_kernel_spmd

F32 = mybir.dt.float32
BF16 = mybir.dt.bfloat16
AF = mybir.ActivationFunctionType
ALU = mybir.AluOpType

ENGS = ("pe", "act", "dve", "pool", "sp")

D = 1024
NCOL = 2304
DFF = 2816
NSC = DFF // 256
EPS = 1e-6
STOP = 99
import os as _os
DBG_TI0 = int(_os.environ.get('DBG_TI0', '0'))


class Sched:
    def __init__(self):
        self.ops = {e: [] for e in ENGS}
        self.cnt = {e: 0 for e in ENGS}
        self.dma_cnt = {}
        self.last_w = {}
        self.readers = {}
        self.seen = {e: {} for e in ENGS}
        self.final_tokens = []
        self.pe_pending = False

    def _deps(self, eng, reads, writes):
        deps = {}

        def add(tok):
            if tok is None:
                return
            k, v = tok
            if eng == "pe" and k == "pe":
                return
            if deps.get(k, 0) < v:
                deps[k] = v

        for r in reads:
            add(self.last_w.get(r))
        for w in writes:
            add(self.last_w.get(w))
            for t in self.readers.get(w, ()):
                add(t)
        out = []
        for k, v in deps.items():
            if self.seen[eng].get(k, 0) >= v:
                continue
            self.seen[eng][k] = v
            out.append((k, v))
        return out

    def _commit(self, tok, reads, writes):
        for r in reads:
            self.readers.setdefault(r, []).append(tok)
        for w in writes:
            self.last_w[w] = tok
            self.readers[w] = []

    def op(self, eng, fn, reads=(), writes=(), sig=True, pre=False):
        waits = self._deps(eng, reads, writes)
        if eng == "pe" and not sig:
            tok = ("pe", self.cnt["pe"] + 1)
            incs = []
            self.pe_pending = True
        else:
            self.cnt[eng] += 1
            tok = (eng, self.cnt[eng])
            incs = [(eng, 1)]
            if eng == "pe":
                self.pe_pending = False
        self.ops[eng].append((fn, waits, incs, False))
        self._commit(tok, reads, writes)
        return tok

    def dma(self, q, fn, semkey, reads=(), writes=(), final=False):
        waits = self._deps(q, reads, writes)
        self.dma_cnt[semkey] = self.dma_cnt.get(semkey, 0) + 16
        tok = (semkey, self.dma_cnt[semkey])
        self.ops[q].append((fn, waits, [(semkey, 16)], True))
        self._commit(tok, reads, writes)
        if final:
            self.final_tokens.append(tok)
        return tok

    def emit(self, nc):
        assert not self.pe_pending
        semkeys = list(ENGS[:4]) + list(self.dma_cnt.keys())
        with contextlib.ExitStack() as st:
            sems = {}
            for k in semkeys:
                sems[k] = st.enter_context(nc.semaphore("s_" + str(k)))
            block = st.enter_context(nc.Block())
            finals = {}
            for k, v in self.final_tokens:
                finals[k] = max(finals.get(k, 0), v)

            def run(engname):
                def body(e):
                    for fn, waits, incs, pre_wait in self.ops[engname]:
                        if pre_wait:
                            for k, v in waits:
                                e.wait_ge(sems[k], v)
                            ins = fn(e)
                        else:
                            for k, v in waits[:-1]:
                                e.wait_ge(sems[k], v)
                            ins = fn(e)
                            if waits:
                                k, v = waits[-1]
                                ins._wait_ge(sems[k], v)
                        for k, a in incs:
                            ins.then_inc(sems[k], a)
                    if engname == "sp":
                        for k, v in finals.items():
                            e.wait_ge(sems[k], v)
                return body

            block.tensor(run("pe"))
            block.scalar(run("act"))
            block.vector(run("dve"))
            block.gpsimd(run("pool"))
            block.sync(run("sp"))


def build(NSEQ, SEQ):
    NG = SEQ // 512
    NT = SEQ // 128
    CL = 16 + SEQ
    nc = bass.Bass("TRN2", target_bir_lowering=False)

    def din(name, shape, dt=F32):
        return nc.dram_tensor(name, shape, dt, kind="ExternalInput").ap()

    x_d = din("x", [NSEQ, SEQ, D])
    meta_d = din("meta", [16, D])
    w_in_d = din("w_in", [D, NCOL])
    w_out_d = din("w_out", [D, D])
    w_gate_d = din("w_gate", [D, DFF])
    w_up_d = din("w_up", [D, DFF])
    w_down_d = din("w_down", [DFF, D])
    gmix_d = din("gmix", [128, 8])
    gffn_d = din("gffn", [128, 8])
    gouta_d = din("gouta", [128, 4])
    goutb_d = din("goutb", [128, 4])
    sink_d = din("sinkl", [128, 4])
    gfin_d = din("gfin", [128, D])
    cos_d = din("cost", [128, NT + 1, 32])
    sin_d = din("sint", [128, NT + 1, 32])
    out_d = nc.dram_tensor("out", [NSEQ, SEQ, D], F32, kind="ExternalOutput").ap()
    wg_s = nc.dram_tensor("wg_s", [D, DFF], BF16, kind="Internal").ap()
    wu_s = nc.dram_tensor("wu_s", [D, DFF], BF16, kind="Internal").ap()
    wd_s = nc.dram_tensor("wd_s", [DFF, D], BF16, kind="Internal").ap()

    st = contextlib.ExitStack()

    def sb(name, shape, dt):
        return st.enter_context(nc.sbuf_tensor(name, shape, dt))

    def ps(name, shape, dt):
        return st.enter_context(nc.psum_tensor(name, shape, dt))

    with st:
        w_in_sb = sb("w_in_sb", [128, 8, NCOL], BF16)
        w_out_sb = sb("w_out_sb", [128, 8, D], BF16)
        kaT = sb("kaT", [128, CL], BF16)
        va = sb("va", [128, NT + 1, 128], BF16)
        kbT = sb("kbT", [128, 4, CL], BF16)
        vb = sb("vb", [128, NT + 1, 512], BF16)
        xg = sb("xg", [128, 4, D], F32)
        uT = sb("uT", [128, 8, 512], BF16)
        xs2 = [sb(f"xs{i}", [128, D], BF16) for i in range(2)]
        qaT = sb("qaT", [128, 4, 512], BF16)
        qbT = sb("qbT", [128, 4, 512], BF16)
        rq = sb("rq", [128, 640], BF16)
        kq = sb("kq", [128, 640], F32)
        scrA = sb("scrA", [128, 2048], BF16)
        scrB = sb("scrB", [128, 512], F32)
        e_sb = [sb(f"e_sb{i}", [128, 512], F32) for i in range(2)]
        sp_sb = [sb(f"sp_sb{i}", [128, 512], BF16) for i in range(2)]
        a_sb = [sb(f"a_sb{i}", [128, 512], BF16) for i in range(2)]
        acc_sb = [sb(f"acc_sb{i}", [128, 512], BF16) for i in range(2)]
        pT = sb("pT", [128, 3, 512], BF16)
        o_raw = sb("o_raw", [128, 4, 512], F32)
        mixT = sb("mixT", [128, 8, 512], BF16)
        hT = [sb(f"hT{i}", [128, 2, 512], BF16) for i in range(2)]
        wg_sl = [sb(f"wg_sl{i}", [128, 8, 256], BF16) for i in range(2)]
        wu_sl = [sb(f"wu_sl{i}", [128, 8, 256], BF16) for i in range(2)]
        wd_sl = [sb(f"wd_sl{i}", [128, 2, D], BF16) for i in range(2)]
        cos_sb = sb("cos_sb", [128, NT + 1, 32], F32)
        sin_sb = sb("sin_sb", [128, NT + 1, 32], F32)
        nsin_sb = sb("nsin_sb", [128, NT + 1, 32], F32)
        gmix = sb("gmix_sb", [128, 8], F32)
        gffn = sb("gffn_sb", [128, 8], F32)
        gouta = sb("gouta_sb", [128, 4], F32)
        goutb = sb("goutb_sb", [128, 4], F32)
        esink = sb("esink_sb", [128, 4], F32)
        gfin = sb("gfin_sb", [128, D], F32)
        ident = sb("ident", [128, 128], BF16)
        negtri = sb("negtri", [128, 128], BF16)
        negones = sb("negones", [128, 128], BF16)
        ones_bf = sb("ones_bf", [128, 128], BF16)
        stat = sb("stat", [128, 16], F32)
        banks = [ps(f"bank{i}", [128, 512], F32) for i in range(7)] + [ps("bank7", [128, 1024], BF16)]

        S = Sched()
        scrA_f = scrA[:].bitcast(F32)

        S.op("pool", lambda e: e.memset(scrB[:, 0:128], 1.0), writes=["scrB"])
        S.op("pool", lambda e: e.affine_select(out=ident[:], in_=scrB[:, 0:128], pattern=[[-1, 128]], compare_op=ALU.is_equal,
                                               fill=0.0, base=0, channel_multiplier=1), reads=["scrB"], writes=["ident"])
        S.op("pool", lambda e: e.memset(ones_bf[:], 1.0), writes=["ones_bf"])
        S.op("pool", lambda e: e.memset(scrB[:, 0:128], -1.0), reads=[], writes=["scrB"])
        S.op("pool", lambda e: e.memset(negones[:], -1.0), writes=["negones"])
        S.op("pool", lambda e: e.affine_select(out=negtri[:], in_=scrB[:, 0:128], pattern=[[-1, 128]], compare_op=ALU.is_ge,
                                               fill=0.0, base=0, channel_multiplier=1), reads=["scrB"], writes=["negtri"])
        S.op("pool", lambda e: e.memset(xg[:, 0, :], 0.0), writes=["xg0"])
        cast_n = [0]

        def ld(q, dst, src, key, wkeys):
            if q == "pool":
                ck = f"castchain{cast_n[0] % 2}"
                cast_n[0] += 1
                S.dma(q, lambda e: e.dma_start(out=dst, in_=src), key, reads=[ck], writes=wkeys + [ck])
            else:
                S.dma(q, lambda e: e.dma_start(out=dst, in_=src), key, writes=wkeys)

        w_in_v = w_in_d.rearrange("(kc p) n -> p kc n", p=128)
        for h in range(2):
            ld("pool", w_in_sb[:, :, h * 1152:(h + 1) * 1152], w_in_v[:, :, h * 1152:(h + 1) * 1152], f"ld_win{h}", [f"w_in{h}"])
        W_IN = ["w_in0", "w_in1"]
        ld("sp", gmix[:], gmix_d, "ld_c0", ["gmix"])
        ld("sp", gffn[:], gffn_d, "ld_c1", ["gffn"])
        ld("sp", gouta[:], gouta_d, "ld_c2", ["gouta"])
        ld("sp", goutb[:], goutb_d, "ld_c3", ["goutb"])
        ld("sp", esink[:], sink_d, "ld_c4", ["esink"])
        ld("sp", gfin[:], gfin_d, "ld_c5", ["gfin"])
        ld("sp", cos_sb[:], cos_d, "ld_c6", ["cos"])
        ld("sp", sin_sb[:], sin_d, "ld_c7", ["sin"])
        ld("pool", w_out_sb[:], w_out_d.rearrange("(kc p) n -> p kc n", p=128), "ld_wout", ["w_out"])
        def rows2(ap):
            return ap.rearrange("r (t c) -> (r t) c", t=2)
        for h in range(2):
            ld("pool", rows2(wg_s)[h * 1024:(h + 1) * 1024, :], rows2(w_gate_d)[h * 1024:(h + 1) * 1024, :], f"ld_wg{h}", [f"wg_s{h}"])
            ld("pool", rows2(wu_s)[h * 1024:(h + 1) * 1024, :], rows2(w_up_d)[h * 1024:(h + 1) * 1024, :], f"ld_wu{h}", [f"wu_s{h}"])
        for h in range(2):
            ld("pool", wd_s[h * 1408:(h + 1) * 1408, :], w_down_d[h * 1408:(h + 1) * 1408, :], f"ld_wd{h}", [f"wd_s{h}"])

        S.op("act", lambda e: e.activation(out=esink[:], in_=esink[:], func=AF.Exp), reads=["esink"], writes=["esink"])
        S.op("dve", lambda e: e.tensor_scalar(out=nsin_sb[:], in0=sin_sb[:], scalar1=-1.0, scalar2=None, op0=ALU.mult),
             reads=["sin"], writes=["nsin"])
        S.dma("sp", lambda e: e.dma_start(out=xg[0:16, 0, :], in_=meta_d), "ld_x0", reads=[], writes=["xg0"])

        bank_bf = [b[:].bitcast(BF16) for b in banks[:7]] + [banks[7][:]]

        def BK(b):
            return [f"bank{b}_0", f"bank{b}_1"] if b in (3, 4) else [f"bank{b}"]

        def rms_transpose(i, gvec, gkey, tb=None):
            xk = f"xg{i}"
            p = i % 2
            xs = xs2[p]
            xsk, sk = f"xs{p}", f"stat{p}"
            c0 = 4 * p
            tb = 5 + p
            S.op("act", lambda e: e.activation(out=xs[:], in_=xg[:, i, :], func=AF.Square, accum_out=stat[:, c0:c0 + 1]),
                 reads=[xk], writes=[xsk, sk])
            S.op("dve", lambda e: e.tensor_scalar(out=stat[:, c0 + 1:c0 + 2], in0=stat[:, c0:c0 + 1], scalar1=1.0 / D, scalar2=EPS,
                                                  op0=ALU.mult, op1=ALU.add), reads=[sk], writes=[sk])
            S.op("act", lambda e: e.activation(out=stat[:, c0 + 2:c0 + 3], in_=stat[:, c0 + 1:c0 + 2], func=AF.Ln), reads=[sk], writes=[sk])
            S.op("act", lambda e: e.activation(out=stat[:, c0 + 3:c0 + 4], in_=stat[:, c0 + 2:c0 + 3], func=AF.Exp, scale=-0.5),
                 reads=[sk], writes=[sk])
            S.op("dve", lambda e: e.tensor_scalar(out=xs[:], in0=xg[:, i, :], scalar1=stat[:, c0 + 3:c0 + 4], scalar2=None, op0=ALU.mult),
                 reads=[xk, sk], writes=[xsk])
            for kc in range(8):
                S.op("pe", lambda e, kc=kc: e.transpose(out=bank_bf[tb][:, kc * 128:(kc + 1) * 128],
                                                        in_=xs[:, kc * 128:(kc + 1) * 128], identity=ident[:]),
                     reads=[xsk, "ident"], writes=BK(tb), sig=(kc == 7))
            S.op("dve", lambda e: e.tensor_tensor(out=uT[:, :, i * 128:(i + 1) * 128],
                                                  in0=bank_bf[tb][:, :].rearrange("p (k t) -> p k t", k=8),
                                                  in1=gvec[:, :].unsqueeze(2).to_broadcast([128, 8, 128]), op=ALU.mult),
                 reads=BK(tb) + [gkey], writes=[f"uT{i}"])

        def rope(src_ps, nh, dsts, ti, skeys, dkey):
            n = nh * 64
            if DBG_TI0:
                ti = 0
            tcv = scrA_f[:, 0:n].rearrange("p (h t d) -> p h t d", h=nh, t=2)
            tsv = scrA_f[:, 512:512 + n].rearrange("p (h t d) -> p h t d", h=nh, t=2)
            srcv = src_ps.rearrange("p (h t d) -> p h t d", h=nh, t=2)
            cosb = cos_sb[:, ti, :].unsqueeze(1).to_broadcast([128, nh, 32])
            sinb = sin_sb[:, ti, :].unsqueeze(1).to_broadcast([128, nh, 32])
            nsinb = nsin_sb[:, ti, :].unsqueeze(1).to_broadcast([128, nh, 32])
            for t in range(2):
                S.op("dve", lambda e, t=t: e.tensor_tensor(out=tcv[:, :, t, :], in0=srcv[:, :, t, :], in1=cosb, op=ALU.mult),
                     reads=skeys + ["cos"], writes=["scrA"])
            S.op("dve", lambda e: e.tensor_tensor(out=tsv[:, :, 0, :], in0=srcv[:, :, 1, :], in1=nsinb, op=ALU.mult),
                 reads=skeys + ["nsin"], writes=["scrA"])
            S.op("dve", lambda e: e.tensor_tensor(out=tsv[:, :, 1, :], in0=srcv[:, :, 0, :], in1=sinb, op=ALU.mult),
                 reads=skeys + ["sin"], writes=["scrA"])
            for h0, nhh, dst in dsts:
                S.op("dve", lambda e, h0=h0, nhh=nhh, dst=dst: e.tensor_tensor(
                    out=dst, in0=scrA_f[:, h0 * 64:(h0 + nhh) * 64].rearrange("p (h d) -> p h d", h=nhh),
                    in1=scrA_f[:, 512 + h0 * 64:512 + (h0 + nhh) * 64].rearrange("p (h d) -> p h d", h=nhh), op=ALU.add),
                     reads=["scrA"], writes=[dkey])

        def mm(out, lhsT, rhs, start, stop, reads, writes, sig, skip=False):
            S.op("pe", lambda e: e.matmul(out, lhsT=lhsT, rhs=rhs, start=start, stop=stop, skip_group_check=skip),
                 reads=reads, writes=writes, sig=sig)

        rq_q = rq[:, 0:512].rearrange("p (j hh d) -> p hh j d", j=4, hh=2)

        def tok_proj(i, ct, meta=False):
            uk = f"uT{i}"
            lts = [uT[:, kc, i * 128:(i + 1) * 128] for kc in range(8)]
            for kc in range(8):
                if not meta:
                    mm(banks[2][:, :], lts[kc], w_in_sb[:, kc, 0:512], kc == 0, kc == 7, [uk, "w_in0"], BK(2), False)
                mm(banks[3][:, 0:256], lts[kc], w_in_sb[:, kc, 512:768], kc == 0, kc == 7, [uk, "w_in0"], BK(3), False)
                mm(banks[4][:, :], lts[kc], w_in_sb[:, kc, 1792:2304], kc == 0, kc == 7, [uk, "w_in1"], BK(4), kc == 7)
            rows = slice(0, 16) if meta else slice(0, 128)
            vk = "vbm" if meta else f"vb{ct}"
            S.op("act", lambda e: e.copy(out=vb[rows, ct, :], in_=banks[4][rows, :]), reads=BK(4), writes=[vk])
            vak = "vam" if meta else f"va{ct}"
            S.op("act", lambda e: e.copy(out=va[rows, ct, :], in_=banks[3][rows, 128:256]), reads=BK(3), writes=[vak])
            if STOP in (13, 23):
                return
            S.op("act", lambda e: e.copy(out=kq[:, 512:640], in_=banks[3][:, 0:128]), reads=BK(3), writes=["kq_k"])
            rope(kq[:, 512:640], 2, [(0, 2, rq[:, 512:640].rearrange("p (h d) -> p h d", h=2))], ct, ["kq_k"], "rqk")
            if not meta and STOP != 25:
                S.op("act", lambda e: e.copy(out=kq[:, 0:512], in_=banks[2][:, :]), reads=BK(2), writes=["kq_q"])
                rope(kq[:, 0:512], 8, [(0, 4, rq_q[:, 0, :, :]), (4, 4, rq_q[:, 1, :, :])], ct, ["kq_q"], "rqq")
            if STOP in (14, 24, 25):
                return
            tb = 7
            if not meta:
                for j in range(4):
                    S.op("pe", lambda e, j=j: e.transpose(out=bank_bf[tb][:, j * 128:(j + 1) * 128],
                                                          in_=rq[:, j * 128:(j + 1) * 128], identity=ident[:]),
                         reads=["rqq", "ident"], writes=[f"bank{tb}"], sig=False)
            S.op("pe", lambda e: e.transpose(out=bank_bf[tb][:, 512:640], in_=rq[:, 512:640], identity=ident[:]),
                 reads=["rqk", "ident"], writes=[f"bank{tb}"], sig=True)
            if meta:
                S.op("dve", lambda e: e.tensor_copy(out=kaT[:, 0:16], in_=bank_bf[tb][:, 512:528]),
                     reads=[f"bank{tb}"], writes=["kaTm"])
            else:
                c0 = 16 + (ct - 1) * 128
                S.op("dve", lambda e: e.tensor_copy(out=kaT[:, c0:c0 + 128], in_=bank_bf[tb][:, 512:640]),
                     reads=[f"bank{tb}"], writes=[f"kaT{ct}"])
                S.op("dve", lambda e: e.tensor_copy(out=qaT[:, :, i * 128:(i + 1) * 128],
                                                    in_=bank_bf[tb][:, 0:512].rearrange("p (j t) -> p j t", j=4)),
                     reads=[f"bank{tb}"], writes=["qaT"])

        def feat_proj(ntok, gt0, meta=False):
            uks = [f"uT{i}" for i in range((ntok + 127) // 128)]
            nb = 0
            for c in range(8):
                if meta and c < 4:
                    continue
                col0 = 768 + c * 128
                bk = nb % 2
                nb += 1
                for kc in range(8):
                    mm(banks[bk][:, 0:ntok], w_in_sb[:, kc, col0:col0 + 128], uT[:, kc, 0:ntok], kc == 0, kc == 7,
                       uks + ["w_in0", "w_in1"], [f"bank{bk}"], kc == 7)
                if c < 4:
                    S.op("act", lambda e, c=c, bk=bk: e.activation(out=qbT[:, c, :], in_=banks[bk][:, :], func=AF.Copy, scale=0.125),
                         reads=[f"bank{bk}"], writes=["qbT"])
                elif meta:
                    S.op("dve", lambda e, c=c, bk=bk: e.tensor_copy(out=kbT[:, c - 4, 0:16], in_=banks[bk][:, 0:16]),
                         reads=[f"bank{bk}"], writes=["kbTm"])
                else:
                    c0 = 16 + gt0 * 128
                    S.op("dve", lambda e, c=c, bk=bk, c0=c0: e.tensor_copy(out=kbT[:, c - 4, c0:c0 + 512], in_=banks[bk][:, :]),
                         reads=[f"bank{bk}"], writes=[f"kbT{gt0 + 1 + t}" for t in range(4)])

        rms_transpose(0, gmix, "gmix")
        tok_proj(0, 0, meta=True)
        feat_proj(128, 0, meta=True)

        ffn_loads = []

        def ffn_load(sc, slot):
            S.dma("sp", lambda e: e.dma_start(out=wg_sl[slot][:], in_=wg_s.rearrange("(kc p) f -> p kc f", p=128)[:, :, sc * 256:(sc + 1) * 256]),
                  f"ld_wgs{slot}", reads=["wg_s0", "wg_s1"], writes=[f"wg_sl{slot}"])
            S.dma("sp", lambda e: e.dma_start(out=wu_sl[slot][:], in_=wu_s.rearrange("(kc p) f -> p kc f", p=128)[:, :, sc * 256:(sc + 1) * 256]),
                  f"ld_wus{slot}", reads=["wu_s0", "wu_s1"], writes=[f"wu_sl{slot}"])
            S.dma("sp", lambda e: e.dma_start(out=wd_sl[slot][:], in_=wd_s.rearrange("(c p) n -> p c n", p=128)[:, 2 * sc:2 * sc + 2, :]),
                  f"ld_wds{slot}", reads=["wd_s0", "wd_s1"], writes=[f"wd_sl{slot}"])

        def attn_a(gi):
            units = [(i, g) for i in range(4) for g in range(2)]

            def res(u):
                if u % 2 == 0:
                    return {0: 0, 1: 1, 2: 2}, banks[3][:, :], BK(3), [pT[:, 0, :], pT[:, 1, :], pT[:, 2, :]], ["pT0", "pT1", "pT2"]
                return {0: 4, 1: 5, 2: 6}, b7f, ["bank7"], [a_sb[0][:, :], a_sb[1][:, :], sp_sb[0][:, :]], ["a0", "a1", "sp0"]

            def segs_of(u):
                i, g = units[u]
                gt = gi * 4 + i
                ct = gt + 1
                rows = slice(64 * g, 64 * g + 64)
                c0 = 16 + gt * 128
                segs = [("cur", 0, kaT[rows, c0:c0 + 128], 128, f"kaT{ct}", f"va{ct}", va[:, ct, 64 * g:64 * g + 64])]
                if gt >= 1:
                    segs.append(("prev", 1, kaT[rows, c0 - 128:c0], 128, f"kaT{ct - 1}", f"va{ct - 1}", va[:, ct - 1, 64 * g:64 * g + 64]))
                segs.append(("meta", 2, kaT[rows, 0:16], 16, "kaTm", "vam", va[0:16, 0, 64 * g:64 * g + 64]))
                return segs

            def front(u):
                i, g = units[u]
                rows = slice(64 * g, 64 * g + 64)
                sbm, od, odk, sbuf, sk = res(u)
                rhs_q = qaT[rows, :, i * 128:(i + 1) * 128]
                for nm, sl, kap, nk, kkey, vkey, vap in segs_of(u):
                    bk = sbm[sl]
                    mm(banks[bk][0:nk, :].rearrange("p (j t) -> p j t", j=4), kap, rhs_q, True, True, [kkey, "qaT"], BK(bk), True)
                    S.op("act", lambda e, bk=bk, nk=nk, sl=sl: e.activation(out=sbuf[sl][0:nk, :], in_=banks[bk][0:nk, :], func=AF.Exp, scale=0.125),
                         reads=BK(bk), writes=[sk[sl]])
                    if nm == "cur":
                        S.op("pool", lambda e: e.affine_select(out=sbuf[0].rearrange("p (j t) -> p j t", j=4),
                                                               in_=sbuf[0].rearrange("p (j t) -> p j t", j=4),
                                                               pattern=[[0, 4], [1, 128]], compare_op=ALU.is_ge, fill=0.0,
                                                               base=0, channel_multiplier=-1), reads=[sk[0]], writes=[sk[0]])
                    elif nm == "prev":
                        S.op("pool", lambda e: e.affine_select(out=sbuf[1].rearrange("p (j t) -> p j t", j=4),
                                                               in_=sbuf[1].rearrange("p (j t) -> p j t", j=4),
                                                               pattern=[[0, 4], [-1, 128]], compare_op=ALU.is_gt, fill=0.0,
                                                               base=0, channel_multiplier=1), reads=[sk[1]], writes=[sk[1]])

            def back(u):
                i, g = units[u]
                sbm, od, odk, sbuf, sk = res(u)
                segs = segs_of(u)
                for which in range(2):
                    for par in range(2):
                        prow = slice(64 * par, 64 * par + 64)
                        for si, (nm, sl, kap, nk, kkey, vkey, vap) in enumerate(segs):
                            rhs_p = sbuf[sl][0:nk, :].rearrange("p (c two t) -> p c two t", c=2, two=2)[:, :, par, :]
                            first, last = si == 0, si == len(segs) - 1
                            if which == 0:
                                mm(od[prow, 0:256].rearrange("p (c t) -> p c t", c=2), vap, rhs_p, first, last,
                                   [vkey, sk[sl]], odk, False, skip=True)
                            else:
                                mm(od[prow, 256:512].rearrange("p (c t) -> p c t", c=2), ones_bf[0:nk, 0:64], rhs_p, first, last,
                                   ["ones_bf", sk[sl]], odk, last and par == 1, skip=True)
                tden = scrB[:, 0:256].rearrange("p (c t) -> p c t", c=2)
                S.op("dve", lambda e: e.tensor_tensor(out=tden, in0=od[:, 256:512].rearrange("p (c t) -> p c t", c=2),
                                                      in1=esink[:, 2 * g:2 * g + 2].unsqueeze(2).to_broadcast([128, 2, 128]), op=ALU.add),
                     reads=odk + ["esink"], writes=["scrB"])
                S.op("dve", lambda e: e.reciprocal(out=scrB[:, 0:256], in_=scrB[:, 0:256]), reads=["scrB"], writes=["scrB"])
                S.op("dve", lambda e: e.tensor_tensor(out=o_raw[:, 2 * g:2 * g + 2, i * 128:(i + 1) * 128],
                                                      in0=od[:, 0:256].rearrange("p (c t) -> p c t", c=2), in1=tden, op=ALU.mult),
                     reads=odk + ["scrB"], writes=["o_raw"])

            front(0)
            for u in range(len(units)):
                if u + 1 < len(units):
                    front(u + 1)
                back(u)

        def out_norm(gvec, gkey, c_off):
            sq = scrA[:, :].rearrange("p (c t) -> p c t", c=4)
            S.op("act", lambda e: e.activation(out=sq, in_=o_raw[:], func=AF.Square), reads=["o_raw"], writes=["scrA"])
            for c in range(4):
                mm(banks[5][:, :], ones_bf[:], sq[:, c, :], c == 0, c == 3, ["scrA", "ones_bf"], ["bank5"], c == 3)
            S.op("dve", lambda e: e.tensor_scalar(out=scrB[:], in0=banks[5][:], scalar1=1.0 / 512, scalar2=EPS, op0=ALU.mult, op1=ALU.add),
                 reads=["bank5"], writes=["scrB"])
            S.op("act", lambda e: e.activation(out=scrB[:], in_=scrB[:], func=AF.Ln), reads=["scrB"], writes=["scrB"])
            S.op("act", lambda e: e.activation(out=scrB[:], in_=scrB[:], func=AF.Exp, scale=-0.5), reads=["scrB"], writes=["scrB"])
            for c in range(4):
                S.op("dve", lambda e, c=c: e.scalar_tensor_tensor(out=mixT[:, c_off + c, :], in0=o_raw[:, c, :], scalar=gvec[:, c:c + 1],
                                                                 in1=scrB[:], op0=ALU.mult, op1=ALU.mult),
                     reads=["o_raw", "scrB", gkey], writes=[f"mix{c_off + c}"])

        b7f = banks[7][:].bitcast(F32)
        ZP = ((0, 1), (2, 5), (6, 7))

        def zap(bk):
            return b7f if bk == 7 else banks[bk][:, :]

        def attn_b(gi):
            plist = []
            for hp in range(4):
                blks = [("diag", 4 * gi + d, d) for d in (3, 2, 1, 0)] + [("full", kb, 0) for kb in range(4 * gi - 1, -1, -1)] + [("meta", -1, 0)]
                for bi, (kind, kb, d) in enumerate(blks):
                    plist.append(dict(hp=hp, kind=kind, kb=kb, d=d, first=(bi == 0), last=(bi == len(blks) - 1)))
            n = len(plist)

            def geom(b):
                q0 = 128 * b["d"] if b["kind"] == "diag" else 0
                nk = 16 if b["kind"] == "meta" else 128
                return q0, nk

            def s0(T):
                b = plist[T]
                q0, nk = geom(b)
                for par in range(2):
                    rows = slice(64 * par, 64 * par + 64)
                    zb = ZP[T % 3][par]
                    if b["kind"] == "meta":
                        kap, kkey = kbT[rows, b["hp"], 0:16], "kbTm"
                    else:
                        c0 = 16 + b["kb"] * 128
                        kap, kkey = kbT[rows, b["hp"], c0:c0 + 128], f"kbT{b['kb'] + 1}"
                    mm(zap(zb)[0:nk, q0:512], kap, qbT[rows, b["hp"], q0:512], True, True, [kkey, "qbT"], BK(zb), par == 1)

            def s1a(T):
                b = plist[T]
                q0, nk = geom(b)
                for par in range(2):
                    zb = ZP[T % 3][par]
                    S.op("act", lambda e, par=par, zb=zb: e.activation(out=e_sb[par][0:nk, q0:512], in_=zap(zb)[0:nk, q0:512], func=AF.Exp),
                         reads=BK(zb), writes=[f"e{par}"])

            def s1(T):
                b = plist[T]
                q0, nk = geom(b)
                q1 = q0 + 128 if b["kind"] == "diag" else 0
                has_acc = (q1 < 512) and not (b["kind"] == "diag" and b["d"] == 3)
                for par in range(2):
                    S.op("act", lambda e, par=par: e.activation(out=sp_sb[par][0:nk, q0:512], in_=e_sb[par][0:nk, q0:512], func=AF.Ln, bias=1.0),
                         reads=[f"e{par}"], writes=[f"sp{par}"])
                    if b["kind"] == "diag":
                        S.op("pool", lambda e, par=par: e.affine_select(out=sp_sb[par][:, q0:q0 + 128], in_=sp_sb[par][:, q0:q0 + 128], pattern=[[1, 128]],
                                                                        compare_op=ALU.is_gt, fill=0.0, base=0, channel_multiplier=-1),
                             reads=[f"sp{par}"], writes=[f"sp{par}"])
                for par in range(2):
                    zb = ZP[T % 3][par]
                    mm(zap(zb)[0:nk, q0:512], negtri[0:nk, 0:nk], sp_sb[par][0:nk, q0:512], False, True,
                       ["negtri", f"sp{par}"], BK(zb), not has_acc, skip=True)
                    if has_acc:
                        mm(zap(zb)[0:nk, q1:512], negones[:, 0:nk], acc_sb[par][:, q1:512], False, True,
                           ["negones", f"acc{par}"], BK(zb), True, skip=True)
                for par in range(2):
                    if b["kind"] == "diag":
                        S.op("dve", lambda e, par=par: e.tensor_copy(out=acc_sb[par][:, q0:q0 + 128], in_=sp_sb[par][:, q0:q0 + 128]),
                             reads=[f"sp{par}"], writes=[f"acc{par}"])
                        if has_acc:
                            S.op("dve", lambda e, par=par: e.tensor_tensor(out=acc_sb[par][:, q1:512], in0=acc_sb[par][:, q1:512],
                                                                           in1=sp_sb[par][:, q1:512], op=ALU.add),
                                 reads=[f"sp{par}", f"acc{par}"], writes=[f"acc{par}"])
                    elif b["kind"] == "full":
                        S.op("dve", lambda e, par=par: e.tensor_tensor(out=acc_sb[par][:, :], in0=acc_sb[par][:, :], in1=sp_sb[par][:, :], op=ALU.add),
                             reads=[f"sp{par}", f"acc{par}"], writes=[f"acc{par}"])

            def s2(T):
                b = plist[T]
                q0, nk = geom(b)
                ob = 3 + (b["hp"] % 2)
                for par in range(2):
                    zb = ZP[T % 3][par]
                    S.op("act", lambda e, par=par, zb=zb: e.activation(out=a_sb[par][0:nk, q0:512], in_=zap(zb)[0:nk, q0:512], func=AF.Exp),
                         reads=BK(zb), writes=[f"a{par}"])
                    if b["kind"] == "diag":
                        S.op("pool", lambda e, par=par: e.affine_select(out=a_sb[par][:, q0:q0 + 128], in_=a_sb[par][:, q0:q0 + 128], pattern=[[1, 128]],
                                                                        compare_op=ALU.is_gt, fill=0.0, base=0, channel_multiplier=-1),
                             reads=[f"a{par}"], writes=[f"a{par}"])
                for par in range(2):
                    h = 2 * b["hp"] + par
                    prow = slice(64 * par, 64 * par + 64)
                    if b["kind"] == "meta":
                        vap, vkey = vb[0:16, 0, h * 64:(h + 1) * 64], "vbm"
                    else:
                        vap, vkey = vb[:, b["kb"] + 1, h * 64:(h + 1) * 64], f"vb{b['kb'] + 1}"
                    mm(banks[ob][prow, q0:512], vap, a_sb[par][0:nk, q0:512], b["first"], b["last"], [vkey, f"a{par}"],
                       [f"bank{ob}_{par}"], par == 1, skip=True)
                if b["last"]:
                    S.op("dve", lambda e: e.tensor_copy(out=o_raw[:, b["hp"], :], in_=banks[ob][:, :]),
                         reads=BK(ob), writes=["o_raw"])

            for T in range(-2, n):
                if T + 2 < n:
                    s0(T + 2)
                if 0 <= T + 1 < n:
                    s1a(T + 1)
                    s1(T + 1)
                if T >= 0:
                    s2(T)

        def outproj(i):
            ob0 = (1, 3)[i % 2]
            for kc in range(8):
                for hf in range(2):
                    mm(banks[ob0 + hf][:, :], mixT[:, kc, i * 128:(i + 1) * 128], w_out_sb[:, kc, hf * 512:(hf + 1) * 512], kc == 0, kc == 7,
                       [f"mix{kc}", "w_out"], BK(ob0 + hf), kc == 7 and hf == 1)
            for hf in range(2):
                S.op("dve", lambda e, hf=hf: e.tensor_tensor(out=xg[:, i, hf * 512:(hf + 1) * 512], in0=xg[:, i, hf * 512:(hf + 1) * 512],
                                                             in1=banks[ob0 + hf][:, :], op=ALU.add),
                     reads=BK(ob0 + hf) + [f"xg{i}"], writes=[f"xg{i}"])

        def ffn(state):
            uks = [f"uT{i}" for i in range(4)]
            n0 = state["n"]
            dcount = [0]

            def gu(sc):
                slot, hb = (n0 + sc) % 2, sc % 2
                for fc in range(2):
                    gb, ub = 2 * fc, 2 * fc + 1
                    for kc in range(8):
                        mm(banks[gb][:, :], wg_sl[slot][:, kc, fc * 128:(fc + 1) * 128], uT[:, kc, :], kc == 0, kc == 7,
                           uks + [f"wg_sl{slot}"], BK(gb), False)
                    for kc in range(8):
                        mm(banks[ub][:, :], wu_sl[slot][:, kc, fc * 128:(fc + 1) * 128], uT[:, kc, :], kc == 0, kc == 7,
                           uks + [f"wu_sl{slot}"], BK(ub), kc == 7)
                    S.op("act", lambda e, gb=gb: e.activation(out=scrB[:], in_=banks[gb][:], func=AF.Silu), reads=BK(gb), writes=["scrB"])
                    S.op("dve", lambda e, ub=ub, fc=fc, hb=hb: e.tensor_tensor(out=hT[hb][:, fc, :], in0=scrB[:], in1=banks[ub][:], op=ALU.mult),
                         reads=["scrB"] + BK(ub), writes=[f"hT{hb}"])

            def down(sc):
                slot, hb = (n0 + sc) % 2, sc % 2
                for i in range(4):
                    for hf in range(2):
                        db = 4 + (dcount[0] % 3)
                        dcount[0] += 1
                        for fc in range(2):
                            mm(banks[db][:, :], hT[hb][:, fc, i * 128:(i + 1) * 128], wd_sl[slot][:, fc, hf * 512:(hf + 1) * 512], fc == 0, fc == 1,
                               [f"hT{hb}", f"wd_sl{slot}"], BK(db), fc == 1)
                        S.op("dve", lambda e, i=i, hf=hf, db=db: e.tensor_tensor(out=xg[:, i, hf * 512:(hf + 1) * 512],
                                                                                 in0=xg[:, i, hf * 512:(hf + 1) * 512], in1=banks[db][:, :], op=ALU.add),
                             reads=BK(db) + [f"xg{i}"], writes=[f"xg{i}"])
                nxt = state["loads"]
                if nxt < state["total"]:
                    ffn_load(nxt % NSC, slot)
                    state["loads"] += 1

            gu(0)
            for sc in range(NSC):
                if sc + 1 < NSC:
                    gu(sc + 1)
                down(sc)
            state["n"] += NSC

        def final_norm(i):
            xk = f"xg{i}"
            p = i % 2
            xs = xs2[p]
            xsk, sk = f"xs{p}", f"statf{p}"
            c0 = 8 + 4 * p
            S.op("act", lambda e: e.activation(out=xs[:], in_=xg[:, i, :], func=AF.Square, accum_out=stat[:, c0:c0 + 1]),
                 reads=[xk], writes=[xsk, sk])
            S.op("dve", lambda e: e.tensor_scalar(out=stat[:, c0 + 1:c0 + 2], in0=stat[:, c0:c0 + 1], scalar1=1.0 / D, scalar2=EPS,
                                                  op0=ALU.mult, op1=ALU.add), reads=[sk], writes=[sk])
            S.op("act", lambda e: e.activation(out=stat[:, c0 + 2:c0 + 3], in_=stat[:, c0 + 1:c0 + 2], func=AF.Ln), reads=[sk], writes=[sk])
            S.op("act", lambda e: e.activation(out=stat[:, c0 + 3:c0 + 4], in_=stat[:, c0 + 2:c0 + 3], func=AF.Exp, scale=-0.5),
                 reads=[sk], writes=[sk])
            S.op("dve", lambda e: e.scalar_tensor_tensor(out=xg[:, i, :], in0=xg[:, i, :], scalar=stat[:, c0 + 3:c0 + 4], in1=gfin[:],
                                                         op0=ALU.mult, op1=ALU.mult),
                 reads=[xk, sk, "gfin"], writes=[xk])

        total_groups = NSEQ * NG
        fstate = dict(n=0, loads=0, total=total_groups * NSC)
        for _ in range(2):
            ffn_load(fstate["loads"] % NSC, fstate["loads"] % 2)
            fstate["loads"] += 1
        groups = [(sq, gi) for sq in range(NSEQ) for gi in range(NG)]

        def load_x(sq, gi, i):
            xv = x_d[sq, gi * 512 + i * 128:gi * 512 + (i + 1) * 128, :]
            S.dma("sp", lambda e: e.dma_start(out=xg[:, i, :], in_=xv), f"ld_x{i}", writes=[f"xg{i}"])

        def store_x(sq, gi, i):
            ov = out_d[sq, gi * 512 + i * 128:gi * 512 + (i + 1) * 128, :]
            S.dma("sp", lambda e: e.dma_start(out=ov, in_=xg[:, i, :]), f"st_out{i}", reads=[f"xg{i}"], final=True)

        for i in range(4):
            load_x(0, 0, i)
        for gidx, (sq, gi) in enumerate(groups):
            rms_transpose(0, gmix, "gmix")
            for i in range(4):
                if i + 1 < 4:
                    rms_transpose(i + 1, gmix, "gmix")
                tok_proj(i, gi * 4 + i + 1)
            feat_proj(512, gi * 4)
            attn_a(gi)
            out_norm(gouta, "gouta", 0)
            attn_b(gi)
            out_norm(goutb, "goutb", 4)
            for i in range(4):
                outproj(i)
                if i >= 1:
                    rms_transpose(i - 1, gffn, "gffn")
            rms_transpose(3, gffn, "gffn")
            ffn(fstate)
            for i in range(4):
                final_norm(i)
                store_x(sq, gi, i)
                if gidx + 1 < len(groups):
                    load_x(groups[gidx + 1][0], groups[gidx + 1][1], i)
        S.emit(nc)
    return nc


def _tables(SEQ):
    NT = SEQ // 128
    half = 32
    inv_freq = (10000.0 ** (-np.arange(half, dtype=np.float32) / half)).astype(np.float32)
    cos = np.zeros((128, NT + 1, 32), np.float32)
    sin = np.zeros((128, NT + 1, 32), np.float32)
    r = np.arange(128)
    for t in range(NT + 1):
        pos = (r if t == 0 else 16 + 128 * (t - 1) + r).astype(np.float32)
        ang = (pos[:, None] * inv_freq[None, :]).astype(np.float32)
        cos[:, t, :] = np.cos(ang)
        sin[:, t, :] = np.sin(ang)
    return cos, sin


def _in_maps(x, meta_tokens, norm_mix, w_in, sinks, norm_out_a, norm_out_b, w_out, norm_ffn, w_gate, w_up, w_down,
             norm_final, n_cores, nseq):
    SEQ = x.shape[1]
    cos, sin = _tables(SEQ)
    c = np.ascontiguousarray
    f = lambda a: np.asarray(a, dtype=np.float32)
    sk = f(sinks).reshape(8)
    sinkl = np.zeros((128, 4), np.float32)
    for ch in range(4):
        sinkl[0:64, ch] = sk[2 * ch]
        sinkl[64:128, ch] = sk[2 * ch + 1]
    common = {
        "meta": c(f(meta_tokens)),
        "w_in": c(f(w_in)[0]), "w_out": c(f(w_out)[0]), "w_gate": c(f(w_gate)[0]), "w_up": c(f(w_up)[0]), "w_down": c(f(w_down)[0]),
        "gmix": c(f(norm_mix).reshape(8, 128).T), "gffn": c(f(norm_ffn).reshape(8, 128).T),
        "gouta": c(f(norm_out_a).reshape(4, 128).T), "goutb": c(f(norm_out_b).reshape(4, 128).T),
        "sinkl": sinkl, "gfin": c(np.broadcast_to(f(norm_final).reshape(1, D), (128, D))),
        "cost": cos, "sint": sin,
    }
    xs = f(x)
    maps = []
    for i in range(n_cores):
        m = dict(common)
        m["x"] = c(xs[i * nseq:(i + 1) * nseq])
        maps.append(m)
    return maps


def kernel(x, meta_tokens, norm_mix, w_in, sinks, norm_out_a, norm_out_b, w_out, norm_ffn, w_gate, w_up, w_down, norm_final):
    n_cores = 8
    B, SEQ, _ = x.shape
    nseq = B // n_cores
    nc = build(nseq, SEQ)
    maps = _in_maps(x, meta_tokens, norm_mix, w_in, sinks, norm_out_a, norm_out_b, w_out, norm_ffn, w_gate, w_up, w_down,
                    norm_final, n_cores, nseq)
    res = run_bass_kernel_spmd(nc, maps, core_ids=list(range(n_cores)))
    return np.concatenate([np.asarray(r["out"], dtype=np.float32) for r in res.results], axis=0)
```
